# Optimizing a Trainium2 kernel written in Bass

```python
import math
import jax, jax.numpy as jnp
from jax import lax
import numpy as np

D_MODEL = 1024
BATCH = 8
SEQ = 2048
DEPTH = 2
DEC_BATCH = 128
DEC_SEQ = 4
PAST_LEN = 8192
PAGE_SIZE = 128

N_A_LAYERS = DEPTH // 2
N_B_LAYERS = DEPTH - N_A_LAYERS

GLA_HEADS = 4
GLA_DV = 192
GLA_DK = GLA_DV // 2
GLA_GATE_RANK = 16
GLA_GATE_NORM = 16.0
GLA_CHUNK = 64
GLA_QK = GLA_HEADS * GLA_DK
GLA_V = GLA_HEADS * GLA_DV

HEAD_DIM = 64
SWA_Q_HEADS = 12
SWA_KV_HEADS = 3
WINDOW = 128
ROT_DIM = HEAD_DIM // 4
ROPE_THETA = 500000.0
SWA_Q = SWA_Q_HEADS * HEAD_DIM
KV_SHARED = 2 * SWA_KV_HEADS * HEAD_DIM

MEM_TOKENS = 256
MEM_HEADS = 4
MEM_HEAD_DIM = 64
MEM_Q = MEM_HEADS * MEM_HEAD_DIM

FFN_DIM = 2816
EPS = 1e-6

A_IN = 2 * GLA_QK + 2 * GLA_V + GLA_GATE_RANK + MEM_Q
A_SPLITS = [GLA_QK, 2 * GLA_QK, 2 * GLA_QK + GLA_V, 2 * GLA_QK + 2 * GLA_V,
            2 * GLA_QK + 2 * GLA_V + GLA_GATE_RANK]
A_MIX = GLA_V + MEM_Q
B_IN = SWA_Q + MEM_Q
B_MIX = SWA_Q + MEM_Q

kernel_name = 'yoco_gla_swa_sink_memxattn_step'


def rmsnorm(x, g):
    xf = x.astype(jnp.float32)
    y = xf * lax.rsqrt(jnp.mean(xf * xf, axis=-1, keepdims=True) + EPS)
    return (y * g.astype(jnp.float32)).astype(x.dtype)


def swiglu_half(x, g, w_gu, w_down):
    h = rmsnorm(x, g)
    gate, up = jnp.split(h @ w_gu, 2, axis=-1)
    return 0.5 * ((jax.nn.silu(gate) * up) @ w_down)


def rope_partial(x, pos):
    half = ROT_DIM // 2
    inv_freq = jnp.exp(-math.log(ROPE_THETA) * jnp.arange(0, ROT_DIM, 2, dtype=jnp.float32) / ROT_DIM)
    ang = pos[:, None] * inv_freq[None, :]
    cos = jnp.cos(ang)[None, :, None, :].astype(x.dtype)
    sin = jnp.sin(ang)[None, :, None, :].astype(x.dtype)
    x1 = x[..., :half]
    x2 = x[..., half:ROT_DIM]
    return jnp.concatenate([x1 * cos - x2 * sin, x2 * cos + x1 * sin, x[..., ROT_DIM:]], axis=-1)


def mem_kv(mem, g, w):
    B, M, _ = mem.shape
    kv = (rmsnorm(mem, g) @ w).reshape(B, M, 2, MEM_HEADS, MEM_HEAD_DIM)
    return kv[:, :, 0], kv[:, :, 1]


def mem_attention(q, mk, mv):
    s = jnp.einsum('blhd,bmhd->bhlm', q, mk).astype(jnp.float32) * MEM_HEAD_DIM ** -0.5
    p = jax.nn.softmax(s, axis=-1).astype(mv.dtype)
    return jnp.einsum('bhlm,bmhd->blhd', p, mv)


def sink_softmax(s, sinks):
    sink = jnp.broadcast_to(sinks.astype(jnp.float32)[:, :, None, None], s.shape[:-1] + (1,))
    p = jax.nn.softmax(jnp.concatenate([s, sink], axis=-1), axis=-1)
    return p[..., :-1]


def gla_chunked(q, k, v, log_a):
    B, L, H, DK = q.shape
    DV = v.shape[-1]
    C = GLA_CHUNK
    nc = L // C

    def to_chunks(t):
        return t.reshape(B, nc, C, H, t.shape[-1]).transpose(1, 0, 3, 2, 4).astype(jnp.float32)

    qc, kc, vc, gc = to_chunks(q), to_chunks(k), to_chunks(v), to_chunks(log_a)
    causal = jnp.tril(jnp.ones((C, C), dtype=bool))

    def step(S, inp):
        qi, ki, vi, gi = inp
        b = jnp.cumsum(gi, axis=2)
        diff = b[:, :, :, None, :] - b[:, :, None, :, :]
        decay = jnp.exp(jnp.where(causal[:, :, None], diff, -jnp.inf))
        A = jnp.einsum('bhid,bhjd,bhijd->bhij', qi, ki, decay)
        o = jnp.einsum('bhij,bhjv->bhiv', A, vi) + jnp.einsum('bhid,bhdv->bhiv', qi * jnp.exp(b), S)
        b_last = b[:, :, -1:, :]
        S = jnp.exp(b_last[:, :, 0, :])[..., None] * S + jnp.einsum(
            'bhjd,bhjv->bhdv', ki * jnp.exp(b_last - b), vi)
        return S, o

    S0 = jnp.zeros((B, H, DK, DV), jnp.float32)
    S, o = lax.scan(step, S0, (qc, kc, vc, gc))
    return o.transpose(1, 0, 3, 2, 4).reshape(B, L, H, DV), S


def gla_recurrent(q, k, v, log_a, S0):
    def step(S, inp):
        qt, kt, vt, gt = inp
        S = jnp.exp(gt)[..., None] * S + kt[..., :, None] * vt[..., None, :]
        return S, jnp.einsum('bhk,bhkv->bhv', qt, S)

    xs = tuple(t.astype(jnp.float32).transpose(1, 0, 2, 3) for t in (q, k, v, log_a))
    S, o = lax.scan(step, S0.astype(jnp.float32), xs)
    return o.transpose(1, 0, 2, 3), S


def swa_prompt(q, k, v, sinks):
    B, L, HQ, HD = q.shape
    KVH = k.shape[2]
    G = HQ // KVH
    BLK = WINDOW
    nb = L // BLK
    qb = q.reshape(B, nb, BLK, KVH, G, HD)

    def band(t):
        tb = t.reshape(B, nb, BLK, KVH, HD)
        prev = jnp.concatenate([jnp.zeros_like(tb[:, :1]), tb[:, :-1]], axis=1)
        return jnp.concatenate([prev, tb], axis=2)

    kb, vb = band(k), band(v)
    s = jnp.einsum('bnqkgd,bnskd->bnkgqs', qb, kb).astype(jnp.float32) * HD ** -0.5
    blk = jnp.arange(nb)[:, None, None]
    qpos = blk * BLK + jnp.arange(BLK)[None, :, None]
    kpos = (blk - 1) * BLK + jnp.arange(2 * BLK)[None, None, :]
    d = qpos - kpos
    mask = (d >= 0) & (d < WINDOW) & (kpos >= 0)
    s = jnp.where(mask[None, :, None, None], s, -jnp.inf)
    p = sink_softmax(s, sinks.reshape(KVH, G))
    o = jnp.einsum('bnkgqs,bnskd->bnqkgd', p.astype(vb.dtype), vb)
    return o.reshape(B, L, HQ * HD)


def swa_sample(q, k_all, v_all, sinks):
    B, Lq, HQ, HD = q.shape
    KVH = k_all.shape[2]
    G = HQ // KVH
    W = k_all.shape[1] - Lq
    qg = q.reshape(B, Lq, KVH, G, HD)
    s = jnp.einsum('bqkgd,bskd->bkgqs', qg, k_all).astype(jnp.float32) * HD ** -0.5
    d = (W + jnp.arange(Lq))[:, None] - jnp.arange(W + Lq)[None, :]
    mask = (d >= 0) & (d < WINDOW)
    s = jnp.where(mask, s, -jnp.inf)
    p = sink_softmax(s, sinks.reshape(KVH, G))
    o = jnp.einsum('bkgqs,bskd->bqkgd', p.astype(v_all.dtype), v_all)
    return o.reshape(B, Lq, HQ * HD)


def trunk(x, pos, mem_k, mem_v, gla_s0, buf_k, buf_v, w, decode):
    B, L, _ = x.shape
    gla_states = []
    for l in range(DEPTH):
        if l == N_A_LAYERS:
            kv = (rmsnorm(x, w['kv_norm']) @ w['w_kv']).reshape(B, L, 2, SWA_KV_HEADS, HEAD_DIM)
            k_sh = rope_partial(kv[:, :, 0], pos)
            v_sh = kv[:, :, 1]
            if decode:
                k_att = jnp.concatenate([buf_k.astype(k_sh.dtype), k_sh], axis=1)
                v_att = jnp.concatenate([buf_v.astype(v_sh.dtype), v_sh], axis=1)
                n_keep = buf_k.shape[1]
            else:
                k_att, v_att = k_sh, v_sh
                n_keep = min(WINDOW, L)
            swa_k_new = k_att[:, -n_keep:]
            swa_v_new = v_att[:, -n_keep:]

        x = x + swiglu_half(x, w['ffn1_norm'][l], w['ffn1_w_gu'][l], w['ffn1_w_down'][l])
        h = rmsnorm(x, w['mix_norm'][l])
        if l < N_A_LAYERS:
            a = l
            q, k, v, r, g_lr, mq = jnp.split(h @ w['a_w_in'][a], A_SPLITS, axis=-1)
            q = q.reshape(B, L, GLA_HEADS, GLA_DK) * GLA_DK ** -0.5
            k = k.reshape(B, L, GLA_HEADS, GLA_DK)
            v = v.reshape(B, L, GLA_HEADS, GLA_DV)
            log_a = jax.nn.log_sigmoid((g_lr @ w['a_w_gate'][a] + w['a_b_gate'][a]).astype(jnp.float32))
            log_a = (log_a / GLA_GATE_NORM).reshape(B, L, GLA_HEADS, GLA_DK)
            if decode:
                o, S = gla_recurrent(q, k, v, log_a, gla_s0[a])
            else:
                o, S = gla_chunked(q, k, v, log_a)
            gla_states.append(S.astype(x.dtype))
            o = rmsnorm(o.astype(x.dtype), w['a_out_norm'][a]) * jax.nn.silu(r.reshape(B, L, GLA_HEADS, GLA_DV))
            tok = o.reshape(B, L, GLA_V)
            w_out = w['a_w_out'][a]
        else:
            bi = l - N_A_LAYERS
            qs, mq = jnp.split(h @ w['b_w_in'][bi], [SWA_Q], axis=-1)
            qs = rope_partial(qs.reshape(B, L, SWA_Q_HEADS, HEAD_DIM), pos)
            if decode:
                tok = swa_sample(qs, k_att, v_att, w['b_sinks'][bi])
            else:
                tok = swa_prompt(qs, k_sh, v_sh, w['b_sinks'][bi])
            w_out = w['b_w_out'][bi]
        mo = mem_attention(mq.reshape(B, L, MEM_HEADS, MEM_HEAD_DIM), mem_k[l], mem_v[l]).reshape(B, L, MEM_Q)
        x = x + jnp.concatenate([tok, mo], axis=-1) @ w_out
        x = x + swiglu_half(x, w['ffn2_norm'][l], w['ffn2_w_gu'][l], w['ffn2_w_down'][l])
    y = rmsnorm(x, w['final_norm'])
    return y, jnp.stack(gla_states), swa_k_new, swa_v_new


def setup_inputs(seed: int = 0) -> dict:
    key = jax.random.key(seed)
    ks = iter(jax.random.split(key, 48))

    def nrm(shape, scale=1.0):
        return jax.random.normal(next(ks), shape, jnp.float32) * scale

    def gain(shape):
        return 1.0 + 0.05 * nrm(shape)

    w_buf = min(WINDOW, PAST_LEN)
    return {
        'x_prompt': nrm((BATCH, SEQ, D_MODEL)),
        'x_sample': nrm((DEC_BATCH, DEC_SEQ, D_MODEL)),
        'state_gla': nrm((N_A_LAYERS, DEC_BATCH, GLA_HEADS, GLA_DK, GLA_DV)),
        'cache_swa_k': nrm((DEC_BATCH, w_buf, SWA_KV_HEADS, HEAD_DIM)),
        'cache_swa_v': nrm((DEC_BATCH, w_buf, SWA_KV_HEADS, HEAD_DIM)),
        'cache_mem_k': nrm((DEPTH, DEC_BATCH, MEM_TOKENS, MEM_HEADS, MEM_HEAD_DIM)),
        'cache_mem_v': nrm((DEPTH, DEC_BATCH, MEM_TOKENS, MEM_HEADS, MEM_HEAD_DIM)),
        'mem_prompt': nrm((BATCH, MEM_TOKENS, D_MODEL)),
        'ffn1_norm': gain((DEPTH, D_MODEL)),
        'ffn1_w_gu': nrm((DEPTH, D_MODEL, 2 * FFN_DIM), D_MODEL ** -0.5),
        'ffn1_w_down': nrm((DEPTH, FFN_DIM, D_MODEL), FFN_DIM ** -0.5),
        'mix_norm': gain((DEPTH, D_MODEL)),
        'ffn2_norm': gain((DEPTH, D_MODEL)),
        'ffn2_w_gu': nrm((DEPTH, D_MODEL, 2 * FFN_DIM), D_MODEL ** -0.5),
        'ffn2_w_down': nrm((DEPTH, FFN_DIM, D_MODEL), FFN_DIM ** -0.5),
        'mem_norm': gain((DEPTH, D_MODEL)),
        'mem_w_kv': nrm((DEPTH, D_MODEL, 2 * MEM_Q), D_MODEL ** -0.5),
        'a_w_in': nrm((N_A_LAYERS, D_MODEL, A_IN), D_MODEL ** -0.5),
        'a_w_gate': nrm((N_A_LAYERS, GLA_GATE_RANK, GLA_QK), GLA_GATE_RANK ** -0.5),
        'a_b_gate': nrm((N_A_LAYERS, GLA_QK), 0.1),
        'a_out_norm': gain((N_A_LAYERS, GLA_DV)),
        'a_w_out': nrm((N_A_LAYERS, A_MIX, D_MODEL), A_MIX ** -0.5),
        'kv_norm': gain((D_MODEL,)),
        'w_kv': nrm((D_MODEL, KV_SHARED), D_MODEL ** -0.5),
        'b_w_in': nrm((N_B_LAYERS, D_MODEL, B_IN), D_MODEL ** -0.5),
        'b_sinks': nrm((N_B_LAYERS, SWA_Q_HEADS), 0.5),
        'b_w_out': nrm((N_B_LAYERS, B_MIX, D_MODEL), B_MIX ** -0.5),
        'final_norm': gain((D_MODEL,)),
    }


def reference(x_prompt, x_sample, state_gla, cache_swa_k, cache_swa_v, cache_mem_k, cache_mem_v,
              mem_prompt, ffn1_norm, ffn1_w_gu, ffn1_w_down, mix_norm, ffn2_norm, ffn2_w_gu,
              ffn2_w_down, mem_norm, mem_w_kv, a_w_in, a_w_gate, a_b_gate, a_out_norm, a_w_out,
              kv_norm, w_kv, b_w_in, b_sinks, b_w_out, final_norm):
    w = dict(ffn1_norm=ffn1_norm, ffn1_w_gu=ffn1_w_gu, ffn1_w_down=ffn1_w_down, mix_norm=mix_norm,
             ffn2_norm=ffn2_norm, ffn2_w_gu=ffn2_w_gu, ffn2_w_down=ffn2_w_down,
             a_w_in=a_w_in, a_w_gate=a_w_gate, a_b_gate=a_b_gate, a_out_norm=a_out_norm,
             a_w_out=a_w_out, kv_norm=kv_norm, w_kv=w_kv, b_w_in=b_w_in, b_sinks=b_sinks,
             b_w_out=b_w_out, final_norm=final_norm)

    mks, mvs = [], []
    for l in range(DEPTH):
        mk, mv = mem_kv(mem_prompt, mem_norm[l], mem_w_kv[l])
        mks.append(mk)
        mvs.append(mv)
    mem_k_prompt = jnp.stack(mks)
    mem_v_prompt = jnp.stack(mvs)

    pos_p = jnp.arange(x_prompt.shape[1], dtype=jnp.float32)
    y_prompt, gla_prompt, swa_k_prompt, swa_v_prompt = trunk(
        x_prompt, pos_p, mem_k_prompt, mem_v_prompt, None, None, None, w, False)

    pos_s = PAST_LEN + jnp.arange(x_sample.shape[1], dtype=jnp.float32)
    y_sample, gla_sample, swa_k_sample, swa_v_sample = trunk(
        x_sample, pos_s, cache_mem_k, cache_mem_v, state_gla, cache_swa_k, cache_swa_v, w, True)

    return (y_prompt, y_sample, gla_prompt, gla_sample, swa_k_prompt, swa_v_prompt,
            swa_k_sample, swa_v_sample, mem_k_prompt, mem_v_prompt)
```

```python
import math
from bisect import bisect_left
from contextlib import ExitStack

import numpy as np
import concourse.bass as bass
import concourse.mybir as mybir
from concourse.bass_utils import run_bass_kernel_spmd

F32 = mybir.dt.float32
BF16 = mybir.dt.bfloat16
ALU = mybir.AluOpType
AF = mybir.ActivationFunctionType

D = 1024
FF = 2816
NCH = 22
EPS = 1e-6
NPP = 1024
NSM = 64
DBG = {}


class Op:
    __slots__ = ("idx", "eng", "fn", "deps", "fdeps", "dma", "signal", "sigval", "sem", "semval")


class Ent:
    __slots__ = ("w", "r")

    def __init__(self):
        self.w = None
        self.r = {}


class Space:
    def __init__(self):
        self.b = [0, 1 << 40]
        self.e = [Ent()]

    def split(self, x):
        i = bisect_left(self.b, x)
        if self.b[i] == x:
            return i
        old = self.e[i - 1]
        new = Ent()
        new.w = old.w
        new.r = dict(old.r)
        self.b.insert(i, x)
        self.e.insert(i, new)
        return i

    def segs(self, a, b):
        i = self.split(a)
        j = self.split(b)
        return self.e[i:j]


def ap_ranges(ap):
    dims = ap.ap
    esz = mybir.dt.size(ap.dtype)
    pstride = dims[0][0]
    off = int(ap.offset)
    if pstride:
        off = off % pstride
    free = [(s, n) for (s, n) in dims[1:] if n > 1 and s != 0]
    free.sort(key=lambda x: -x[0])
    run = 1
    while free and free[-1][0] == run:
        run *= free[-1][1]
        free.pop()
    starts = [off]
    for s, n in free:
        starts = [st + i * s for st in starts for i in range(n)]
        if len(starts) > 64:
            break
    if len(starts) > 64:
        hi = off + sum(s * (n - 1) for (s, n) in dims[1:] if n > 1 and s > 0) + 1
        return [(off * esz, hi * esz)]
    return [(st * esz, (st + run) * esz) for st in starts]


class Prog:
    ENGS = ("pe", "act", "dve", "pool", "sp")
    NS = 8

    def __init__(self, nc):
        self.nc = nc
        self.ops = []
        self.spaces = {}
        self.onchip = set()
        self.pe_recent = []

    def add(self, eng, fn, reads=(), writes=(), dma=False, pe_rows=None):
        o = Op()
        o.idx = len(self.ops)
        o.eng = eng
        o.fn = fn
        o.dma = dma
        o.signal = False
        o.sigval = 0
        o.sem = None
        o.semval = 0
        o.fdeps = []
        deps = {}
        rkey = ("dma", o.idx) if dma else eng
        acc = []
        for ap in reads:
            if ap is None or ap.name not in self.onchip:
                continue
            acc.append((ap, ap.name == "psum"))
        for ap in writes:
            if ap is None or ap.name not in self.onchip:
                continue
            acc.append((ap, True))
        banks = set()
        for ap, wr in acc:
            sp = self.spaces.setdefault(ap.name, Space())
            rngs = ap_ranges(ap)
            if ap.name == "psum":
                lo = min(r[0] for r in rngs) // 2048
                hi = (max(r[1] for r in rngs) + 2047) // 2048
                rngs = [(lo * 2048, hi * 2048)]
                if wr and eng == "pe":
                    banks.update(range(lo, hi))
            for (a_, b_) in rngs:
                for e in sp.segs(a_, b_):
                    if e.w is not None:
                        deps[e.w.idx] = e.w
                    if wr:
                        for r in e.r.values():
                            deps[r.idx] = r
                        e.w = o
                        e.r = {}
                    else:
                        e.r[rkey] = o
        deps.pop(o.idx, None)
        red = {}
        for d in deps.values():
            if d.dma:
                red[("dma", d.idx)] = d
            else:
                cur = red.get(d.eng)
                if cur is None or cur.idx < d.idx:
                    red[d.eng] = d
        o.deps = list(red.values())
        if eng == "pe" and pe_rows is not None:
            for (p, prow, pbanks) in reversed(self.pe_recent):
                if not (prow[1] <= pe_rows[0] or pe_rows[1] <= prow[0]):
                    break
                if pbanks & banks:
                    o.fdeps.append(p)
                    break
            self.pe_recent.append((o, pe_rows, banks))
            if len(self.pe_recent) > 32:
                self.pe_recent.pop(0)
        self.ops.append(o)
        return o

    def emit(self, stack):
        nc = self.nc
        ndma = {e: 0 for e in self.ENGS}
        for o in self.ops:
            for d in o.deps:
                if not d.dma:
                    if d.eng == "pe" and o.eng == "pe" and not o.dma:
                        continue
                    d.signal = True
            for d in o.fdeps:
                d.signal = True
        cnt = {e: 0 for e in self.ENGS}
        for o in self.ops:
            if o.dma:
                n = ndma[o.eng]
                ndma[o.eng] = n + 1
                o.sem = (o.eng, n % self.NS)
                o.semval = 16 * (n // self.NS + 1)
            elif o.signal:
                cnt[o.eng] += 1
                o.sigval = cnt[o.eng]
        if DBG.get("KDEBUG"):
            print("SEMCOUNTS", cnt, {e: 16 * (v // self.NS + 1) for e, v in ndma.items()}, flush=True)
        sems = {}
        for e in self.ENGS:
            sems[e] = stack.enter_context(nc.semaphore("s_" + e))
        dsems = {}
        for e in self.ENGS:
            for i in range(min(self.NS, ndma[e])):
                dsems[(e, i)] = stack.enter_context(nc.semaphore("d_%s_%d" % (e, i)))
        block = stack.enter_context(nc.Block())
        by_eng = {e: [o for o in self.ops if o.eng == e] for e in self.ENGS}

        def run(engname, eng):
            waited = {}
            for o in by_eng[engname]:
                for d in o.deps:
                    if d.dma:
                        k = d.sem
                        v = d.semval
                        s = dsems[k]
                    else:
                        if d.eng == "pe" and engname == "pe" and not o.dma:
                            continue
                        k = d.eng
                        v = d.sigval
                        s = sems[k]
                    if waited.get(k, 0) < v:
                        eng.wait_ge(s, v)
                        waited[k] = v
                for d in o.fdeps:
                    if waited.get(d.eng, 0) < d.sigval:
                        eng.wait_ge(sems[d.eng], d.sigval)
                        waited[d.eng] = d.sigval
                if o.dma:
                    if o.semval > 16 and waited.get(o.sem, 0) < o.semval - 16:
                        eng.wait_ge(dsems[o.sem], o.semval - 16)
                        waited[o.sem] = o.semval - 16
                    o.fn(eng).then_inc(dsems[o.sem], 16)
                else:
                    ins = o.fn(eng)
                    if o.signal:
                        ins.then_inc(sems[engname], 1)
            last = {}
            for o in by_eng[engname]:
                if o.dma:
                    last[o.sem] = max(last.get(o.sem, 0), o.semval)
            for k, v in last.items():
                if waited.get(k, 0) < v:
                    eng.wait_ge(dsems[k], v)

        @block.tensor
        def _(e):
            run("pe", e)

        @block.scalar
        def _(e):
            run("act", e)

        @block.vector
        def _(e):
            run("dve", e)

        @block.gpsimd
        def _(e):
            run("pool", e)

        @block.sync
        def _(e):
            run("sp", e)


C_TRIU = 0
C_TRIG = 128
C_SU = 256
C_SG = 320
C_OH = 384
C_MC = 400
C_MN = 404
C_W = 468


def make_consts():
    c = np.zeros((128, C_W), np.float32)
    p = np.arange(128)[:, None]
    i = np.arange(128)[None, :]
    c[:, C_TRIU:C_TRIU + 128] = (p <= i)
    c[:, C_TRIG:C_TRIG + 128] = (p > i)
    p6 = np.arange(64)[:, None]
    i6 = np.arange(64)[None, :]
    same = (p6 // 4) == (i6 // 4)
    c[:64, C_SU:C_SU + 64] = same & (p6 <= i6)
    c[:64, C_SG:C_SG + 64] = same & (p6 > i6)
    c[:64, C_OH:C_OH + 16] = (p6 // 4) == np.arange(16)[None, :]
    c[:, C_MC:C_MC + 4] = (p >= (np.arange(4)[None, :] + 1))
    mn = np.zeros((64, 16, 4), np.float32)
    for pp in range(64):
        for t in range(4):
            if pp % 4 <= t:
                mn[pp, pp // 4, t] = 1.0
    c[:64, C_MN:C_MN + 64] = mn.reshape(64, 64)
    return c


def make_rope():
    rot = 16
    half = 8
    inv_freq = np.exp(-math.log(500000.0) * np.arange(0, rot, 2, dtype=np.float32) / rot).astype(np.float32)
    out = np.zeros((2, 2, 128, NPP + NSM), np.float32)
    for ps in range(2):
        pos = np.concatenate([np.arange(ps * NPP, (ps + 1) * NPP, dtype=np.float32),
                              np.tile(8192.0 + np.arange(4, dtype=np.float32), 16)]).astype(np.float32)
        ang = (pos[:, None] * inv_freq[None, :]).astype(np.float32)
        cs = np.cos(ang).astype(np.float32).T
        sn = np.sin(ang).astype(np.float32).T
        cosr = np.ones((64, pos.shape[0]), np.float32)
        sinr = np.zeros((64, pos.shape[0]), np.float32)
        cosr[0:half] = cs
        cosr[half:rot] = cs
        sinr[0:half] = -sn
        sinr[half:rot] = sn
        out[ps, 0] = np.concatenate([cosr, cosr], 0)
        out[ps, 1] = np.concatenate([sinr, sinr], 0)
    return out


def rope_perm_cols(ncols):
    idx = np.arange(ncols)
    d = idx % 64
    partner = np.where(d < 8, idx + 8, np.where(d < 16, idx - 8, idx))
    return partner


def build_nc():
    nc = bass.Bass("TRN2", target_bir_lowering=False)
    P = Prog(nc)

    def din(name, shape):
        return nc.dram_tensor(name, list(shape), F32, kind="ExternalInput").ap()

    def dout(name, shape):
        return nc.dram_tensor(name, list(shape), F32, kind="ExternalOutput").ap()

    xT_p = din("xT_p", [D, 2048])
    xT_s = din("xT_s", [D, NSM])
    memT = din("memT", [D, 256])
    state = din("state", [16, 4, 96, 192])
    kcT = din("kcT", [16, 3, 64, 128])
    vc = din("vc", [16, 128, 192])
    cmkT = din("cmkT", [2, 16, 4, 64, 256])
    cmv = din("cmv", [2, 16, 256, 256])
    w_gu = [din("w_gu1", [2, D, 2 * FF]), din("w_gu2", [2, D, 2 * FF])]
    w_dn = [din("w_d1", [2, FF, D]), din("w_d2", [2, FF, D])]
    w_memkv = din("w_memkv", [2, D, 512])
    a_w_in = din("a_w_in", [D, 2576])
    a_w_gate = din("a_w_gate", [16, 384])
    a_b_gate = din("a_b_gate", [1, 384])
    a_w_out = din("a_w_out", [D, D])
    w_kvx = din("w_kvx", [D, 960])
    b_w_inx = din("b_w_inx", [D, 1792])
    b_w_out = din("b_w_out", [D, D])
    gains_d = din("gains", [128, 10, 8])
    gout_d = din("gout", [128, 2])
    sinks_d = din("sinks", [128, 12])
    consts_d = din("consts", [128, C_W])
    rope_d = din("rope", [2, 2, 128, NPP + NSM])

    yT_o = dout("yT", [D, 2048 + NSM])
    gla_p_o = dout("gla_p", [4, 96, 192])
    gla_s_o = dout("gla_s", [16, 4, 96, 192])
    swa_kT_p_o = dout("swa_kT_p", [3, 64, 128])
    swa_v_p_o = dout("swa_v_p", [128, 192])
    swa_kT_s_o = dout("swa_kT_s", [16, 3, 64, 128])
    swa_v_s_o = dout("swa_v_s", [16, 128, 192])
    mem_k_o = dout("mem_k_p", [2, 256, 256])
    mem_v_o = dout("mem_v_p", [2, 256, 256])

    with ExitStack() as st, nc.allow_low_precision("bf16 matmul operands"), \
            nc.allow_non_contiguous_dma("strided weight/cache tiles"):

        def sbt(name, shape, dt):
            t = st.enter_context(nc.sbuf_tensor("sb_" + name, list(shape), dt))
            P.onchip.add("sb_" + name)
            return t

        NT = NPP + NSM
        xT = sbt("xT", [128, 8, NT], F32)
        gsc = sbt("gsc", [128, 10, 8], F32)
        gout = sbt("gout", [128, 2], F32)
        esink = sbt("esink", [128, 12], F32)
        cst = sbt("cst", [128, C_W], F32)
        ones_bf = sbt("ones_bf", [128, 128], BF16)
        ones_f = sbt("ones_f", [1, 128], F32)
        epsD = sbt("epsD", [128, 2], F32)
        wgate = sbt("wgate", [16, 384], F32)
        bgate = sbt("bgate", [1, 384], F32)
        cosT = sbt("cosT", [128, NT], F32)
        sinT = sbt("sinT", [128, NT], F32)
        mkT = sbt("mkT", [128, 2, 2, 256], BF16)
        mv = sbt("mv", [128, 2, 2, 256], BF16)
        S = sbt("S", [96, 4, 192], F32)
        S_bf = sbt("S_bf", [96, 4, 192], BF16)
        kTd = sbt("kTd", [128, 3, NT], BF16)
        vtokB = sbt("vtokB", [128, 9, 192], BF16)
        kT_prev = sbt("kT_prev", [128, 3, 128], BF16)
        v_prev = sbt("v_prev", [128, 192], BF16)
        kf_p = sbt("kf_p", [64, 3, 128], F32)
        vf_p = sbt("vf_p", [128, 192], F32)
        ARENA_BYTES = (nc.sbuf_bytes_remaining - 2048) // 64 * 64
        arena = sbt("arena", [128, ARENA_BYTES // 4], F32)
        psum = st.enter_context(nc.psum_tensor("psum", [128, 4096], F32))
        P.onchip.add("psum")

        state_ = {"aoff": 0, "bank": 0}

        tmpc = {}

        def arena_reset():
            state_["aoff"] = 0
            state_["atop"] = ARENA_BYTES
            tmpc.clear()

        def amark():
            return state_["aoff"]

        def apop(m):
            state_["aoff"] = m
            for k in [k for k, v in tmpc.items() if v[2] >= m]:
                del tmpc[k]

        def TM(key, shape, dt, nbuf=2):
            ent = tmpc.get(key)
            if ent is None:
                off = state_["aoff"]
                ent = tmpc[key] = [[A(shape, dt) for _ in range(nbuf)], 0, off]
            ent[1] += 1
            return ent[0][(ent[1] - 1) % nbuf]

        def A(shape, dt, top=False, at=None):
            esz = mybir.dt.size(dt)
            nfree = 1
            for s in shape[1:]:
                nfree *= s
            nbytes = (nfree * esz + 63) // 64 * 64
            if at is not None:
                off = at
            elif top:
                state_["atop"] -= nbytes
                off = state_["atop"]
            else:
                off = state_["aoff"]
                state_["aoff"] = off + nbytes
            assert state_["aoff"] <= state_["atop"], ("arena overflow", state_["aoff"], state_["atop"])
            state_["hw"] = max(state_.get("hw", 0), state_["aoff"])
            ap = arena[:, off // 4:(off + nbytes) // 4]
            if dt != F32:
                ap = ap.bitcast(dt)
            ap = ap[:, 0:nfree]
            if len(shape) == 3:
                ap = ap.rearrange("p (a b) -> p a b", a=shape[1])
            elif len(shape) == 4:
                ap = ap.rearrange("p (a b c) -> p a b c", a=shape[1], b=shape[2])
            if shape[0] < 128:
                ap = ap[0:shape[0]]
            return ap

        def PS(nb=1):
            b = state_["bank"]
            if nb == 2 and b % 2 == 1:
                b += 1
            if b + nb > 8:
                b = 0
            state_["bank"] = (b + nb) % 8
            return psum[:, b * 512:(b + nb) * 512]

        def mm(out, lhsT, rhs, start=True, stop=True):
            r0 = lhsT.base_partition()
            P.add("pe", lambda e: e.matmul(out, lhsT=lhsT, rhs=rhs, start=start, stop=stop),
                  reads=[lhsT, rhs], writes=[out], pe_rows=(r0, r0 + lhsT.shape[0]))

        def act(out, in_, func, bias=None, scale=1.0):
            if bias is None:
                P.add("act", lambda e: e.activation(out=out, in_=in_, func=func, scale=scale),
                      reads=[in_], writes=[out])
            elif isinstance(bias, float):
                P.add("act", lambda e: e.activation(out=out, in_=in_, func=func, bias=bias, scale=scale),
                      reads=[in_], writes=[out])
            else:
                P.add("act", lambda e: e.activation(out=out, in_=in_, func=func, bias=bias, scale=scale),
                      reads=[in_, bias], writes=[out])

        def tt(eng, out, in0, in1, op):
            P.add(eng, lambda e: e.tensor_tensor(out=out, in0=in0, in1=in1, op=op), reads=[in0, in1], writes=[out])

        def ts(eng, out, in0, s1, op0, s2=None, op1=None):
            rd = [in0] + [s for s in (s1, s2) if s is not None and not isinstance(s, (int, float))]
            if op1 is None:
                P.add(eng, lambda e: e.tensor_scalar(out=out, in0=in0, scalar1=s1, scalar2=None, op0=op0),
                      reads=rd, writes=[out])
            else:
                P.add(eng, lambda e: e.tensor_scalar(out=out, in0=in0, scalar1=s1, scalar2=s2, op0=op0, op1=op1),
                      reads=rd, writes=[out])

        def stt(eng, out, in0, scalar, in1, op0, op1):
            rd = [in0, in1] + ([] if isinstance(scalar, (int, float)) else [scalar])
            P.add(eng, lambda e: e.scalar_tensor_tensor(out=out, in0=in0, scalar=scalar, in1=in1, op0=op0, op1=op1),
                  reads=rd, writes=[out])

        def cp(eng, out, in_):
            if eng == "act":
                P.add("act", lambda e: e.copy(out=out, in_=in_), reads=[in_], writes=[out])
            else:
                P.add(eng, lambda e: e.tensor_copy(out=out, in_=in_), reads=[in_], writes=[out])

        def recip(out, in_):
            P.add("dve", lambda e: e.reciprocal(out=out, in_=in_), reads=[in_], writes=[out])

        def act_recip(out, in_):
            act(out, in_, AF.Ln)
            act(out, out, AF.Exp, scale=-1.0)

        def act_rsqrt(out, in_, eps_ap):
            act(out, in_, AF.Ln, bias=eps_ap)
            act(out, out, AF.Exp, scale=-0.5)

        def memset(eng, out, val):
            P.add(eng, lambda e: e.memset(out, val), writes=[out])

        def dma(eng, out, in_):
            P.add(eng, lambda e: e.dma_start(out=out, in_=in_), reads=[in_], writes=[out], dma=True)

        def kp(ap):
            return ap.rearrange("(k p) n -> p k n", p=128)

        MUL = ALU.mult
        ADD = ALU.add

        dma("sp", gsc[:], gains_d)
        dma("sp", gout[:], gout_d)
        dma("sp", esink[:], sinks_d)
        dma("sp", cst[:], consts_d)
        dma("sp", wgate[:], a_w_gate)
        dma("sp", bgate[:], a_b_gate)
        memset("dve", ones_bf[:], 1.0)
        memset("dve", ones_f[:], 1.0)
        memset("dve", epsD[:, 0:1], D * EPS)
        memset("dve", epsD[:, 1:2], 192 * EPS)
        memset("dve", S[:], 0.0)
        memset("dve", S_bf[:], 0.0)
        ts("dve", gsc[:], gsc[:], float(math.sqrt(D)), MUL)
        ts("dve", gout[:], gout[:], float(math.sqrt(192.0)), MUL)
        act(esink[:], esink[:], AF.Exp)

        triU = cst[:, C_TRIU:C_TRIU + 128]
        triG = cst[:, C_TRIG:C_TRIG + 128]
        sU = cst[0:64, C_SU:C_SU + 64]
        sG = cst[0:64, C_SG:C_SG + 64]
        onehot = cst[0:64, C_OH:C_OH + 16]
        maskC = cst[:, C_MC:C_MC + 4]
        maskN = cst[0:64, C_MN:C_MN + 64]

        def norm(src, n, gi, dst, sq):
            act(sq[:, :, 0:n], src, AF.Square)
            pp = PS()
            for k in range(8):
                mm(pp[:, 0:n], ones_bf[:], sq[:, k, 0:n], start=(k == 0), stop=(k == 7))
            sd = TM("nsd", [128, 512], F32)
            act_rsqrt(sd[:, 0:n], pp[:, 0:n], epsD[:, 0:1])
            for k in range(8):
                stt("dve", dst[:, k, 0:n], src[:, k, :], gsc[:, gi, k:k + 1], sd[:, 0:n], MUL, MUL)

        def load_pass_inputs(ps_, part=None):
            for pi, (a0, a1) in enumerate(((0, 512), (512, NPP))):
                if part is None or part == pi:
                    dma("sp", xT[:, :, a0:a1], kp(xT_p[:, ps_ * NPP + a0:ps_ * NPP + a1]))
            if part == 0:
                return
            if ps_ == 1:
                dma("sp", xT[:, :, NPP:NT], kp(xT_s))
            dma("sp", cosT[:], rope_d[ps_, 0])
            dma("sp", sinT[:], rope_d[ps_, 1])

        load_pass_inputs(0)

        arena_reset()
        memx = A([128, 8, 256], F32)
        sqm = A([128, 8, 256], BF16)
        hm = A([128, 8, 256], BF16)
        Wm = A([128, 8, 512], BF16)
        dma("sp", memx, kp(memT))
        for l in range(2):
            dma("pool", Wm, kp(w_memkv[l]))
            norm(memx, 256, 6 + l, hm, sqm)
            for jt in range(2):
                pp = PS()
                for k in range(8):
                    mm(pp[:, 0:512], hm[:, k, jt * 128:(jt + 1) * 128], Wm[:, k, :], start=(k == 0), stop=(k == 7))
                stg = TM("stg", [128, 512], F32)
                cp("act", stg, pp[:, 0:512])
                dma("sp", mem_k_o[l, jt * 128:(jt + 1) * 128, :], stg[:, 0:256])
                dma("sp", mem_v_o[l, jt * 128:(jt + 1) * 128, :], stg[:, 256:512])
                cp("dve", mv[:, l, jt, :], stg[:, 256:512])
            for c in range(2):
                pp = PS()
                for k in range(8):
                    mm(pp[:, 0:256], Wm[:, k, c * 128:(c + 1) * 128], hm[:, k, :], start=(k == 0), stop=(k == 7))
                cp("act", mkT[:, l, c, :], pp[:, 0:256])

        prepped = {}

        def ffn(l, which, tgs, ntp, hook=None):
            if DBG.get("KSKIPFFN"):
                return
            arena_reset()
            gi = (0 if which == 0 else 4) + l
            wgu = w_gu[which][l]
            wd = w_dn[which][l]
            hT = A([128, 8, ntp], BF16)
            sg = [A([128, 512], F32) for _ in range(2)]
            sq = A([128, 8, 512], BF16)
            for (c0, n) in tgs:
                norm(xT[:, :, c0:c0 + n], n, gi, hT[:, :, c0:c0 + n], sq)
            state_["aoff_lz"] = state_["aoff"]
            wbuf = [A([128, 8, 512], BF16, top=True) for _ in range(3)]
            Wds = [A([128, 12, D], BF16, top=True) for _ in range(2)]
            actT = A([128, 12, ntp], BF16)
            cnt = 0
            ui = 0
            halves = ((0, 10), (10, 12))

            def load_wd(hi):
                f0_, nf_ = halves[hi]
                dma("pool", Wds[hi][:, 0:nf_, :], wd[f0_ * 128:(f0_ + nf_) * 128, :].rearrange("(c p) n -> p c n", p=128))

            def load_unit(fc0_, buf_):
                dma("pool", buf_[:, :, 0:256], kp(wgu[:, fc0_ * 128:fc0_ * 128 + 256]))
                dma("pool", buf_[:, :, 256:512], kp(wgu[:, FF + fc0_ * 128:FF + fc0_ * 128 + 256]))

            units = [2 * u for u in range(11)]
            load_unit(units[0], wbuf[0])
            load_unit(units[1], wbuf[1])
            load_wd(0)
            nload = 2
            for hi, (f0, nf) in enumerate(halves):
                Wd = Wds[hi]
                for u in range(nf // 2):
                    fc0 = f0 + 2 * u
                    buf = wbuf[ui % 3]
                    ui += 1
                    if nload < len(units):
                        load_unit(units[nload], wbuf[nload % 3])
                        nload += 1
                    if hi == 0 and u == 1:
                        load_wd(1)
                    for j in range(2):
                        fi = 2 * u + j
                        for (c0, n) in tgs:
                            pg = PS()
                            pu = PS()
                            for k in range(8):
                                mm(pg[:, 0:n], buf[:, k, j * 128:(j + 1) * 128], hT[:, k, c0:c0 + n],
                                   start=(k == 0), stop=(k == 7))
                            for k in range(8):
                                mm(pu[:, 0:n], buf[:, k, 256 + j * 128:256 + (j + 1) * 128], hT[:, k, c0:c0 + n],
                                   start=(k == 0), stop=(k == 7))
                            s = sg[cnt % 2]
                            cnt += 1
                            act(s[:, 0:n], pg[:, 0:n], AF.Silu)
                            tt("dve", actT[:, fi, c0:c0 + n], s[:, 0:n], pu[:, 0:n], MUL)
                for ti, (c0, n) in enumerate(tgs):
                    for dc in range(8):
                        po = PS()
                        for fi in range(nf):
                            mm(po[:, 0:n], Wd[:, fi, dc * 128:(dc + 1) * 128], actT[:, fi, c0:c0 + n],
                               start=(fi == 0), stop=(fi == nf - 1))
                        stt("dve", xT[:, dc, c0:c0 + n], po[:, 0:n], 0.5, xT[:, dc, c0:c0 + n], MUL, ADD)
                        if hook is not None and hi == 1 and ti == 1 and dc == 0:
                            lz = state_["aoff_lz"]
                            hook()
                            assert state_["aoff"] <= lz, ("landing zone overflow", state_["aoff"], lz)

        def phase_entry(key, mode, c0, n, gi, wspecs):
            if mode == "body":
                d = prepped.pop(key)
                state_["aoff"] = d["aoff"]
                state_["atop"] = ARENA_BYTES
                tmpc.clear()
                return d["hT"], d["W"]
            arena_reset()
            W = []
            if mode == "all":
                for (shape, src) in wspecs:
                    w = A(shape, BF16)
                    dma("pool", w, src)
                    W.append(w)
            hT = A([128, 8, n], BF16)
            mk_ = amark()
            sq = A([128, 8, n], BF16)
            norm(xT[:, :, c0:c0 + n], n, gi, hT, sq)
            apop(mk_)
            if mode == "prep":
                for (shape, src) in wspecs:
                    w = A(shape, BF16)
                    dma("pool", w, src)
                    W.append(w)
                prepped[key] = dict(hT=hT, W=W, aoff=state_["aoff"])
                return None, None
            return hT, W

        def mem_attn_prompt(l, mqT, n, mix, slot0):
            def BK(b_):
                return psum[:, b_ * 512:(b_ + 1) * 512]
            for pr_ in range(2):
                pts = {}
                for jt in range(2):
                    for mi in range(2):
                        m = 2 * pr_ + mi
                        c = m // 2
                        p0 = (m % 2) * 64
                        sp_ = BK(2 * mi + jt)
                        mm(sp_[:, 0:n], mkT[p0:p0 + 64, l, c, jt * 128:(jt + 1) * 128], mqT[p0:p0 + 64, c, 0:n])
                for mi in range(2):
                    for jt in range(2):
                        pt = TM("mpt", [128, 512], BF16, 4)
                        act(pt[:, 0:n], BK(2 * mi + jt)[:, 0:n], AF.Exp, scale=0.125)
                        pts[(mi, jt)] = pt
                for jt in range(2):
                    for mi in range(2):
                        m = 2 * pr_ + mi
                        p0 = (m % 2) * 64
                        mm(BK(4 + 2 * mi)[p0:p0 + 64, 0:n], mv[:, l, jt, m * 64:(m + 1) * 64], pts[(mi, jt)][:, 0:n],
                           start=(jt == 0), stop=(jt == 1))
                for jt in range(2):
                    for mi in range(2):
                        m = 2 * pr_ + mi
                        p0 = (m % 2) * 64
                        mm(BK(5 + 2 * mi)[p0:p0 + 64, 0:n], ones_bf[:, 0:64], pts[(mi, jt)][:, 0:n],
                           start=(jt == 0), stop=(jt == 1))
                for mi in range(2):
                    m = 2 * pr_ + mi
                    p0 = (m % 2) * 64
                    rinv = TM("mrinv", [128, 512], F32)
                    act_recip(rinv[p0:p0 + 64, 0:n], BK(5 + 2 * mi)[p0:p0 + 64, 0:n])
                    tt("dve", mix[p0:p0 + 64, slot0 + m // 2, 0:n], BK(4 + 2 * mi)[p0:p0 + 64, 0:n], rinv[p0:p0 + 64, 0:n], MUL)

        def load_mem_cache(l, hb, ck, cv):
            dma("pool", ck, cmkT[l, 8 * hb:8 * hb + 8].rearrange("b (c two) d j -> (two d) b c j", two=2))
            dma("pool", cv, cmv[l, 8 * hb:8 * hb + 8].rearrange("b (jt p) f -> p b jt f", p=128))

        def mem_attn_sample(l, mqT, mix, slot0, pre=None):
            for hb in range(2):
                if pre is not None:
                    ck, cv = pre[hb]
                else:
                    ck = TM("ck", [128, 8, 2, 256], BF16, 1)
                    cv = TM("cv", [128, 8, 2, 256], BF16, 1)
                    load_mem_cache(l, hb, ck, cv)
                sp_ = PS()
                for (m, b) in [(m_, b_) for m_ in (0, 2, 1, 3) for b_ in range(8)]:
                    bg = 8 * hb + b
                    c = m // 2
                    p0 = (m % 2) * 64
                    for jt in range(2):
                        col = jt * 128 + (b * 4 + m) * 4
                        mm(sp_[:, col:col + 4], ck[p0:p0 + 64, b, c, jt * 128:(jt + 1) * 128],
                           mqT[p0:p0 + 64, c, 4 * bg:4 * bg + 4])
                pt = TM("spt", [128, 256], BF16)
                act(pt, sp_[:, 0:256], AF.Exp, scale=0.125)
                po = PS()
                pr = PS()
                for b in range(8):
                    for m in range(4):
                        col = (b * 4 + m) * 4
                        p0 = (m % 2) * 64
                        for jt in range(2):
                            mm(po[p0:p0 + 64, col:col + 4], cv[:, b, jt, m * 64:(m + 1) * 64],
                               pt[:, jt * 128 + col:jt * 128 + col + 4], start=(jt == 0), stop=(jt == 1))
                for jt in range(2):
                    mm(pr[:, 0:128], ones_bf[:, :], pt[:, jt * 128:(jt + 1) * 128], start=(jt == 0), stop=(jt == 1))
                rinv = TM("srinv", [128, 128], F32)
                act_recip(rinv, pr[:, 0:128])
                for m in range(4):
                    p0 = (m % 2) * 64
                    ov = po[p0:p0 + 64, 0:128].rearrange("p (b m t) -> p b m t", b=8, m=4)[:, :, m, :]
                    rv = rinv[p0:p0 + 64, :].rearrange("p (b m t) -> p b m t", b=8, m=4)[:, :, m, :]
                    outv = mix[p0:p0 + 64, slot0 + m // 2, 32 * hb:32 * hb + 32].rearrange("p (b t) -> p b t", b=8)
                    tt("dve", outv, ov, rv, MUL)

        def mixer_a(c0, n, sample, mode="all", cache_copy=False):
            T = 64 if sample else 128
            ntile = 1 if sample else n // 128
            tmask = sU if sample else triU
            gmask = sG if sample else triG
            wspecs = [([128, 8, 272], kp(a_w_in[:, 2304:2576])), ([128, 8, 768], kp(a_w_in[:, 768:1536]))]
            if mode == "all":
                wspecs.append(([128, 8, 768], kp(a_w_in[:, 0:768])))
            hT, W = phase_entry("ma", mode, c0, n, 2, wspecs)
            if mode == "prep":
                return
            Wgm = W[0]
            if len(W) < 3:
                w1 = A([128, 8, 768], BF16)
                dma("pool", w1, kp(a_w_in[:, 0:768]))
                W.append(w1)
            Wa = [W[2], W[1]]
            Shs = {}
            if sample:
                ShT = [A([96, 16, 192], F32, top=True) for _ in range(2)]
                ShbT = [A([96, 16, 192], BF16, top=True) for _ in range(2)]

                def load_state(h_):
                    dma("sp", ShT[h_ % 2], state[:, h_].rearrange("b d v -> d b v"))
                    dma("pool", ShbT[h_ % 2], state[:, h_].rearrange("b d v -> d b v"))
                    Shs[h_] = (ShT[h_ % 2], ShbT[h_ % 2])

                load_state(0)
                load_state(1)
            if cache_copy:
                stgT = A([64, 4, 3, 124], F32, top=True)
                stvT = A([124, 4, 192], F32, top=True)
                for hb_ in range(4):
                    bs_ = slice(4 * hb_, 4 * hb_ + 4)
                    dma("sp", stgT, kcT[bs_, :, :, 4:128].rearrange("b h d j -> d b h j"))
                    dma("sp", stvT, vc[bs_, 4:128, :].rearrange("b j f -> j b f"))
                    dma("sp", swa_kT_s_o[bs_, :, :, 0:124].rearrange("b h d j -> d b h j"), stgT)
                    dma("sp", swa_v_s_o[bs_, 0:124, :].rearrange("b j f -> j b f"), stvT)
            qs = A([96, 4, n], BF16)
            ks = A([96, 4, n], BF16)
            khat = A([128, ntile, 384], BF16)
            vtok = A([128, ntile, 768], BF16)
            sr = A([128, 6, n], BF16)
            mix = A([128, 8, n], BF16)
            mqT = A([128, 2, n], BF16)
            ebl = A([96, 4, 64], F32)
            mk_ = amark()
            glr = A([16, n], F32)
            lp = A([128, ntile, 384], F32)
            ekr = A([128, ntile, 384], F32)
            pp = PS()
            for k in range(8):
                mm(pp[0:16, 0:n], Wgm[:, k, 0:16], hT[:, k, :], start=(k == 0), stop=(k == 7))
            cp("act", glr, pp[0:16, 0:n])
            for c in range(2):
                pp = PS()
                for k in range(8):
                    mm(pp[:, 0:n], Wgm[:, k, 16 + c * 128:16 + (c + 1) * 128], hT[:, k, :], start=(k == 0), stop=(k == 7))
                cp("act", mqT[:, c, :], pp[:, 0:n])
            for t in range(ntile):
                tc0 = t * 128
                pz = PS()
                mm(pz[0:T, 0:384], glr[:, tc0:tc0 + T], wgate[:], start=True, stop=False)
                mm(pz[0:T, 0:384], ones_f[0:1, 0:T], bgate[:], start=False, stop=True)
                e1 = TM("e1", [128, 384], F32)
                act(e1[0:T], pz[0:T, 0:384], AF.Exp, scale=-1.0)
                act(lp[0:T, t, :], e1[0:T], AF.Ln, bias=1.0)
                pr = PS()
                mm(pr[0:T, 0:384], gmask[0:T, 0:T], lp[0:T, t, :])
                act(ekr[0:T, t, :], pr[0:T, 0:384], AF.Exp, scale=-1.0 / 16.0)
            for t in range(ntile):
                tc0 = t * 128
                pa = PS()
                pb = PS()
                for k in range(8):
                    mm(pa[0:T, 0:512], hT[:, k, tc0:tc0 + T], Wa[1][:, k, 0:512], start=(k == 0), stop=(k == 7))
                for k in range(8):
                    mm(pb[0:T, 0:256], hT[:, k, tc0:tc0 + T], Wa[1][:, k, 512:768], start=(k == 0), stop=(k == 7))
                cp("act", vtok[0:T, t, 0:512], pa[0:T, 0:512])
                cp("act", vtok[0:T, t, 512:768], pb[0:T, 0:256])
            dma("pool", Wa[1], kp(a_w_in[:, 1536:2304]))
            for h in range(4):
                ebh = TM("ebh", [96, n], F32)
                enbh = TM("enbh", [96, n], F32)
                for t in range(ntile):
                    tc0 = t * 128
                    pb = PS()
                    mm(pb[0:96, 0:T], lp[0:T, t, 96 * h:96 * h + 96], tmask[0:T, 0:T])
                    act(ebh[:, tc0:tc0 + T], pb[0:96, 0:T], AF.Exp, scale=-1.0 / 16.0)
                    act(enbh[:, tc0:tc0 + T], pb[0:96, 0:T], AF.Exp, scale=1.0 / 16.0)
                if sample:
                    cp("dve", ebl[:, h, :], ebh[:, 0:64])
                else:
                    cp("dve", ebl[:, h, 0:ntile], ebh.rearrange("p (t i) -> p t i", i=128)[:, :, 127])
                pq = PS()
                for k in range(8):
                    mm(pq[0:96, 0:n], Wa[0][:, k, 96 * h:96 * h + 96], hT[:, k, :], start=(k == 0), stop=(k == 7))
                stt("dve", qs[:, h, :], pq[0:96, 0:n], float(96.0 ** -0.5), ebh, MUL, MUL)
                pk = PS()
                for k in range(8):
                    mm(pk[0:96, 0:n], Wa[0][:, k, 384 + 96 * h:384 + 96 * h + 96], hT[:, k, :], start=(k == 0), stop=(k == 7))
                tt("dve", ks[:, h, :], pk[0:96, 0:n], enbh, MUL)
            for t in range(ntile):
                tc0 = t * 128
                pk = PS()
                for k in range(8):
                    mm(pk[0:T, 0:384], hT[:, k, tc0:tc0 + T], Wa[0][:, k, 384:768], start=(k == 0), stop=(k == 7))
                tt("dve", khat[0:T, t, :], pk[0:T, 0:384], ekr[0:T, t, :], MUL)
            for ch in range(6):
                pa = PS()
                for k in range(8):
                    mm(pa[:, 0:n], Wa[1][:, k, 128 * ch:128 * ch + 128], hT[:, k, :], start=(k == 0), stop=(k == 7))
                act(sr[:, ch, :], pa[:, 0:n], AF.Silu)
            apop(mk_)
            Wo = A([128, 8, D], BF16, at=0)
            for h in range(4):
                p0 = (h % 2) * 64
                dma("pool", Wo[:, h, :], a_w_out[192 * h:192 * h + 128, :])
                dma("pool", Wo[p0:p0 + 64, 4 + h // 2, :], a_w_out[192 * h + 128:192 * h + 192, :])
            dma("pool", Wo[:, 6:8, :], a_w_out[768:1024, :].rearrange("(s p) n -> p s n", p=128))

            def BK(b_):
                return psum[:, b_ * 512:(b_ + 1) * 512]

            def gate_store(h, tc0, oA, oB, rsA, rsB):
                p0 = (h % 2) * 64
                ta = TM("fta%d" % h, [128, 128], F32)
                tb = TM("ftb%d" % h, [128, 128], F32)
                stt("dve", ta[:, 0:T], oA, gout[:, 0:1], rsA, MUL, MUL)
                tt("pool", mix[:, h, tc0:tc0 + T], ta[:, 0:T], sr[:, h, tc0:tc0 + T], MUL)
                stt("dve", tb[p0:p0 + 64, 0:T], oB, gout[p0:p0 + 64, 1:2], rsB, MUL, MUL)
                tt("pool", mix[p0:p0 + 64, 4 + h // 2, tc0:tc0 + T], tb[p0:p0 + 64, 0:T],
                   sr[p0:p0 + 64, 4 + h // 2, tc0:tc0 + T], MUL)

            if not sample:
                ams = {}
                pob = {}
                sqs = {}

                def S1(t):
                    tc0 = t * 128
                    for h in range(4):
                        mm(BK(0)[:, h * 128:(h + 1) * 128], ks[:, h, tc0:tc0 + 128], qs[:, h, tc0:tc0 + 128])

                def S2(t):
                    for h in range(4):
                        am = TM("am%d" % h, [128, 128], BF16)
                        tt("dve", am, BK(0)[:, h * 128:(h + 1) * 128], triU, MUL)
                        ams[(t, h)] = am

                def S3(t):
                    tc0 = t * 128
                    for h in range(4):
                        p0 = (h % 2) * 64
                        bk = BK(2 + state_["gl"] % 6)
                        state_["gl"] += 1
                        pob[(t, h)] = bk
                        am = ams[(t, h)]
                        mm(bk[:, 0:128], vtok[:, t, 192 * h:192 * h + 128], am, start=True, stop=False)
                        mm(bk[:, 0:128], S_bf[:, h, 0:128], qs[:, h, tc0:tc0 + 128], start=False, stop=True)
                        mm(bk[p0:p0 + 64, 128:256], vtok[:, t, 192 * h + 128:192 * h + 192], am, start=True, stop=False)
                        mm(bk[p0:p0 + 64, 128:256], S_bf[:, h, 128:192], qs[:, h, tc0:tc0 + 128], start=False, stop=True)
                        mm(bk[0:96, 256:448], khat[:, t, 96 * h:96 * h + 96], vtok[:, t, 192 * h:192 * h + 192])

                def S4(t):
                    for h in range(4):
                        p0 = (h % 2) * 64
                        bk = pob[(t, h)]
                        stt("dve", S[:, h, :], S[:, h, :], ebl[:, h, t:t + 1], bk[0:96, 256:448], MUL, ADD)
                        cp("act", S_bf[:, h, :], S[:, h, :])
                        sqa = TM("sqa%d" % h, [128, 128], BF16)
                        sqb = TM("sqb%d" % h, [128, 128], BF16)
                        act(sqa, bk[:, 0:128], AF.Square)
                        act(sqb[p0:p0 + 64, :], bk[p0:p0 + 64, 128:256], AF.Square)
                        sqs[(t, h)] = (sqa, sqb)

                def S5(t):
                    for h in range(4):
                        p0 = (h % 2) * 64
                        sqa, sqb = sqs[(t, h)]
                        mm(BK(1)[:, h * 128:(h + 1) * 128], ones_bf[:, :], sqa, start=True, stop=False)
                        mm(BK(1)[:, h * 128:(h + 1) * 128], ones_bf[p0:p0 + 64, :], sqb[p0:p0 + 64, :], start=False, stop=True)

                def S6(t):
                    tc0 = t * 128
                    rs = TM("frs4", [128, 512], F32)
                    act_rsqrt(rs, BK(1)[:, 0:512], epsD[:, 1:2])
                    for h in range(4):
                        p0 = (h % 2) * 64
                        bk = pob[(t, h)]
                        gate_store(h, tc0, bk[:, 0:128], bk[p0:p0 + 64, 128:256],
                                   rs[:, h * 128:(h + 1) * 128], rs[p0:p0 + 64, h * 128:(h + 1) * 128])

                state_["gl"] = 0
                S1(0)
                S2(0)
                for t in range(ntile):
                    S3(t)
                    if t + 1 < ntile:
                        S1(t + 1)
                        S2(t + 1)
                    S4(t)
                    S5(t)
                    S6(t)
            else:
                def out_stages(h):
                    p0 = (h % 2) * 64
                    Sh, Shb = Shs[h]
                    st_ = {}

                    def s1():
                        st_["pA"] = PS()
                        mm(st_["pA"][0:64, 0:64], ks[:, h, 0:64], qs[:, h, 0:64])

                    def s2():
                        st_["am"] = TM("ams", [64, 64], BF16)
                        tt("dve", st_["am"], st_["pA"][0:64, 0:64], sU, MUL)

                    def s3():
                        st_["poA"] = PS()
                        mm(st_["poA"][:, 0:64], vtok[0:64, 0, 192 * h:192 * h + 128], st_["am"])
                        st_["poB"] = PS()
                        mm(st_["poB"][p0:p0 + 64, 0:64], vtok[0:64, 0, 192 * h + 128:192 * h + 192], st_["am"])

                    def s4():
                        st_["p2A"] = PS()
                        st_["p2B"] = PS()
                        for b in range(16):
                            mm(st_["p2A"][:, 4 * b:4 * b + 4], Shb[:, b, 0:128], qs[:, h, 4 * b:4 * b + 4])
                        for b in range(16):
                            mm(st_["p2B"][p0:p0 + 64, 4 * b:4 * b + 4], Shb[:, b, 128:192], qs[:, h, 4 * b:4 * b + 4])

                    def s5():
                        oA1 = TM("oA1", [128, 64], F32)
                        oB1 = TM("oB1", [128, 64], F32)
                        cp("act", oA1, st_["p2A"][:, 0:64])
                        cp("act", oB1[p0:p0 + 64, :], st_["p2B"][p0:p0 + 64, 0:64])
                        tt("dve", oA1, st_["poA"][:, 0:64], oA1, ADD)
                        tt("dve", oB1[p0:p0 + 64, :], st_["poB"][p0:p0 + 64, 0:64], oB1[p0:p0 + 64, :], ADD)
                        st_["oA1"] = oA1
                        st_["oB1"] = oB1

                    def s6():
                        sqa = TM("sqa", [128, 64], BF16)
                        sqb = TM("sqb", [128, 64], BF16)
                        act(sqa, st_["oA1"], AF.Square)
                        act(sqb[p0:p0 + 64, :], st_["oB1"][p0:p0 + 64, :], AF.Square)
                        st_["pq"] = PS()
                        mm(st_["pq"][:, 0:64], ones_bf[:, :], sqa, start=True, stop=False)
                        mm(st_["pq"][:, 0:64], ones_bf[p0:p0 + 64, :], sqb[p0:p0 + 64, :], start=False, stop=True)

                    def s7():
                        rs = TM("frs", [128, 64], F32)
                        act_rsqrt(rs, st_["pq"][:, 0:64], epsD[:, 1:2])
                        st_["rs"] = rs

                    def s8():
                        gate_store(h, 0, st_["oA1"], st_["oB1"][p0:p0 + 64, :], st_["rs"], st_["rs"][p0:p0 + 64, :])

                    return [s1, s2, s3, s4, s5, s6, s7, s8]

                def upd_steps(h):
                    Sh, Shb = Shs[h]
                    st_ = {}

                    def u0():
                        Vb = TM("Vb", [64, 16, 192], BF16, 1)
                        vin = vtok[0:64, 0, 192 * h:192 * h + 192].unsqueeze(1).to_broadcast([64, 16, 192])
                        ohb = onehot.unsqueeze(2).to_broadcast([64, 16, 192])
                        tt("dve", Vb, vin, ohb, MUL)
                        st_["Vb"] = Vb
                        st_["Sn"] = TM("Sn", [96, 16, 192], F32, 1)

                    def ub(bp):
                        def f():
                            eblv = ebl[:, h, :].rearrange("p (b t) -> p b t", t=4)[:, :, 3]
                            pd = PS()
                            mm(pd[0:96, 0:384], khat[0:64, 0, 96 * h:96 * h + 96],
                               st_["Vb"][:, 2 * bp:2 * bp + 2, :].rearrange("p b v -> p (b v)"))
                            tmp = TM("stmp", [96, 2, 192], F32)
                            tt("dve", tmp, Sh[:, 2 * bp:2 * bp + 2, :],
                               eblv[:, 2 * bp:2 * bp + 2].unsqueeze(2).to_broadcast([96, 2, 192]), MUL)
                            tt("dve", st_["Sn"][:, 2 * bp:2 * bp + 2, :], tmp,
                               pd[0:96, 0:384].rearrange("p (b v) -> p b v", b=2), ADD)
                        return f

                    def ue():
                        dma("sp", gla_s_o[:, h].rearrange("b d v -> d b v"), st_["Sn"])

                    return [u0] + [ub(bp) for bp in range(8)] + [ue]

                for f in out_stages(0):
                    f()
                for h in range(4):
                    if 1 <= h + 1 < 4 and h + 1 >= 2:
                        load_state(h + 1)
                    ups = upd_steps(h)
                    outs = out_stages(h + 1) if h + 1 < 4 else []
                    k_ = 0
                    for i_, u in enumerate(ups):
                        u()
                        if i_ >= 1 and k_ < len(outs):
                            outs[k_]()
                            k_ += 1
                    while k_ < len(outs):
                        outs[k_]()
                        k_ += 1
            if sample:
                mem_attn_sample(0, mqT, mix, 6)
            else:
                mem_attn_prompt(0, mqT, n, mix, 6)
            for dc in range(8):
                po = PS()
                for s_ in range(8):
                    mm(po[:, 0:n], Wo[:, s_, dc * 128:(dc + 1) * 128], mix[:, s_, :], start=(s_ == 0), stop=(s_ == 7))
                tt("dve", xT[:, dc, c0:c0 + n], po[:, 0:n], xT[:, dc, c0:c0 + n], ADD)

        def kv_phase(ps_, c0, n, sample, mode="all"):
            hT, W = phase_entry("kv", mode, c0, n, 8, [([128, 8, 960], kp(w_kvx))])
            if mode == "prep":
                return
            Wk = W[0]
            need_f32 = sample or (ps_ == 1 and c0 == 512)
            kf = kf_p[:] if need_f32 else None
            for kvh in range(3):
                p1 = PS()
                p2 = PS()
                for k in range(8):
                    mm(p1[:, 0:n], Wk[:, k, 128 * kvh:128 * kvh + 128], hT[:, k, :], start=(k == 0), stop=(k == 7))
                for k in range(8):
                    mm(p2[:, 0:n], Wk[:, k, 384 + 128 * kvh:384 + 128 * kvh + 128], hT[:, k, :], start=(k == 0), stop=(k == 7))
                t1 = TM("t1", [128, 512], F32)
                t2 = TM("t2", [128, 512], F32)
                tt("dve", t1[:, 0:n], p1[:, 0:n], cosT[:, c0:c0 + n], MUL)
                tt("dve", t2[:, 0:n], p2[:, 0:n], sinT[:, c0:c0 + n], MUL)
                tt("pool", kTd[:, kvh, c0:c0 + n], t1[:, 0:n], t2[:, 0:n], ADD)
                if need_f32:
                    if sample:
                        tt("pool", kf[:, kvh, 0:64], t1[0:64, 0:64], t2[0:64, 0:64], ADD)
                    else:
                        tt("pool", kf[:, kvh, :], t1[0:64, n - 128:n], t2[0:64, n - 128:n], ADD)
            T = 64 if sample else 128
            ntile = 1 if sample else n // 128
            for t in range(ntile):
                tc0 = t * 128
                ti = 8 if sample else (c0 // 128 + t)
                pv = PS()
                for k in range(8):
                    mm(pv[0:T, 0:192], hT[:, k, tc0:tc0 + T], Wk[:, k, 768:960], start=(k == 0), stop=(k == 7))
                cp("act", vtokB[0:T, ti, :], pv[0:T, 0:192])
                last = (not sample) and ps_ == 1 and c0 == 512 and t == ntile - 1
                if last or sample:
                    vf = vf_p[:]
                    cp("dve", vf[0:T], pv[0:T, 0:192])
                    if last:
                        dma("sp", swa_v_p_o, vf)
                    else:
                        for b in range(16):
                            dma("sp", swa_v_s_o[b, 124:128, :], vf[4 * b:4 * b + 4, :])
            if (not sample) and ps_ == 1 and c0 == 512:
                dma("sp", swa_kT_p_o.rearrange("h d j -> d h j"), kf)
            if (not sample) and ps_ == 0 and c0 == 512:
                cp("pool", kT_prev[:], kTd[:, :, NPP - 128:NPP])
                cp("pool", v_prev[:], vtokB[:, 7, :])
            if sample:
                for kvh in range(3):
                    dma("sp", swa_kT_s_o[:, kvh, :, 124:128].rearrange("b d t -> d b t"),
                        kf[:, kvh, 0:64].rearrange("p (b t) -> p b t", t=4))

        def mixer_b(ps_, c0, n, sample, mode="all"):
            wspecs = [([128, 8, 1536], kp(b_w_inx[:, 0:1536]))]
            if mode == "all":
                wspecs.append(([128, 8, 256], kp(b_w_inx[:, 1536:1792])))
            hT, W = phase_entry("mb", mode, c0, n, 3, wspecs)
            if mode == "prep":
                return
            Wq = W[0]
            if len(W) < 2:
                w2 = A([128, 8, 256], BF16)
                dma("pool", w2, kp(b_w_inx[:, 1536:1792]))
                W.append(w2)
            Wq2 = W[1]
            if sample:
                kc = A([128, 16, 3, 128], BF16, top=True)
                vcb = A([128, 16, 192], BF16, top=True)
                dma("pool", kc[0:64], kcT.rearrange("b h d j -> d b h j"))
                dma("pool", kc[64:128], kcT.rearrange("b h d j -> d b h j"))
                dma("pool", vcb, vc.rearrange("b j f -> j b f"))
                mem_pre = []
                for hb_ in range(2):
                    ck_ = A([128, 8, 2, 256], BF16, top=True)
                    cv_ = A([128, 8, 2, 256], BF16, top=True)
                    load_mem_cache(1, hb_, ck_, cv_)
                    mem_pre.append((ck_, cv_))
            Wo = A([128, 8, D], BF16)
            dma("pool", Wo, kp(b_w_out))
            qT = A([128, 6, n], BF16)
            for c in range(6):
                p1 = PS()
                p2 = PS()
                for k in range(8):
                    mm(p1[:, 0:n], Wq[:, k, 128 * c:128 * c + 128], hT[:, k, :], start=(k == 0), stop=(k == 7))
                for k in range(8):
                    mm(p2[:, 0:n], Wq[:, k, 768 + 128 * c:768 + 128 * c + 128], hT[:, k, :], start=(k == 0), stop=(k == 7))
                t1 = TM("t1", [128, 512], F32)
                t2 = TM("t2", [128, 512], F32)
                tt("dve", t1[:, 0:n], p1[:, 0:n], cosT[:, c0:c0 + n], MUL)
                tt("dve", t2[:, 0:n], p2[:, 0:n], sinT[:, c0:c0 + n], MUL)
                tt("pool", qT[:, c, :], t1[:, 0:n], t2[:, 0:n], ADD)
            mqT = A([128, 2, n], BF16)
            for c in range(2):
                pp = PS()
                for k in range(8):
                    mm(pp[:, 0:n], Wq2[:, k, c * 128:(c + 1) * 128], hT[:, k, :], start=(k == 0), stop=(k == 7))
                cp("act", mqT[:, c, :], pp[:, 0:n])
            mix = A([128, 8, n], BF16)
            KSUB = int(DBG.get("KSUB", 99))
            if KSUB <= 1:
                return
            if not sample:
                def BK(b_):
                    return psum[:, b_ * 512:(b_ + 1) * 512]
                units = [(t, kvh) for t in range(n // 128) for kvh in range(3)]
                pTs = {}

                def info(ui):
                    t, kvh = units[ui]
                    lt = c0 // 128 + t
                    gt = ps_ * 8 + lt
                    return t, kvh, lt, gt, ([1] if gt == 0 else [0, 1])

                def U1(ui):
                    t, kvh, lt, gt, ws = info(ui)
                    tc0 = t * 128
                    worder = [1, 0] if len(ws) == 2 else [1]
                    for wi, which in enumerate(worder):
                        for ge in range(2):
                            for par in range(2):
                                hd = 4 * kvh + 2 * ge + par
                                c = hd // 2
                                p0 = par * 64
                                if which == 1:
                                    kk = kTd[p0:p0 + 64, kvh, c0 + tc0:c0 + tc0 + 128]
                                elif lt == 0:
                                    kk = kT_prev[p0:p0 + 64, kvh, :]
                                else:
                                    kk = kTd[p0:p0 + 64, kvh, c0 + tc0 - 128:c0 + tc0]
                                col = (wi * 2 + ge) * 128
                                mm(BK(2 * (ui % 2) + par)[:, col:col + 128], kk, qT[p0:p0 + 64, c, tc0:tc0 + 128])

                def U2(ui):
                    t, kvh, lt, gt, ws = info(ui)
                    nw = len(ws)
                    pT = TM("pT", [128, 2, 512], BF16)
                    pTs[ui] = pT
                    for par in range(2):
                        ex = TM("ex", [128, 512], F32, 4)
                        act(ex[:, 0:256 * nw], BK(2 * (ui % 2) + par)[:, 0:256 * nw], AF.Exp, scale=0.125)
                        for wi in range(nw):
                            msk = cst[:, C_TRIU + 128 * wi:C_TRIU + 128 * wi + 128].unsqueeze(1).to_broadcast([128, 2, 128])
                            outv = pT[:, wi, :].rearrange("p (ge pa i) -> p ge pa i", ge=2, pa=2)[:, :, par, :]
                            inv = ex[:, 256 * wi:256 * wi + 256].rearrange("p (ge i) -> p ge i", ge=2)
                            tt("pool", outv, inv, msk, MUL)

                def U3(ui):
                    t, kvh, lt, gt, ws = info(ui)
                    pT = pTs[ui]
                    po = BK(4 + 2 * (ui % 2))
                    pr = BK(5 + 2 * (ui % 2))
                    for which in ws:
                        for par in range(2):
                            if which == 1:
                                vv = vtokB[:, lt, 64 * kvh:64 * kvh + 64]
                            elif lt == 0:
                                vv = v_prev[:, 64 * kvh:64 * kvh + 64]
                            else:
                                vv = vtokB[:, lt - 1, 64 * kvh:64 * kvh + 64]
                            rhs = pT[:, 1 - which, :].rearrange("p (ge pa i) -> p ge pa i", ge=2, pa=2)[:, :, par, :]
                            mm(po[par * 64:(par + 1) * 64, 0:256], vv, rhs, start=(which == ws[0]), stop=(which == 1))
                    for which in ws:
                        mm(pr[:, 0:512], ones_bf[:, :], pT[:, 1 - which, :], start=(which == ws[0]), stop=(which == 1))

                def U4(ui):
                    t, kvh, lt, gt, ws = info(ui)
                    tc0 = t * 128
                    po = BK(4 + 2 * (ui % 2))
                    pr = BK(5 + 2 * (ui % 2))
                    den = TM("den", [128, 2, 128], F32)
                    for par in range(2):
                        p0 = par * 64
                        prv = pr[p0:p0 + 64, 0:512].rearrange("p (ge pa i) -> p ge pa i", ge=2, pa=2)[:, :, par, :]
                        esv = esink[p0:p0 + 64, 4 * kvh:4 * kvh + 4].rearrange("p (ge pa) -> p ge pa", pa=2)[:, :, par]
                        tt("dve", den[p0:p0 + 64], prv, esv.unsqueeze(2).to_broadcast([64, 2, 128]), ADD)
                    act_recip(den, den)
                    for par in range(2):
                        p0 = par * 64
                        tt("dve", mix[p0:p0 + 64, 2 * kvh:2 * kvh + 2, tc0:tc0 + 128],
                           po[p0:p0 + 64, 0:256].rearrange("p (ge i) -> p ge i", ge=2), den[p0:p0 + 64], MUL)

                U1(0)
                U2(0)
                for ui in range(len(units)):
                    if ui + 1 < len(units):
                        U1(ui + 1)
                    U3(ui)
                    if ui + 1 < len(units):
                        U2(ui + 1)
                    U4(ui)
                if KSUB <= 3:
                    return
                mem_attn_prompt(1, mqT, n, mix, 6)
                if KSUB <= 4:
                    return
            else:
                sC = PS(2)
                sN = PS(2)
                for hd in (0, 2, 4, 6, 8, 10, 1, 3, 5, 7, 9, 11):
                    kvh = hd // 4
                    c = hd // 2
                    p0 = (hd % 2) * 64
                    for b in range(16):
                        col = hd * 64 + b * 4
                        mm(sC[:, col:col + 4], kc[p0:p0 + 64, b, kvh, :], qT[p0:p0 + 64, c, 4 * b:4 * b + 4])
                    mm(sN[0:64, hd * 64:(hd + 1) * 64], kTd[p0:p0 + 64, kvh, NPP:NPP + 64], qT[p0:p0 + 64, c, 0:64])
                exC = A([128, 768], F32)
                exN = A([64, 768], F32)
                act(exC, sC[:, 0:768], AF.Exp, scale=0.125)
                act(exN, sN[0:64, 0:768], AF.Exp, scale=0.125)
                eC = A([128, 768], BF16)
                eN = A([64, 768], BF16)
                tt("dve", eC.rearrange("p (a t) -> p a t", t=4), exC.rearrange("p (a t) -> p a t", t=4),
                   maskC.unsqueeze(1).to_broadcast([128, 192, 4]), MUL)
                tt("dve", eN.rearrange("p (h f) -> p h f", h=12), exN.rearrange("p (h f) -> p h f", h=12),
                   maskN.unsqueeze(1).to_broadcast([64, 12, 64]), MUL)
                po = PS(2)
                pr = PS(2)
                po2 = sC
                for hd in range(12):
                    kvh = hd // 4
                    p0 = (hd % 2) * 64
                    for b in range(16):
                        col = hd * 64 + b * 4
                        mm(po[p0:p0 + 64, col:col + 4], vcb[:, b, 64 * kvh:64 * kvh + 64], eC[:, col:col + 4])
                    mm(po2[p0:p0 + 64, hd * 64:(hd + 1) * 64], vtokB[0:64, 8, 64 * kvh:64 * kvh + 64],
                       eN[:, hd * 64:(hd + 1) * 64])
                o2s = A([128, 768], F32)
                cp("act", o2s, po2[:, 0:768])
                for (a0, a1) in ((0, 512), (512, 768)):
                    mm(pr[:, a0:a1], ones_bf[:, :], eC[:, a0:a1], start=True, stop=False)
                    mm(pr[:, a0:a1], ones_bf[0:64, :], eN[:, a0:a1], start=False, stop=True)
                den = TM("dens", [128, 12, 64], F32, 1)
                tt("dve", den, pr[:, 0:768].rearrange("p (h f) -> p h f", h=12),
                   esink[:, 0:12].unsqueeze(2).to_broadcast([128, 12, 64]), ADD)
                act_recip(den, den)
                for par in range(2):
                    p0 = par * 64
                    pov = po[p0:p0 + 64, 0:768].rearrange("p (c pa f) -> p c pa f", c=6, pa=2)[:, :, par, :]
                    o2v = o2s[p0:p0 + 64, :].rearrange("p (c pa f) -> p c pa f", c=6, pa=2)[:, :, par, :]
                    dnv = den[p0:p0 + 64].rearrange("p (c pa) f -> p c pa f", pa=2)[:, :, par, :]
                    osum = TM("osum", [128, 6, 64], F32, 1)
                    tt("dve", osum[p0:p0 + 64], pov, o2v, ADD)
                    tt("dve", mix[p0:p0 + 64, 0:6, :], osum[p0:p0 + 64], dnv, MUL)
                mem_attn_sample(1, mqT, mix, 6, pre=mem_pre)
            for dc in range(8):
                po = PS()
                for s_ in range(8):
                    mm(po[:, 0:n], Wo[:, s_, dc * 128:(dc + 1) * 128], mix[:, s_, :], start=(s_ == 0), stop=(s_ == 7))
                tt("dve", xT[:, dc, c0:c0 + n], po[:, 0:n], xT[:, dc, c0:c0 + n], ADD)

        def final(ps_, c0, n, sample):
            arena_reset()
            sq = A([128, 8, n], BF16)
            yst = A([128, 8, n], F32)
            norm(xT[:, :, c0:c0 + n], n, 9, yst, sq)
            oc = 2048 if sample else ps_ * NPP + c0
            dma("sp", kp(yT_o[:, oc:oc + n]), yst)

        KST = int(DBG.get("KSTAGE", 99))
        KPASS = int(DBG.get("KPASS", 2))
        for ps_ in range(min(2, KPASS)):
            tgs = [(0, 512, False), (512, 512, False)] + ([(NPP, NSM, True)] if ps_ == 1 else [])
            ntp = NPP + (NSM if ps_ == 1 else 0)
            tg2 = [(c0, n) for (c0, n, _) in tgs]
            HK = not DBG.get("KNOHOOK")
            HK2 = HK and KST >= 7 and KPASS >= 2
            if ps_ == 1:
                load_pass_inputs(1, part=(1 if HK2 else None))
            if KST >= 1:
                ffn(0, 0, tg2, ntp, hook=(lambda: mixer_a(0, 512, False, mode="prep")) if (HK and KST >= 2) else None)
            if KST >= 2:
                for (c0, n, smp) in tgs:
                    if smp and KST == 2 and DBG.get("KNOSAMPLE"):
                        continue
                    mixer_a(c0, n, smp, mode=("body" if (HK and c0 == 0) else "all"),
                            cache_copy=(ps_ == 1 and c0 == 0))
                if ps_ == 1:
                    dma("sp", gla_p_o.rearrange("h d v -> d h v"), S[:])
            if KST >= 3:
                ffn(0, 1, tg2, ntp, hook=(lambda: kv_phase(ps_, 0, 512, False, mode="prep")) if (HK and KST >= 4) else None)
            if KST >= 4:
                for (c0, n, smp) in tgs:
                    kv_phase(ps_, c0, n, smp, mode=("body" if (HK and c0 == 0) else "all"))
            if KST >= 5:
                ffn(1, 0, tg2, ntp, hook=(lambda: mixer_b(ps_, 0, 512, False, mode="prep")) if (HK and KST >= 6) else None)
            if KST >= 6:
                for (c0, n, smp) in tgs:
                    mixer_b(ps_, c0, n, smp, mode=("body" if (HK and c0 == 0) else "all"))
            def fin_hook(ps_=ps_):
                final(ps_, 0, 512, False)
                if ps_ == 0:
                    load_pass_inputs(1, part=0)
            if KST >= 7:
                ffn(1, 1, tg2, ntp, hook=fin_hook if HK2 else None)
            for (c0, n, smp) in tgs:
                if HK2 and c0 == 0:
                    continue
                final(ps_, c0, n, smp)

        P.emit(st)
    return nc


def prep(x_prompt, x_sample, state_gla, cache_swa_k, cache_swa_v, cache_mem_k, cache_mem_v,
         mem_prompt, ffn1_norm, ffn1_w_gu, ffn1_w_down, mix_norm, ffn2_norm, ffn2_w_gu,
         ffn2_w_down, mem_norm, mem_w_kv, a_w_in, a_w_gate, a_b_gate, a_out_norm, a_w_out,
         kv_norm, w_kv, b_w_in, b_sinks, b_w_out, final_norm):
    f = lambda a: np.ascontiguousarray(np.asarray(a, dtype=np.float32))
    x_prompt, x_sample, state_gla = f(x_prompt), f(x_sample), f(state_gla)
    cache_swa_k, cache_swa_v = f(cache_swa_k), f(cache_swa_v)
    cache_mem_k, cache_mem_v, mem_prompt = f(cache_mem_k), f(cache_mem_v), f(mem_prompt)
    n = 8
    gains = np.stack([f(ffn1_norm)[0], f(ffn1_norm)[1], f(mix_norm)[0], f(mix_norm)[1],
                      f(ffn2_norm)[0], f(ffn2_norm)[1], f(mem_norm)[0], f(mem_norm)[1],
                      f(kv_norm), f(final_norm)], 0)
    gains_p = f(gains.reshape(10, 8, 128).transpose(2, 0, 1))
    go = f(a_out_norm).reshape(192)
    gout = np.zeros((128, 2), np.float32)
    gout[:, 0] = go[0:128]
    gout[0:64, 1] = go[128:192]
    gout[64:128, 1] = go[128:192]
    sinks = f(np.broadcast_to(f(b_sinks).reshape(1, 12), (128, 12)))
    wkv = f(w_kv)
    wk = wkv[:, 0:192]
    wkp = wk[:, rope_perm_cols(192)]
    kd = np.concatenate([np.concatenate([wk[:, 64 * h:64 * h + 64]] * 2, 1) for h in range(3)], 1)
    kdp = np.concatenate([np.concatenate([wkp[:, 64 * h:64 * h + 64]] * 2, 1) for h in range(3)], 1)
    w_kvx = f(np.concatenate([kd, kdp, wkv[:, 192:384]], 1))
    bwi = f(b_w_in)[0]
    wq = bwi[:, 0:768]
    b_w_inx = f(np.concatenate([wq, wq[:, rope_perm_cols(768)], bwi[:, 768:1024]], 1))
    awi = f(a_w_in)[0]
    rcols = [1536 + 192 * h + j for h in range(4) for j in range(128)] + \
            [1536 + 192 * h + 128 + j for h in range(4) for j in range(64)]
    awi = np.concatenate([awi[:, 0:1536], awi[:, rcols], awi[:, 2304:2576]], 1)
    shared = dict(
        w_gu1=f(ffn1_w_gu), w_d1=f(ffn1_w_down), w_gu2=f(ffn2_w_gu), w_d2=f(ffn2_w_down),
        w_memkv=f(mem_w_kv), a_w_in=f(awi), a_w_gate=f(a_w_gate)[0], a_b_gate=f(a_b_gate).reshape(1, 384),
        a_w_out=f(a_w_out)[0], w_kvx=w_kvx, b_w_inx=b_w_inx, b_w_out=f(b_w_out)[0],
        gains=gains_p, gout=gout, sinks=sinks, consts=make_consts(), rope=make_rope(),
    )
    in_maps = []
    for c in range(n):
        bs = slice(16 * c, 16 * c + 16)
        m = dict(shared)
        m["xT_p"] = f(x_prompt[c].T)
        m["xT_s"] = f(x_sample[bs].reshape(64, D).T)
        m["memT"] = f(mem_prompt[c].T)
        m["state"] = f(state_gla[0, bs])
        m["kcT"] = f(cache_swa_k[bs].transpose(0, 2, 3, 1))
        m["vc"] = f(cache_swa_v[bs].reshape(16, 128, 192))
        m["cmkT"] = f(cache_mem_k[:, bs].transpose(0, 1, 3, 4, 2))
        m["cmv"] = f(cache_mem_v[:, bs].reshape(2, 16, 256, 256))
        in_maps.append(m)
    return in_maps


def assemble(R):
    n = 8
    y_prompt = np.stack([R[c]["yT"][:, 0:2048].T for c in range(n)], 0)
    y_sample = np.concatenate([R[c]["yT"][:, 2048:2112].T.reshape(16, 4, D) for c in range(n)], 0)
    gla_prompt = np.stack([R[c]["gla_p"] for c in range(n)], 0)[None]
    gla_sample = np.concatenate([R[c]["gla_s"] for c in range(n)], 0)[None]
    swa_k_prompt = np.stack([R[c]["swa_kT_p"].transpose(2, 0, 1) for c in range(n)], 0)
    swa_v_prompt = np.stack([R[c]["swa_v_p"].reshape(128, 3, 64) for c in range(n)], 0)
    swa_k_sample = np.concatenate([R[c]["swa_kT_s"].transpose(0, 3, 1, 2) for c in range(n)], 0)
    swa_v_sample = np.concatenate([R[c]["swa_v_s"].reshape(16, 128, 3, 64) for c in range(n)], 0)
    mem_k_prompt = np.stack([R[c]["mem_k_p"].reshape(2, 256, 4, 64) for c in range(n)], 1)
    mem_v_prompt = np.stack([R[c]["mem_v_p"].reshape(2, 256, 4, 64) for c in range(n)], 1)
    outs = (y_prompt, y_sample, gla_prompt, gla_sample, swa_k_prompt, swa_v_prompt,
            swa_k_sample, swa_v_sample, mem_k_prompt, mem_v_prompt)
    return tuple(np.ascontiguousarray(o, dtype=np.float32) for o in outs)


def kernel(**inputs):
    in_maps = prep(**inputs)
    nc = build_nc()
    res = run_bass_kernel_spmd(nc, in_maps, core_ids=list(range(8)))
    return assemble(res.results)
```

```python
import math
from bisect import bisect_left
from contextlib import ExitStack

import numpy as np
import concourse.bass as bass
import concourse.mybir as mybir
from concourse.bass_utils import run_bass_kernel_spmd

F32 = mybir.dt.float32
BF16 = mybir.dt.bfloat16
ALU = mybir.AluOpType
AF = mybir.ActivationFunctionType

D = 1024
FF = 2816
NCH = 22
EPS = 1e-6
NPP = 1024
NSM = 64
DBG = {}


class Op:
    __slots__ = ("idx", "eng", "fn", "deps", "fdeps", "dma", "signal", "sigval", "sem", "semval")


class Ent:
    __slots__ = ("w", "r")

    def __init__(self):
        self.w = None
        self.r = {}


class Space:
    def __init__(self):
        self.b = [0, 1 << 40]
        self.e = [Ent()]

    def split(self, x):
        i = bisect_left(self.b, x)
        if self.b[i] == x:
            return i
        old = self.e[i - 1]
        new = Ent()
        new.w = old.w
        new.r = dict(old.r)
        self.b.insert(i, x)
        self.e.insert(i, new)
        return i

    def segs(self, a, b):
        i = self.split(a)
        j = self.split(b)
        return self.e[i:j]


def ap_ranges(ap):
    dims = ap.ap
    esz = mybir.dt.size(ap.dtype)
    pstride = dims[0][0]
    off = int(ap.offset)
    if pstride:
        off = off % pstride
    free = [(s, n) for (s, n) in dims[1:] if n > 1 and s != 0]
    free.sort(key=lambda x: -x[0])
    run = 1
    while free and free[-1][0] == run:
        run *= free[-1][1]
        free.pop()
    starts = [off]
    for s, n in free:
        starts = [st + i * s for st in starts for i in range(n)]
        if len(starts) > 64:
            break
    if len(starts) > 64:
        hi = off + sum(s * (n - 1) for (s, n) in dims[1:] if n > 1 and s > 0) + 1
        return [(off * esz, hi * esz)]
    return [(st * esz, (st + run) * esz) for st in starts]


class Prog:
    ENGS = ("pe", "act", "dve", "pool", "sp")
    NS = 8

    def __init__(self, nc):
        self.nc = nc
        self.ops = []
        self.spaces = {}
        self.onchip = set()
        self.pe_recent = []

    def add(self, eng, fn, reads=(), writes=(), dma=False, pe_rows=None):
        o = Op()
        o.idx = len(self.ops)
        o.eng = eng
        o.fn = fn
        o.dma = dma
        o.signal = False
        o.sigval = 0
        o.sem = None
        o.semval = 0
        o.fdeps = []
        deps = {}
        rkey = ("dma", o.idx) if dma else eng
        acc = []
        for ap in reads:
            if ap is None or ap.name not in self.onchip:
                continue
            acc.append((ap, ap.name == "psum"))
        for ap in writes:
            if ap is None or ap.name not in self.onchip:
                continue
            acc.append((ap, True))
        banks = set()
        for ap, wr in acc:
            sp = self.spaces.setdefault(ap.name, Space())
            rngs = ap_ranges(ap)
            if ap.name == "psum":
                lo = min(r[0] for r in rngs) // 2048
                hi = (max(r[1] for r in rngs) + 2047) // 2048
                rngs = [(lo * 2048, hi * 2048)]
                if wr and eng == "pe":
                    banks.update(range(lo, hi))
            for (a_, b_) in rngs:
                for e in sp.segs(a_, b_):
                    if e.w is not None:
                        deps[e.w.idx] = e.w
                    if wr:
                        for r in e.r.values():
                            deps[r.idx] = r
                        e.w = o
                        e.r = {}
                    else:
                        e.r[rkey] = o
        deps.pop(o.idx, None)
        red = {}
        for d in deps.values():
            if d.dma:
                red[("dma", d.idx)] = d
            else:
                cur = red.get(d.eng)
                if cur is None or cur.idx < d.idx:
                    red[d.eng] = d
        o.deps = list(red.values())
        if eng == "pe" and pe_rows is not None:
            for (p, prow, pbanks) in reversed(self.pe_recent):
                if not (prow[1] <= pe_rows[0] or pe_rows[1] <= prow[0]):
                    break
                if pbanks & banks:
                    o.fdeps.append(p)
                    break
            self.pe_recent.append((o, pe_rows, banks))
            if len(self.pe_recent) > 32:
                self.pe_recent.pop(0)
        self.ops.append(o)
        return o

    def emit(self, stack):
        nc = self.nc
        ndma = {e: 0 for e in self.ENGS}
        for o in self.ops:
            for d in o.deps:
                if not d.dma:
                    if d.eng == "pe" and o.eng == "pe" and not o.dma:
                        continue
                    d.signal = True
            for d in o.fdeps:
                d.signal = True
        cnt = {e: 0 for e in self.ENGS}
        for o in self.ops:
            if o.dma:
                n = ndma[o.eng]
                ndma[o.eng] = n + 1
                o.sem = (o.eng, n % self.NS)
                o.semval = 16 * (n // self.NS + 1)
            elif o.signal:
                cnt[o.eng] += 1
                o.sigval = cnt[o.eng]
        if DBG.get("KDEBUG"):
            print("SEMCOUNTS", cnt, {e: 16 * (v // self.NS + 1) for e, v in ndma.items()}, flush=True)
        sems = {}
        for e in self.ENGS:
            sems[e] = stack.enter_context(nc.semaphore("s_" + e))
        dsems = {}
        for e in self.ENGS:
            for i in range(min(self.NS, ndma[e])):
                dsems[(e, i)] = stack.enter_context(nc.semaphore("d_%s_%d" % (e, i)))
        block = stack.enter_context(nc.Block())
        by_eng = {e: [o for o in self.ops if o.eng == e] for e in self.ENGS}

        def run(engname, eng):
            waited = {}
            for o in by_eng[engname]:
                for d in o.deps:
                    if d.dma:
                        k = d.sem
                        v = d.semval
                        s = dsems[k]
                    else:
                        if d.eng == "pe" and engname == "pe" and not o.dma:
                            continue
                        k = d.eng
                        v = d.sigval
                        s = sems[k]
                    if waited.get(k, 0) < v:
                        eng.wait_ge(s, v)
                        waited[k] = v
                for d in o.fdeps:
                    if waited.get(d.eng, 0) < d.sigval:
                        eng.wait_ge(sems[d.eng], d.sigval)
                        waited[d.eng] = d.sigval
                if o.dma:
                    if o.semval > 16 and waited.get(o.sem, 0) < o.semval - 16:
                        eng.wait_ge(dsems[o.sem], o.semval - 16)
                        waited[o.sem] = o.semval - 16
                    o.fn(eng).then_inc(dsems[o.sem], 16)
                else:
                    ins = o.fn(eng)
                    if o.signal:
                        ins.then_inc(sems[engname], 1)
            last = {}
            for o in by_eng[engname]:
                if o.dma:
                    last[o.sem] = max(last.get(o.sem, 0), o.semval)
            for k, v in last.items():
                if waited.get(k, 0) < v:
                    eng.wait_ge(dsems[k], v)

        @block.tensor
        def _(e):
            run("pe", e)

        @block.scalar
        def _(e):
            run("act", e)

        @block.vector
        def _(e):
            run("dve", e)

        @block.gpsimd
        def _(e):
            run("pool", e)

        @block.sync
        def _(e):
            run("sp", e)


C_TRIU = 0
C_TRIG = 128
C_SU = 256
C_SG = 320
C_OH = 384
C_MC = 400
C_MN = 404
C_PM = 468
C_W = 596


def make_consts():
    c = np.zeros((128, C_W), np.float32)
    p = np.arange(128)[:, None]
    i = np.arange(128)[None, :]
    c[:, C_TRIU:C_TRIU + 128] = (p <= i)
    c[:, C_TRIG:C_TRIG + 128] = (p > i)
    p6 = np.arange(64)[:, None]
    i6 = np.arange(64)[None, :]
    same = (p6 // 4) == (i6 // 4)
    c[:64, C_SU:C_SU + 64] = same & (p6 <= i6)
    c[:64, C_SG:C_SG + 64] = same & (p6 > i6)
    c[:64, C_OH:C_OH + 16] = (p6 // 4) == np.arange(16)[None, :]
    c[:, C_MC:C_MC + 4] = (p >= (np.arange(4)[None, :] + 1))
    mn = np.zeros((64, 16, 4), np.float32)
    for pp in range(64):
        for t in range(4):
            if pp % 4 <= t:
                mn[pp, pp // 4, t] = 1.0
    c[:64, C_MN:C_MN + 64] = mn.reshape(64, 64)
    part = rope_perm_cols(128)
    for d in range(128):
        c[part[d], C_PM + d] = 1.0
    return c


def make_rope():
    rot = 16
    half = 8
    inv_freq = np.exp(-math.log(500000.0) * np.arange(0, rot, 2, dtype=np.float32) / rot).astype(np.float32)
    out = np.zeros((2, 2, 128, NPP + NSM), np.float32)
    for ps in range(2):
        pos = np.concatenate([np.arange(ps * NPP, (ps + 1) * NPP, dtype=np.float32),
                              np.tile(8192.0 + np.arange(4, dtype=np.float32), 16)]).astype(np.float32)
        ang = (pos[:, None] * inv_freq[None, :]).astype(np.float32)
        cs = np.cos(ang).astype(np.float32).T
        sn = np.sin(ang).astype(np.float32).T
        cosr = np.ones((64, pos.shape[0]), np.float32)
        sinr = np.zeros((64, pos.shape[0]), np.float32)
        cosr[0:half] = cs
        cosr[half:rot] = cs
        sinr[0:half] = -sn
        sinr[half:rot] = sn
        out[ps, 0] = np.concatenate([cosr, cosr], 0)
        out[ps, 1] = np.concatenate([sinr, sinr], 0)
    return out


def rope_perm_cols(ncols):
    idx = np.arange(ncols)
    d = idx % 64
    partner = np.where(d < 8, idx + 8, np.where(d < 16, idx - 8, idx))
    return partner


def build_nc():
    nc = bass.Bass("TRN2", target_bir_lowering=False)
    P = Prog(nc)

    def din(name, shape):
        return nc.dram_tensor(name, list(shape), F32, kind="ExternalInput").ap()

    def dout(name, shape):
        return nc.dram_tensor(name, list(shape), F32, kind="ExternalOutput").ap()

    xT_p = din("xT_p", [D, 2048])
    xT_s = din("xT_s", [D, NSM])
    memT = din("memT", [D, 256])
    state = din("state", [16, 4, 96, 192])
    kcT = din("kcT", [16, 3, 64, 128])
    vc = din("vc", [16, 128, 192])
    cmkT = din("cmkT", [2, 16, 4, 64, 256])
    cmv = din("cmv", [2, 16, 256, 256])
    w_gu = [din("w_gu1", [2, D, 2 * FF]), din("w_gu2", [2, D, 2 * FF])]
    w_dn = [din("w_d1", [2, FF, D]), din("w_d2", [2, FF, D])]
    w_memkv = din("w_memkv", [2, D, 512])
    a_w_in = din("a_w_in", [D, 2576])
    a_w_gate = din("a_w_gate", [16, 384])
    a_b_gate = din("a_b_gate", [1, 384])
    a_w_out = din("a_w_out", [D, D])
    w_kvx = din("w_kvx", [D, 960])
    b_w_inx = din("b_w_inx", [D, 1792])
    b_w_out = din("b_w_out", [D, D])
    gains_d = din("gains", [128, 10, 8])
    gout_d = din("gout", [128, 2])
    sinks_d = din("sinks", [128, 12])
    consts_d = din("consts", [128, C_W])
    rope_d = din("rope", [2, 2, 128, NPP + NSM])

    yT_o = dout("yT", [D, 2048 + NSM])
    gla_p_o = dout("gla_p", [4, 96, 192])
    gla_s_o = dout("gla_s", [16, 4, 96, 192])
    swa_kT_p_o = dout("swa_kT_p", [3, 64, 128])
    swa_v_p_o = dout("swa_v_p", [128, 192])
    swa_kT_s_o = dout("swa_kT_s", [16, 3, 64, 128])
    swa_v_s_o = dout("swa_v_s", [16, 128, 192])
    mem_k_o = dout("mem_k_p", [2, 256, 256])
    mem_v_o = dout("mem_v_p", [2, 256, 256])

    with ExitStack() as st, nc.allow_low_precision("bf16 matmul operands"), \
            nc.allow_non_contiguous_dma("strided weight/cache tiles"):

        def sbt(name, shape, dt):
            t = st.enter_context(nc.sbuf_tensor("sb_" + name, list(shape), dt))
            P.onchip.add("sb_" + name)
            return t

        NT = NPP + NSM
        xT = sbt("xT", [128, 8, NT], F32)
        gsc = sbt("gsc", [128, 10, 8], F32)
        gout = sbt("gout", [128, 2], F32)
        esink = sbt("esink", [128, 12], F32)
        cst = sbt("cst", [128, C_W], F32)
        ones_bf = sbt("ones_bf", [128, 128], BF16)
        ones_f = sbt("ones_f", [1, 128], F32)
        epsD = sbt("epsD", [128, 2], F32)
        wgate = sbt("wgate", [16, 384], F32)
        bgate = sbt("bgate", [1, 384], F32)
        cosT = sbt("cosT", [128, NT], F32)
        sinT = sbt("sinT", [128, NT], F32)
        mkT = sbt("mkT", [128, 2, 2, 256], BF16)
        mv = sbt("mv", [128, 2, 2, 256], BF16)
        S = sbt("S", [96, 4, 192], F32)
        S_bf = sbt("S_bf", [96, 4, 192], BF16)
        kTd = sbt("kTd", [128, 3, NT], BF16)
        vtokB = sbt("vtokB", [128, 9, 192], BF16)
        kT_prev = sbt("kT_prev", [128, 3, 128], BF16)
        v_prev = sbt("v_prev", [128, 192], BF16)
        perm_bf = sbt("perm_bf", [128, 128], BF16)
        kf_p = sbt("kf_p", [64, 3, 128], F32)
        vf_p = sbt("vf_p", [128, 192], F32)
        ARENA_BYTES = (nc.sbuf_bytes_remaining - 2048) // 64 * 64
        arena = sbt("arena", [128, ARENA_BYTES // 4], F32)
        psum = st.enter_context(nc.psum_tensor("psum", [128, 4096], F32))
        P.onchip.add("psum")

        state_ = {"aoff": 0, "bank": 0}

        tmpc = {}

        def arena_reset():
            state_["aoff"] = 0
            state_["atop"] = ARENA_BYTES
            tmpc.clear()

        def amark():
            return state_["aoff"]

        def apop(m):
            state_["aoff"] = m
            for k in [k for k, v in tmpc.items() if v[2] >= m]:
                del tmpc[k]

        def TM(key, shape, dt, nbuf=2):
            ent = tmpc.get(key)
            if ent is None:
                off = state_["aoff"]
                ent = tmpc[key] = [[A(shape, dt) for _ in range(nbuf)], 0, off]
            ent[1] += 1
            return ent[0][(ent[1] - 1) % nbuf]

        def A(shape, dt, top=False, at=None):
            esz = mybir.dt.size(dt)
            nfree = 1
            for s in shape[1:]:
                nfree *= s
            nbytes = (nfree * esz + 63) // 64 * 64
            if at is not None:
                off = at
            elif top:
                state_["atop"] -= nbytes
                off = state_["atop"]
            else:
                off = state_["aoff"]
                state_["aoff"] = off + nbytes
            assert state_["aoff"] <= state_["atop"], ("arena overflow", state_["aoff"], state_["atop"])
            state_["hw"] = max(state_.get("hw", 0), state_["aoff"])
            ap = arena[:, off // 4:(off + nbytes) // 4]
            if dt != F32:
                ap = ap.bitcast(dt)
            ap = ap[:, 0:nfree]
            if len(shape) == 3:
                ap = ap.rearrange("p (a b) -> p a b", a=shape[1])
            elif len(shape) == 4:
                ap = ap.rearrange("p (a b c) -> p a b c", a=shape[1], b=shape[2])
            if shape[0] < 128:
                ap = ap[0:shape[0]]
            return ap

        def PS(nb=1):
            b = state_["bank"]
            if nb == 2 and b % 2 == 1:
                b += 1
            if b + nb > 8:
                b = 0
            state_["bank"] = (b + nb) % 8
            return psum[:, b * 512:(b + nb) * 512]

        def mm(out, lhsT, rhs, start=True, stop=True):
            r0 = lhsT.base_partition()
            P.add("pe", lambda e: e.matmul(out, lhsT=lhsT, rhs=rhs, start=start, stop=stop),
                  reads=[lhsT, rhs], writes=[out], pe_rows=(r0, r0 + lhsT.shape[0]))

        def act(out, in_, func, bias=None, scale=1.0):
            if bias is None:
                P.add("act", lambda e: e.activation(out=out, in_=in_, func=func, scale=scale),
                      reads=[in_], writes=[out])
            elif isinstance(bias, float):
                P.add("act", lambda e: e.activation(out=out, in_=in_, func=func, bias=bias, scale=scale),
                      reads=[in_], writes=[out])
            else:
                P.add("act", lambda e: e.activation(out=out, in_=in_, func=func, bias=bias, scale=scale),
                      reads=[in_, bias], writes=[out])

        def tt(eng, out, in0, in1, op):
            P.add(eng, lambda e: e.tensor_tensor(out=out, in0=in0, in1=in1, op=op), reads=[in0, in1], writes=[out])

        def ts(eng, out, in0, s1, op0, s2=None, op1=None):
            rd = [in0] + [s for s in (s1, s2) if s is not None and not isinstance(s, (int, float))]
            if op1 is None:
                P.add(eng, lambda e: e.tensor_scalar(out=out, in0=in0, scalar1=s1, scalar2=None, op0=op0),
                      reads=rd, writes=[out])
            else:
                P.add(eng, lambda e: e.tensor_scalar(out=out, in0=in0, scalar1=s1, scalar2=s2, op0=op0, op1=op1),
                      reads=rd, writes=[out])

        def stt(eng, out, in0, scalar, in1, op0, op1):
            rd = [in0, in1] + ([] if isinstance(scalar, (int, float)) else [scalar])
            P.add(eng, lambda e: e.scalar_tensor_tensor(out=out, in0=in0, scalar=scalar, in1=in1, op0=op0, op1=op1),
                  reads=rd, writes=[out])

        def cp(eng, out, in_):
            if eng == "act":
                P.add("act", lambda e: e.copy(out=out, in_=in_), reads=[in_], writes=[out])
            else:
                P.add(eng, lambda e: e.tensor_copy(out=out, in_=in_), reads=[in_], writes=[out])

        def recip(out, in_):
            P.add("dve", lambda e: e.reciprocal(out=out, in_=in_), reads=[in_], writes=[out])

        def act_recip(out, in_):
            act(out, in_, AF.Ln)
            act(out, out, AF.Exp, scale=-1.0)

        def act_rsqrt(out, in_, eps_ap):
            act(out, in_, AF.Ln, bias=eps_ap)
            act(out, out, AF.Exp, scale=-0.5)

        def memset(eng, out, val):
            P.add(eng, lambda e: e.memset(out, val), writes=[out])

        def dma(eng, out, in_):
            P.add(eng, lambda e: e.dma_start(out=out, in_=in_), reads=[in_], writes=[out], dma=True)

        def kp(ap):
            return ap.rearrange("(k p) n -> p k n", p=128)

        MUL = ALU.mult
        ADD = ALU.add

        dma("sp", gsc[:], gains_d)
        dma("sp", gout[:], gout_d)
        dma("sp", esink[:], sinks_d)
        dma("sp", cst[:], consts_d)
        dma("sp", wgate[:], a_w_gate)
        dma("sp", bgate[:], a_b_gate)
        memset("dve", ones_bf[:], 1.0)
        memset("dve", ones_f[:], 1.0)
        memset("dve", epsD[:, 0:1], D * EPS)
        memset("dve", epsD[:, 1:2], 192 * EPS)
        memset("dve", S[:], 0.0)
        memset("dve", S_bf[:], 0.0)
        ts("dve", gsc[:], gsc[:], float(math.sqrt(D)), MUL)
        ts("dve", gout[:], gout[:], float(math.sqrt(192.0)), MUL)
        act(esink[:], esink[:], AF.Exp)
        cp("dve", perm_bf[:], cst[:, C_PM:C_PM + 128])

        triU = cst[:, C_TRIU:C_TRIU + 128]
        triG = cst[:, C_TRIG:C_TRIG + 128]
        sU = cst[0:64, C_SU:C_SU + 64]
        sG = cst[0:64, C_SG:C_SG + 64]
        onehot = cst[0:64, C_OH:C_OH + 16]
        maskC = cst[:, C_MC:C_MC + 4]
        maskN = cst[0:64, C_MN:C_MN + 64]

        def norm(src, n, gi, dst, sq):
            act(sq[:, :, 0:n], src, AF.Square)
            pp = PS()
            for k in range(8):
                mm(pp[:, 0:n], ones_bf[:], sq[:, k, 0:n], start=(k == 0), stop=(k == 7))
            sd = TM("nsd", [128, 512], F32)
            act_rsqrt(sd[:, 0:n], pp[:, 0:n], epsD[:, 0:1])
            for k in range(8):
                stt("dve", dst[:, k, 0:n], src[:, k, :], gsc[:, gi, k:k + 1], sd[:, 0:n], MUL, MUL)

        def load_pass_inputs(ps_, part=None):
            for pi, (a0, a1) in enumerate(((0, 512), (512, NPP))):
                if part is None or part == pi:
                    dma("sp", xT[:, :, a0:a1], kp(xT_p[:, ps_ * NPP + a0:ps_ * NPP + a1]))
            if part == 0:
                return
            if ps_ == 1:
                dma("sp", xT[:, :, NPP:NT], kp(xT_s))
            dma("sp", cosT[:], rope_d[ps_, 0])
            dma("sp", sinT[:], rope_d[ps_, 1])

        load_pass_inputs(0)

        arena_reset()
        memx = A([128, 8, 256], F32)
        sqm = A([128, 8, 256], BF16)
        hm = A([128, 8, 256], BF16)
        Wm = A([128, 8, 512], BF16)
        dma("sp", memx, kp(memT))
        for l in range(2):
            dma("pool", Wm, kp(w_memkv[l]))
            norm(memx, 256, 6 + l, hm, sqm)
            for jt in range(2):
                pp = PS()
                for k in range(8):
                    mm(pp[:, 0:512], hm[:, k, jt * 128:(jt + 1) * 128], Wm[:, k, :], start=(k == 0), stop=(k == 7))
                stg = TM("stg", [128, 512], F32)
                cp("act", stg, pp[:, 0:512])
                dma("sp", mem_k_o[l, jt * 128:(jt + 1) * 128, :], stg[:, 0:256])
                dma("sp", mem_v_o[l, jt * 128:(jt + 1) * 128, :], stg[:, 256:512])
                cp("dve", mv[:, l, jt, :], stg[:, 256:512])
            for c in range(2):
                pp = PS()
                for k in range(8):
                    mm(pp[:, 0:256], Wm[:, k, c * 128:(c + 1) * 128], hm[:, k, :], start=(k == 0), stop=(k == 7))
                cp("act", mkT[:, l, c, :], pp[:, 0:256])

        prepped = {}

        def ffn(l, which, tgs, ntp, hook=None):
            if DBG.get("KSKIPFFN"):
                return
            arena_reset()
            gi = (0 if which == 0 else 4) + l
            wgu = w_gu[which][l]
            wd = w_dn[which][l]
            hT = A([128, 8, ntp], BF16)
            sg = [A([128, 512], F32) for _ in range(2)]
            sq = A([128, 8, 512], BF16)
            for (c0, n) in tgs:
                norm(xT[:, :, c0:c0 + n], n, gi, hT[:, :, c0:c0 + n], sq)
            state_["aoff_lz"] = state_["aoff"]
            wbuf = [A([128, 8, 512], BF16, top=True) for _ in range(3)]
            Wds = [A([128, 12, D], BF16, top=True) for _ in range(2)]
            actT = A([128, 12, ntp], BF16)
            cnt = 0
            ui = 0
            halves = ((0, 10), (10, 12))

            def load_wd(hi):
                f0_, nf_ = halves[hi]
                dma("pool", Wds[hi][:, 0:nf_, :], wd[f0_ * 128:(f0_ + nf_) * 128, :].rearrange("(c p) n -> p c n", p=128))

            def load_unit(fc0_, buf_):
                dma("pool", buf_[:, :, 0:256], kp(wgu[:, fc0_ * 128:fc0_ * 128 + 256]))
                dma("pool", buf_[:, :, 256:512], kp(wgu[:, FF + fc0_ * 128:FF + fc0_ * 128 + 256]))

            units = [2 * u for u in range(11)]
            load_unit(units[0], wbuf[0])
            load_unit(units[1], wbuf[1])
            load_wd(0)
            nload = 2
            for hi, (f0, nf) in enumerate(halves):
                Wd = Wds[hi]
                for u in range(nf // 2):
                    fc0 = f0 + 2 * u
                    buf = wbuf[ui % 3]
                    ui += 1
                    if nload < len(units):
                        load_unit(units[nload], wbuf[nload % 3])
                        nload += 1
                    if hi == 0 and u == 1:
                        load_wd(1)
                    for j in range(2):
                        fi = 2 * u + j
                        for (c0, n) in tgs:
                            pg = PS()
                            pu = PS()
                            for k in range(8):
                                mm(pg[:, 0:n], buf[:, k, j * 128:(j + 1) * 128], hT[:, k, c0:c0 + n],
                                   start=(k == 0), stop=(k == 7))
                            for k in range(8):
                                mm(pu[:, 0:n], buf[:, k, 256 + j * 128:256 + (j + 1) * 128], hT[:, k, c0:c0 + n],
                                   start=(k == 0), stop=(k == 7))
                            s = sg[cnt % 2]
                            cnt += 1
                            act(s[:, 0:n], pg[:, 0:n], AF.Silu)
                            tt("dve", actT[:, fi, c0:c0 + n], s[:, 0:n], pu[:, 0:n], MUL)
                for ti, (c0, n) in enumerate(tgs):
                    for dc in range(8):
                        po = PS()
                        for fi in range(nf):
                            mm(po[:, 0:n], Wd[:, fi, dc * 128:(dc + 1) * 128], actT[:, fi, c0:c0 + n],
                               start=(fi == 0), stop=(fi == nf - 1))
                        stt("dve", xT[:, dc, c0:c0 + n], po[:, 0:n], 0.5, xT[:, dc, c0:c0 + n], MUL, ADD)
                        if hook is not None and hi == 1 and ti == 1 and dc == 0:
                            lz = state_["aoff_lz"]
                            hook()
                            assert state_["aoff"] <= lz, ("landing zone overflow", state_["aoff"], lz)

        def phase_entry(key, mode, c0, n, gi, wspecs):
            if mode == "body":
                d = prepped.pop(key)
                state_["aoff"] = d["aoff"]
                state_["atop"] = ARENA_BYTES
                tmpc.clear()
                return d["hT"], d["W"]
            arena_reset()
            W = []
            if mode == "all":
                for (shape, src) in wspecs:
                    w = A(shape, BF16)
                    dma("pool", w, src)
                    W.append(w)
            hT = A([128, 8, n], BF16)
            mk_ = amark()
            sq = A([128, 8, n], BF16)
            norm(xT[:, :, c0:c0 + n], n, gi, hT, sq)
            apop(mk_)
            if mode == "prep":
                for (shape, src) in wspecs:
                    w = A(shape, BF16)
                    dma("pool", w, src)
                    W.append(w)
                prepped[key] = dict(hT=hT, W=W, aoff=state_["aoff"])
                return None, None
            return hT, W

        def mem_attn_prompt(l, mqT, n, mix, slot0):
            def BK(b_):
                return psum[:, b_ * 512:(b_ + 1) * 512]
            for pr_ in range(2):
                pts = {}
                for jt in range(2):
                    for mi in range(2):
                        m = 2 * pr_ + mi
                        c = m // 2
                        p0 = (m % 2) * 64
                        sp_ = BK(2 * mi + jt)
                        mm(sp_[:, 0:n], mkT[p0:p0 + 64, l, c, jt * 128:(jt + 1) * 128], mqT[p0:p0 + 64, c, 0:n])
                for mi in range(2):
                    for jt in range(2):
                        pt = TM("mpt", [128, 512], BF16, 4)
                        act(pt[:, 0:n], BK(2 * mi + jt)[:, 0:n], AF.Exp, scale=0.125)
                        pts[(mi, jt)] = pt
                for mi in range(2):
                    m = 2 * pr_ + mi
                    p0 = (m % 2) * 64
                    po = BK(4 + 2 * mi)
                    pr = BK(5 + 2 * mi)
                    for jt in range(2):
                        mm(po[p0:p0 + 64, 0:n], mv[:, l, jt, m * 64:(m + 1) * 64], pts[(mi, jt)][:, 0:n], start=(jt == 0), stop=(jt == 1))
                    for jt in range(2):
                        mm(pr[p0:p0 + 64, 0:n], ones_bf[:, 0:64], pts[(mi, jt)][:, 0:n], start=(jt == 0), stop=(jt == 1))
                for mi in range(2):
                    m = 2 * pr_ + mi
                    p0 = (m % 2) * 64
                    rinv = TM("mrinv", [128, 512], F32)
                    act_recip(rinv[p0:p0 + 64, 0:n], BK(5 + 2 * mi)[p0:p0 + 64, 0:n])
                    tt("dve", mix[p0:p0 + 64, slot0 + m // 2, 0:n], BK(4 + 2 * mi)[p0:p0 + 64, 0:n], rinv[p0:p0 + 64, 0:n], MUL)

        def load_mem_cache(l, hb, ck, cv):
            dma("pool", ck, cmkT[l, 8 * hb:8 * hb + 8].rearrange("b (c two) d j -> (two d) b c j", two=2))
            dma("pool", cv, cmv[l, 8 * hb:8 * hb + 8].rearrange("b (jt p) f -> p b jt f", p=128))

        def mem_attn_sample(l, mqT, mix, slot0, pre=None):
            for hb in range(2):
                if pre is not None:
                    ck, cv = pre[hb]
                else:
                    ck = TM("ck", [128, 8, 2, 256], BF16, 1)
                    cv = TM("cv", [128, 8, 2, 256], BF16, 1)
                    load_mem_cache(l, hb, ck, cv)
                sp_ = PS()
                for (m, b) in [(m_, b_) for m_ in (0, 2, 1, 3) for b_ in range(8)]:
                    bg = 8 * hb + b
                    c = m // 2
                    p0 = (m % 2) * 64
                    for jt in range(2):
                        col = jt * 128 + (b * 4 + m) * 4
                        mm(sp_[:, col:col + 4], ck[p0:p0 + 64, b, c, jt * 128:(jt + 1) * 128],
                           mqT[p0:p0 + 64, c, 4 * bg:4 * bg + 4])
                pt = TM("spt", [128, 256], BF16)
                act(pt, sp_[:, 0:256], AF.Exp, scale=0.125)
                po = PS()
                pr = PS()
                for b in range(8):
                    for m in range(4):
                        col = (b * 4 + m) * 4
                        p0 = (m % 2) * 64
                        for jt in range(2):
                            mm(po[p0:p0 + 64, col:col + 4], cv[:, b, jt, m * 64:(m + 1) * 64],
                               pt[:, jt * 128 + col:jt * 128 + col + 4], start=(jt == 0), stop=(jt == 1))
                for jt in range(2):
                    mm(pr[:, 0:128], ones_bf[:, :], pt[:, jt * 128:(jt + 1) * 128], start=(jt == 0), stop=(jt == 1))
                rinv = TM("srinv", [128, 128], F32)
                act_recip(rinv, pr[:, 0:128])
                for m in range(4):
                    p0 = (m % 2) * 64
                    ov = po[p0:p0 + 64, 0:128].rearrange("p (b m t) -> p b m t", b=8, m=4)[:, :, m, :]
                    rv = rinv[p0:p0 + 64, :].rearrange("p (b m t) -> p b m t", b=8, m=4)[:, :, m, :]
                    outv = mix[p0:p0 + 64, slot0 + m // 2, 32 * hb:32 * hb + 32].rearrange("p (b t) -> p b t", b=8)
                    tt("dve", outv, ov, rv, MUL)

        def mixer_a(c0, n, sample, mode="all", cache_copy=False):
            T = 64 if sample else 128
            ntile = 1 if sample else n // 128
            tmask = sU if sample else triU
            gmask = sG if sample else triG
            wspecs = [([128, 8, 272], kp(a_w_in[:, 2304:2576])), ([128, 8, 768], kp(a_w_in[:, 768:1536]))]
            if mode == "all":
                wspecs.append(([128, 8, 768], kp(a_w_in[:, 0:768])))
            hT, W = phase_entry("ma", mode, c0, n, 2, wspecs)
            if mode == "prep":
                return
            Wgm = W[0]
            if len(W) < 3:
                w1 = A([128, 8, 768], BF16)
                dma("pool", w1, kp(a_w_in[:, 0:768]))
                W.append(w1)
            Wa = [W[2], W[1]]
            Shs = {}
            if sample:
                ShT = [A([96, 16, 192], F32, top=True) for _ in range(2)]
                ShbT = [A([96, 16, 192], BF16, top=True) for _ in range(2)]

                def load_state(h_):
                    dma("sp", ShT[h_ % 2], state[:, h_].rearrange("b d v -> d b v"))
                    dma("pool", ShbT[h_ % 2], state[:, h_].rearrange("b d v -> d b v"))
                    Shs[h_] = (ShT[h_ % 2], ShbT[h_ % 2])

                load_state(0)
                load_state(1)
            if cache_copy:
                stgT = A([64, 4, 3, 124], F32, top=True)
                stvT = A([124, 4, 192], F32, top=True)
                for hb_ in range(4):
                    bs_ = slice(4 * hb_, 4 * hb_ + 4)
                    dma("sp", stgT, kcT[bs_, :, :, 4:128].rearrange("b h d j -> d b h j"))
                    dma("sp", stvT, vc[bs_, 4:128, :].rearrange("b j f -> j b f"))
                    dma("sp", swa_kT_s_o[bs_, :, :, 0:124].rearrange("b h d j -> d b h j"), stgT)
                    dma("sp", swa_v_s_o[bs_, 0:124, :].rearrange("b j f -> j b f"), stvT)
            qs = A([96, 4, n], BF16)
            ks = A([96, 4, n], BF16)
            khat = A([128, ntile, 384], BF16)
            vtok = A([128, ntile, 768], BF16)
            sr = A([128, 6, n], BF16)
            mix = A([128, 8, n], BF16)
            mqT = A([128, 2, n], BF16)
            ebl = A([96, 4, 64], F32)
            mk_ = amark()
            glr = A([16, n], F32)
            lp = A([128, ntile, 384], F32)
            ekr = A([128, ntile, 384], F32)
            pp = PS()
            for k in range(8):
                mm(pp[0:16, 0:n], Wgm[:, k, 0:16], hT[:, k, :], start=(k == 0), stop=(k == 7))
            cp("act", glr, pp[0:16, 0:n])
            for c in range(2):
                pp = PS()
                for k in range(8):
                    mm(pp[:, 0:n], Wgm[:, k, 16 + c * 128:16 + (c + 1) * 128], hT[:, k, :], start=(k == 0), stop=(k == 7))
                cp("act", mqT[:, c, :], pp[:, 0:n])
            for t in range(ntile):
                tc0 = t * 128
                pz = PS()
                mm(pz[0:T, 0:384], glr[:, tc0:tc0 + T], wgate[:], start=True, stop=False)
                mm(pz[0:T, 0:384], ones_f[0:1, 0:T], bgate[:], start=False, stop=True)
                e1 = TM("e1", [128, 384], F32)
                act(e1[0:T], pz[0:T, 0:384], AF.Exp, scale=-1.0)
                act(lp[0:T, t, :], e1[0:T], AF.Ln, bias=1.0)
                pr = PS()
                mm(pr[0:T, 0:384], gmask[0:T, 0:T], lp[0:T, t, :])
                act(ekr[0:T, t, :], pr[0:T, 0:384], AF.Exp, scale=-1.0 / 16.0)
            for t in range(ntile):
                tc0 = t * 128
                pa = PS()
                pb = PS()
                for k in range(8):
                    mm(pa[0:T, 0:512], hT[:, k, tc0:tc0 + T], Wa[1][:, k, 0:512], start=(k == 0), stop=(k == 7))
                for k in range(8):
                    mm(pb[0:T, 0:256], hT[:, k, tc0:tc0 + T], Wa[1][:, k, 512:768], start=(k == 0), stop=(k == 7))
                cp("act", vtok[0:T, t, 0:512], pa[0:T, 0:512])
                cp("act", vtok[0:T, t, 512:768], pb[0:T, 0:256])
            dma("pool", Wa[1], kp(a_w_in[:, 1536:2304]))
            for h in range(4):
                ebh = TM("ebh", [96, n], F32)
                enbh = TM("enbh", [96, n], F32)
                for t in range(ntile):
                    tc0 = t * 128
                    pb = PS()
                    mm(pb[0:96, 0:T], lp[0:T, t, 96 * h:96 * h + 96], tmask[0:T, 0:T])
                    act(ebh[:, tc0:tc0 + T], pb[0:96, 0:T], AF.Exp, scale=-1.0 / 16.0)
                    act(enbh[:, tc0:tc0 + T], pb[0:96, 0:T], AF.Exp, scale=1.0 / 16.0)
                if sample:
                    cp("dve", ebl[:, h, :], ebh[:, 0:64])
                else:
                    cp("dve", ebl[:, h, 0:ntile], ebh.rearrange("p (t i) -> p t i", i=128)[:, :, 127])
                pq = PS()
                for k in range(8):
                    mm(pq[0:96, 0:n], Wa[0][:, k, 96 * h:96 * h + 96], hT[:, k, :], start=(k == 0), stop=(k == 7))
                stt("dve", qs[:, h, :], pq[0:96, 0:n], float(96.0 ** -0.5), ebh, MUL, MUL)
                pk = PS()
                for k in range(8):
                    mm(pk[0:96, 0:n], Wa[0][:, k, 384 + 96 * h:384 + 96 * h + 96], hT[:, k, :], start=(k == 0), stop=(k == 7))
                tt("dve", ks[:, h, :], pk[0:96, 0:n], enbh, MUL)
            for t in range(ntile):
                tc0 = t * 128
                pk = PS()
                for k in range(8):
                    mm(pk[0:T, 0:384], hT[:, k, tc0:tc0 + T], Wa[0][:, k, 384:768], start=(k == 0), stop=(k == 7))
                tt("dve", khat[0:T, t, :], pk[0:T, 0:384], ekr[0:T, t, :], MUL)
            for ch in range(6):
                pa = PS()
                for k in range(8):
                    mm(pa[:, 0:n], Wa[1][:, k, 128 * ch:128 * ch + 128], hT[:, k, :], start=(k == 0), stop=(k == 7))
                act(sr[:, ch, :], pa[:, 0:n], AF.Silu)
            apop(mk_)
            Wo = A([128, 8, D], BF16, at=0)
            for h in range(4):
                p0 = (h % 2) * 64
                dma("pool", Wo[:, h, :], a_w_out[192 * h:192 * h + 128, :])
                dma("pool", Wo[p0:p0 + 64, 4 + h // 2, :], a_w_out[192 * h + 128:192 * h + 192, :])
            dma("pool", Wo[:, 6:8, :], a_w_out[768:1024, :].rearrange("(s p) n -> p s n", p=128))

            def BK(b_):
                return psum[:, b_ * 512:(b_ + 1) * 512]

            def gate_store(h, tc0, oA, oB, rsA, rsB):
                p0 = (h % 2) * 64
                ta = TM("fta%d" % h, [128, 128], F32)
                tb = TM("ftb%d" % h, [128, 128], F32)
                stt("dve", ta[:, 0:T], oA, gout[:, 0:1], rsA, MUL, MUL)
                tt("pool", mix[:, h, tc0:tc0 + T], ta[:, 0:T], sr[:, h, tc0:tc0 + T], MUL)
                stt("dve", tb[p0:p0 + 64, 0:T], oB, gout[p0:p0 + 64, 1:2], rsB, MUL, MUL)
                tt("pool", mix[p0:p0 + 64, 4 + h // 2, tc0:tc0 + T], tb[p0:p0 + 64, 0:T],
                   sr[p0:p0 + 64, 4 + h // 2, tc0:tc0 + T], MUL)

            if not sample:
                ams = {}
                pob = {}
                sqs = {}

                def S1(t):
                    tc0 = t * 128
                    for h in range(4):
                        mm(BK(0)[:, h * 128:(h + 1) * 128], ks[:, h, tc0:tc0 + 128], qs[:, h, tc0:tc0 + 128])

                def S2(t):
                    for h in range(4):
                        am = TM("am%d" % h, [128, 128], BF16)
                        tt("dve", am, BK(0)[:, h * 128:(h + 1) * 128], triU, MUL)
                        ams[(t, h)] = am

                def S3(t):
                    tc0 = t * 128
                    for h in range(4):
                        p0 = (h % 2) * 64
                        bk = BK(2 + state_["gl"] % 6)
                        state_["gl"] += 1
                        pob[(t, h)] = bk
                        am = ams[(t, h)]
                        mm(bk[:, 0:128], vtok[:, t, 192 * h:192 * h + 128], am, start=True, stop=False)
                        mm(bk[:, 0:128], S_bf[:, h, 0:128], qs[:, h, tc0:tc0 + 128], start=False, stop=True)
                        mm(bk[p0:p0 + 64, 128:256], vtok[:, t, 192 * h + 128:192 * h + 192], am, start=True, stop=False)
                        mm(bk[p0:p0 + 64, 128:256], S_bf[:, h, 128:192], qs[:, h, tc0:tc0 + 128], start=False, stop=True)
                        mm(bk[0:96, 256:448], khat[:, t, 96 * h:96 * h + 96], vtok[:, t, 192 * h:192 * h + 192])

                def S4(t):
                    for h in range(4):
                        p0 = (h % 2) * 64
                        bk = pob[(t, h)]
                        stt("dve", S[:, h, :], S[:, h, :], ebl[:, h, t:t + 1], bk[0:96, 256:448], MUL, ADD)
                        cp("act", S_bf[:, h, :], S[:, h, :])
                        sqa = TM("sqa%d" % h, [128, 128], BF16)
                        sqb = TM("sqb%d" % h, [128, 128], BF16)
                        act(sqa, bk[:, 0:128], AF.Square)
                        act(sqb[p0:p0 + 64, :], bk[p0:p0 + 64, 128:256], AF.Square)
                        sqs[(t, h)] = (sqa, sqb)

                def S5(t):
                    for h in range(4):
                        p0 = (h % 2) * 64
                        sqa, sqb = sqs[(t, h)]
                        mm(BK(1)[:, h * 128:(h + 1) * 128], ones_bf[:, :], sqa, start=True, stop=False)
                        mm(BK(1)[:, h * 128:(h + 1) * 128], ones_bf[p0:p0 + 64, :], sqb[p0:p0 + 64, :], start=False, stop=True)

                def S6(t):
                    tc0 = t * 128
                    rs = TM("frs4", [128, 512], F32)
                    act_rsqrt(rs, BK(1)[:, 0:512], epsD[:, 1:2])
                    for h in range(4):
                        p0 = (h % 2) * 64
                        bk = pob[(t, h)]
                        gate_store(h, tc0, bk[:, 0:128], bk[p0:p0 + 64, 128:256],
                                   rs[:, h * 128:(h + 1) * 128], rs[p0:p0 + 64, h * 128:(h + 1) * 128])

                state_["gl"] = 0
                S1(0)
                S2(0)
                for t in range(ntile):
                    S3(t)
                    if t + 1 < ntile:
                        S1(t + 1)
                        S2(t + 1)
                    S4(t)
                    S5(t)
                    S6(t)
            else:
                def out_stages(h):
                    p0 = (h % 2) * 64
                    Sh, Shb = Shs[h]
                    st_ = {}

                    def s1():
                        st_["pA"] = PS()
                        mm(st_["pA"][0:64, 0:64], ks[:, h, 0:64], qs[:, h, 0:64])

                    def s2():
                        st_["am"] = TM("ams", [64, 64], BF16)
                        tt("dve", st_["am"], st_["pA"][0:64, 0:64], sU, MUL)

                    def s3():
                        st_["poA"] = PS()
                        mm(st_["poA"][:, 0:64], vtok[0:64, 0, 192 * h:192 * h + 128], st_["am"])
                        st_["poB"] = PS()
                        mm(st_["poB"][p0:p0 + 64, 0:64], vtok[0:64, 0, 192 * h + 128:192 * h + 192], st_["am"])

                    def s4():
                        st_["p2A"] = PS()
                        st_["p2B"] = PS()
                        for b in range(16):
                            mm(st_["p2A"][:, 4 * b:4 * b + 4], Shb[:, b, 0:128], qs[:, h, 4 * b:4 * b + 4])
                        for b in range(16):
                            mm(st_["p2B"][p0:p0 + 64, 4 * b:4 * b + 4], Shb[:, b, 128:192], qs[:, h, 4 * b:4 * b + 4])

                    def s5():
                        oA1 = TM("oA1", [128, 64], F32)
                        oB1 = TM("oB1", [128, 64], F32)
                        cp("act", oA1, st_["p2A"][:, 0:64])
                        cp("act", oB1[p0:p0 + 64, :], st_["p2B"][p0:p0 + 64, 0:64])
                        tt("dve", oA1, st_["poA"][:, 0:64], oA1, ADD)
                        tt("dve", oB1[p0:p0 + 64, :], st_["poB"][p0:p0 + 64, 0:64], oB1[p0:p0 + 64, :], ADD)
                        st_["oA1"] = oA1
                        st_["oB1"] = oB1

                    def s6():
                        sqa = TM("sqa", [128, 64], BF16)
                        sqb = TM("sqb", [128, 64], BF16)
                        act(sqa, st_["oA1"], AF.Square)
                        act(sqb[p0:p0 + 64, :], st_["oB1"][p0:p0 + 64, :], AF.Square)
                        st_["pq"] = PS()
                        mm(st_["pq"][:, 0:64], ones_bf[:, :], sqa, start=True, stop=False)
                        mm(st_["pq"][:, 0:64], ones_bf[p0:p0 + 64, :], sqb[p0:p0 + 64, :], start=False, stop=True)

                    def s7():
                        rs = TM("frs", [128, 64], F32)
                        act_rsqrt(rs, st_["pq"][:, 0:64], epsD[:, 1:2])
                        st_["rs"] = rs

                    def s8():
                        gate_store(h, 0, st_["oA1"], st_["oB1"][p0:p0 + 64, :], st_["rs"], st_["rs"][p0:p0 + 64, :])

                    return [s1, s2, s3, s4, s5, s6, s7, s8]

                def upd_steps(h):
                    Sh, Shb = Shs[h]
                    st_ = {}

                    def u0():
                        Vb = TM("Vb", [64, 16, 192], BF16, 1)
                        vin = vtok[0:64, 0, 192 * h:192 * h + 192].unsqueeze(1).to_broadcast([64, 16, 192])
                        ohb = onehot.unsqueeze(2).to_broadcast([64, 16, 192])
                        tt("dve", Vb, vin, ohb, MUL)
                        st_["Vb"] = Vb
                        st_["Sn"] = TM("Sn", [96, 16, 192], F32, 1)

                    def ub(bp):
                        def f():
                            eblv = ebl[:, h, :].rearrange("p (b t) -> p b t", t=4)[:, :, 3]
                            pd = PS()
                            mm(pd[0:96, 0:384], khat[0:64, 0, 96 * h:96 * h + 96],
                               st_["Vb"][:, 2 * bp:2 * bp + 2, :].rearrange("p b v -> p (b v)"))
                            tmp = TM("stmp", [96, 2, 192], F32)
                            tt("dve", tmp, Sh[:, 2 * bp:2 * bp + 2, :],
                               eblv[:, 2 * bp:2 * bp + 2].unsqueeze(2).to_broadcast([96, 2, 192]), MUL)
                            tt("dve", st_["Sn"][:, 2 * bp:2 * bp + 2, :], tmp,
                               pd[0:96, 0:384].rearrange("p (b v) -> p b v", b=2), ADD)
                        return f

                    def ue():
                        dma("sp", gla_s_o[:, h].rearrange("b d v -> d b v"), st_["Sn"])

                    return [u0] + [ub(bp) for bp in range(8)] + [ue]

                for f in out_stages(0):
                    f()
                for h in range(4):
                    if 1 <= h + 1 < 4 and h + 1 >= 2:
                        load_state(h + 1)
                    ups = upd_steps(h)
                    outs = out_stages(h + 1) if h + 1 < 4 else []
                    k_ = 0
                    for i_, u in enumerate(ups):
                        u()
                        if i_ >= 1 and k_ < len(outs):
                            outs[k_]()
                            k_ += 1
                    while k_ < len(outs):
                        outs[k_]()
                        k_ += 1
            if sample:
                mem_attn_sample(0, mqT, mix, 6)
            else:
                mem_attn_prompt(0, mqT, n, mix, 6)
            for dc in range(8):
                po = PS()
                for s_ in range(8):
                    mm(po[:, 0:n], Wo[:, s_, dc * 128:(dc + 1) * 128], mix[:, s_, :], start=(s_ == 0), stop=(s_ == 7))
                tt("dve", xT[:, dc, c0:c0 + n], po[:, 0:n], xT[:, dc, c0:c0 + n], ADD)

        def kv_phase(ps_, c0, n, sample, mode="all"):
            hT, W = phase_entry("kv", mode, c0, n, 8, [([128, 8, 960], kp(w_kvx))])
            if mode == "prep":
                return
            Wk = W[0]
            need_f32 = sample or (ps_ == 1 and c0 == 512)
            kf = kf_p[:] if need_f32 else None
            for kvh in range(3):
                p1 = PS()
                p2 = PS()
                for k in range(8):
                    mm(p1[:, 0:n], Wk[:, k, 128 * kvh:128 * kvh + 128], hT[:, k, :], start=(k == 0), stop=(k == 7))
                for k in range(8):
                    mm(p2[:, 0:n], Wk[:, k, 384 + 128 * kvh:384 + 128 * kvh + 128], hT[:, k, :], start=(k == 0), stop=(k == 7))
                t1 = TM("t1", [128, 512], F32)
                t2 = TM("t2", [128, 512], F32)
                tt("dve", t1[:, 0:n], p1[:, 0:n], cosT[:, c0:c0 + n], MUL)
                tt("dve", t2[:, 0:n], p2[:, 0:n], sinT[:, c0:c0 + n], MUL)
                tt("pool", kTd[:, kvh, c0:c0 + n], t1[:, 0:n], t2[:, 0:n], ADD)
                if need_f32:
                    if sample:
                        tt("pool", kf[:, kvh, 0:64], t1[0:64, 0:64], t2[0:64, 0:64], ADD)
                    else:
                        tt("pool", kf[:, kvh, :], t1[0:64, n - 128:n], t2[0:64, n - 128:n], ADD)
            T = 64 if sample else 128
            ntile = 1 if sample else n // 128
            for t in range(ntile):
                tc0 = t * 128
                ti = 8 if sample else (c0 // 128 + t)
                pv = PS()
                for k in range(8):
                    mm(pv[0:T, 0:192], hT[:, k, tc0:tc0 + T], Wk[:, k, 768:960], start=(k == 0), stop=(k == 7))
                cp("act", vtokB[0:T, ti, :], pv[0:T, 0:192])
                last = (not sample) and ps_ == 1 and c0 == 512 and t == ntile - 1
                if last or sample:
                    vf = vf_p[:]
                    cp("dve", vf[0:T], pv[0:T, 0:192])
                    if last:
                        dma("sp", swa_v_p_o, vf)
                    else:
                        for b in range(16):
                            dma("sp", swa_v_s_o[b, 124:128, :], vf[4 * b:4 * b + 4, :])
            if (not sample) and ps_ == 1 and c0 == 512:
                dma("sp", swa_kT_p_o.rearrange("h d j -> d h j"), kf)
            if (not sample) and ps_ == 0 and c0 == 512:
                cp("pool", kT_prev[:], kTd[:, :, NPP - 128:NPP])
                cp("pool", v_prev[:], vtokB[:, 7, :])
            if sample:
                for kvh in range(3):
                    dma("sp", swa_kT_s_o[:, kvh, :, 124:128].rearrange("b d t -> d b t"),
                        kf[:, kvh, 0:64].rearrange("p (b t) -> p b t", t=4))

        def mixer_b(ps_, c0, n, sample, mode="all"):
            wspecs = [([128, 8, 768], kp(b_w_inx[:, 0:768]))]
            if mode == "all":
                wspecs.append(([128, 8, 256], kp(b_w_inx[:, 1536:1792])))
            hT, W = phase_entry("mb", mode, c0, n, 3, wspecs)
            if mode == "prep":
                return
            Wq = W[0]
            if len(W) < 2:
                w2 = A([128, 8, 256], BF16)
                dma("pool", w2, kp(b_w_inx[:, 1536:1792]))
                W.append(w2)
            Wq2 = W[1]
            if sample:
                kc = A([128, 16, 3, 128], BF16, top=True)
                vcb = A([128, 16, 192], BF16, top=True)
                dma("pool", kc[0:64], kcT.rearrange("b h d j -> d b h j"))
                dma("pool", kc[64:128], kcT.rearrange("b h d j -> d b h j"))
                dma("pool", vcb, vc.rearrange("b j f -> j b f"))
                mem_pre = []
                for hb_ in range(2):
                    ck_ = A([128, 8, 2, 256], BF16, top=True)
                    cv_ = A([128, 8, 2, 256], BF16, top=True)
                    load_mem_cache(1, hb_, ck_, cv_)
                    mem_pre.append((ck_, cv_))
            Wo = A([128, 8, D], BF16)
            dma("pool", Wo, kp(b_w_out))
            qT = A([128, 6, n], BF16)
            for c in range(6):
                p1 = PS()
                p2 = PS()
                for k in range(8):
                    mm(p1[:, 0:n], Wq[:, k, 128 * c:128 * c + 128], hT[:, k, :], start=(k == 0), stop=(k == 7))
                qb = TM("qb", [128, 512], BF16)
                cp("act", qb[:, 0:n], p1[:, 0:n])
                mm(p2[:, 0:n], perm_bf[:], qb[:, 0:n])
                t1 = TM("t1", [128, 512], F32)
                t2 = TM("t2", [128, 512], F32)
                tt("dve", t1[:, 0:n], p1[:, 0:n], cosT[:, c0:c0 + n], MUL)
                tt("dve", t2[:, 0:n], p2[:, 0:n], sinT[:, c0:c0 + n], MUL)
                tt("pool", qT[:, c, :], t1[:, 0:n], t2[:, 0:n], ADD)
            mqT = A([128, 2, n], BF16)
            for c in range(2):
                pp = PS()
                for k in range(8):
                    mm(pp[:, 0:n], Wq2[:, k, c * 128:(c + 1) * 128], hT[:, k, :], start=(k == 0), stop=(k == 7))
                cp("act", mqT[:, c, :], pp[:, 0:n])
            mix = A([128, 8, n], BF16)
            KSUB = int(DBG.get("KSUB", 99))
            if KSUB <= 1:
                return
            if not sample:
                def BK(b_):
                    return psum[:, b_ * 512:(b_ + 1) * 512]
                units = [(t, kvh) for t in range(n // 128) for kvh in range(3)]
                pTs = {}

                def info(ui):
                    t, kvh = units[ui]
                    lt = c0 // 128 + t
                    gt = ps_ * 8 + lt
                    return t, kvh, lt, gt, ([1] if gt == 0 else [0, 1])

                def U1(ui):
                    t, kvh, lt, gt, ws = info(ui)
                    tc0 = t * 128
                    worder = [1, 0] if len(ws) == 2 else [1]
                    for wi, which in enumerate(worder):
                        for ge in range(2):
                            for par in range(2):
                                hd = 4 * kvh + 2 * ge + par
                                c = hd // 2
                                p0 = par * 64
                                if which == 1:
                                    kk = kTd[p0:p0 + 64, kvh, c0 + tc0:c0 + tc0 + 128]
                                elif lt == 0:
                                    kk = kT_prev[p0:p0 + 64, kvh, :]
                                else:
                                    kk = kTd[p0:p0 + 64, kvh, c0 + tc0 - 128:c0 + tc0]
                                col = (wi * 2 + ge) * 128
                                mm(BK(2 * (ui % 2) + par)[:, col:col + 128], kk, qT[p0:p0 + 64, c, tc0:tc0 + 128])

                def U2(ui):
                    t, kvh, lt, gt, ws = info(ui)
                    nw = len(ws)
                    pT = TM("pT", [128, 2, 512], BF16)
                    pTs[ui] = pT
                    for par in range(2):
                        ex = TM("ex", [128, 512], F32, 4)
                        act(ex[:, 0:256 * nw], BK(2 * (ui % 2) + par)[:, 0:256 * nw], AF.Exp, scale=0.125)
                        for wi in range(nw):
                            msk = cst[:, C_TRIU + 128 * wi:C_TRIU + 128 * wi + 128].unsqueeze(1).to_broadcast([128, 2, 128])
                            outv = pT[:, wi, :].rearrange("p (ge pa i) -> p ge pa i", ge=2, pa=2)[:, :, par, :]
                            inv = ex[:, 256 * wi:256 * wi + 256].rearrange("p (ge i) -> p ge i", ge=2)
                            tt("pool", outv, inv, msk, MUL)

                def U3(ui):
                    t, kvh, lt, gt, ws = info(ui)
                    pT = pTs[ui]
                    po = BK(4 + 2 * (ui % 2))
                    pr = BK(5 + 2 * (ui % 2))
                    for par in range(2):
                        for which in ws:
                            if which == 1:
                                vv = vtokB[:, lt, 64 * kvh:64 * kvh + 64]
                            elif lt == 0:
                                vv = v_prev[:, 64 * kvh:64 * kvh + 64]
                            else:
                                vv = vtokB[:, lt - 1, 64 * kvh:64 * kvh + 64]
                            rhs = pT[:, 1 - which, :].rearrange("p (ge pa i) -> p ge pa i", ge=2, pa=2)[:, :, par, :]
                            mm(po[par * 64:(par + 1) * 64, 0:256], vv, rhs, start=(which == ws[0]), stop=(which == 1))
                    for which in ws:
                        mm(pr[:, 0:512], ones_bf[:, :], pT[:, 1 - which, :], start=(which == ws[0]), stop=(which == 1))

                def U4(ui):
                    t, kvh, lt, gt, ws = info(ui)
                    tc0 = t * 128
                    po = BK(4 + 2 * (ui % 2))
                    pr = BK(5 + 2 * (ui % 2))
                    den = TM("den", [128, 2, 128], F32)
                    for par in range(2):
                        p0 = par * 64
                        prv = pr[p0:p0 + 64, 0:512].rearrange("p (ge pa i) -> p ge pa i", ge=2, pa=2)[:, :, par, :]
                        esv = esink[p0:p0 + 64, 4 * kvh:4 * kvh + 4].rearrange("p (ge pa) -> p ge pa", pa=2)[:, :, par]
                        tt("dve", den[p0:p0 + 64], prv, esv.unsqueeze(2).to_broadcast([64, 2, 128]), ADD)
                    act_recip(den, den)
                    for par in range(2):
                        p0 = par * 64
                        tt("dve", mix[p0:p0 + 64, 2 * kvh:2 * kvh + 2, tc0:tc0 + 128],
                           po[p0:p0 + 64, 0:256].rearrange("p (ge i) -> p ge i", ge=2), den[p0:p0 + 64], MUL)

                U1(0)
                U2(0)
                for ui in range(len(units)):
                    if ui + 1 < len(units):
                        U1(ui + 1)
                    U3(ui)
                    if ui + 1 < len(units):
                        U2(ui + 1)
                    U4(ui)
                if KSUB <= 3:
                    return
                mem_attn_prompt(1, mqT, n, mix, 6)
                if KSUB <= 4:
                    return
            else:
                sC = PS(2)
                sN = PS(2)
                for hd in (0, 2, 4, 6, 8, 10, 1, 3, 5, 7, 9, 11):
                    kvh = hd // 4
                    c = hd // 2
                    p0 = (hd % 2) * 64
                    for b in range(16):
                        col = hd * 64 + b * 4
                        mm(sC[:, col:col + 4], kc[p0:p0 + 64, b, kvh, :], qT[p0:p0 + 64, c, 4 * b:4 * b + 4])
                    mm(sN[0:64, hd * 64:(hd + 1) * 64], kTd[p0:p0 + 64, kvh, NPP:NPP + 64], qT[p0:p0 + 64, c, 0:64])
                exC = A([128, 768], F32)
                exN = A([64, 768], F32)
                act(exC, sC[:, 0:768], AF.Exp, scale=0.125)
                act(exN, sN[0:64, 0:768], AF.Exp, scale=0.125)
                eC = A([128, 768], BF16)
                eN = A([64, 768], BF16)
                tt("dve", eC.rearrange("p (a t) -> p a t", t=4), exC.rearrange("p (a t) -> p a t", t=4),
                   maskC.unsqueeze(1).to_broadcast([128, 192, 4]), MUL)
                tt("dve", eN.rearrange("p (h f) -> p h f", h=12), exN.rearrange("p (h f) -> p h f", h=12),
                   maskN.unsqueeze(1).to_broadcast([64, 12, 64]), MUL)
                po = PS(2)
                pr = PS(2)
                po2 = sC
                for hd in range(12):
                    kvh = hd // 4
                    p0 = (hd % 2) * 64
                    for b in range(16):
                        col = hd * 64 + b * 4
                        mm(po[p0:p0 + 64, col:col + 4], vcb[:, b, 64 * kvh:64 * kvh + 64], eC[:, col:col + 4])
                    mm(po2[p0:p0 + 64, hd * 64:(hd + 1) * 64], vtokB[0:64, 8, 64 * kvh:64 * kvh + 64],
                       eN[:, hd * 64:(hd + 1) * 64])
                o2s = A([128, 768], F32)
                cp("act", o2s, po2[:, 0:768])
                for (a0, a1) in ((0, 512), (512, 768)):
                    mm(pr[:, a0:a1], ones_bf[:, :], eC[:, a0:a1], start=True, stop=False)
                    mm(pr[:, a0:a1], ones_bf[0:64, :], eN[:, a0:a1], start=False, stop=True)
                den = TM("dens", [128, 12, 64], F32, 1)
                tt("dve", den, pr[:, 0:768].rearrange("p (h f) -> p h f", h=12),
                   esink[:, 0:12].unsqueeze(2).to_broadcast([128, 12, 64]), ADD)
                act_recip(den, den)
                for par in range(2):
                    p0 = par * 64
                    pov = po[p0:p0 + 64, 0:768].rearrange("p (c pa f) -> p c pa f", c=6, pa=2)[:, :, par, :]
                    o2v = o2s[p0:p0 + 64, :].rearrange("p (c pa f) -> p c pa f", c=6, pa=2)[:, :, par, :]
                    dnv = den[p0:p0 + 64].rearrange("p (c pa) f -> p c pa f", pa=2)[:, :, par, :]
                    osum = TM("osum", [128, 6, 64], F32, 1)
                    tt("dve", osum[p0:p0 + 64], pov, o2v, ADD)
                    tt("dve", mix[p0:p0 + 64, 0:6, :], osum[p0:p0 + 64], dnv, MUL)
                mem_attn_sample(1, mqT, mix, 6, pre=mem_pre)
            for dc in range(8):
                po = PS()
                for s_ in range(8):
                    mm(po[:, 0:n], Wo[:, s_, dc * 128:(dc + 1) * 128], mix[:, s_, :], start=(s_ == 0), stop=(s_ == 7))
                tt("dve", xT[:, dc, c0:c0 + n], po[:, 0:n], xT[:, dc, c0:c0 + n], ADD)

        def final(ps_, c0, n, sample):
            arena_reset()
            sq = A([128, 8, n], BF16)
            yst = A([128, 8, n], F32)
            norm(xT[:, :, c0:c0 + n], n, 9, yst, sq)
            oc = 2048 if sample else ps_ * NPP + c0
            dma("sp", kp(yT_o[:, oc:oc + n]), yst)

        KST = int(DBG.get("KSTAGE", 99))
        KPASS = int(DBG.get("KPASS", 2))
        for ps_ in range(min(2, KPASS)):
            tgs = [(0, 512, False), (512, 512, False)] + ([(NPP, NSM, True)] if ps_ == 1 else [])
            ntp = NPP + (NSM if ps_ == 1 else 0)
            tg2 = [(c0, n) for (c0, n, _) in tgs]
            HK = not DBG.get("KNOHOOK")
            HK2 = HK and KST >= 7 and KPASS >= 2
            if ps_ == 1:
                load_pass_inputs(1, part=(1 if HK2 else None))
            if KST >= 1:
                ffn(0, 0, tg2, ntp, hook=(lambda: mixer_a(0, 512, False, mode="prep")) if (HK and KST >= 2) else None)
            if KST >= 2:
                for (c0, n, smp) in tgs:
                    if smp and KST == 2 and DBG.get("KNOSAMPLE"):
                        continue
                    mixer_a(c0, n, smp, mode=("body" if (HK and c0 == 0) else "all"),
                            cache_copy=(ps_ == 1 and c0 == 0))
                if ps_ == 1:
                    dma("sp", gla_p_o.rearrange("h d v -> d h v"), S[:])
            if KST >= 3:
                ffn(0, 1, tg2, ntp, hook=(lambda: kv_phase(ps_, 0, 512, False, mode="prep")) if (HK and KST >= 4) else None)
            if KST >= 4:
                for (c0, n, smp) in tgs:
                    kv_phase(ps_, c0, n, smp, mode=("body" if (HK and c0 == 0) else "all"))
            if KST >= 5:
                ffn(1, 0, tg2, ntp, hook=(lambda: mixer_b(ps_, 0, 512, False, mode="prep")) if (HK and KST >= 6) else None)
            if KST >= 6:
                for (c0, n, smp) in tgs:
                    mixer_b(ps_, c0, n, smp, mode=("body" if (HK and c0 == 0) else "all"))
            def fin_hook(ps_=ps_):
                final(ps_, 0, 512, False)
                if ps_ == 0:
                    load_pass_inputs(1, part=0)
            if KST >= 7:
                ffn(1, 1, tg2, ntp, hook=fin_hook if HK2 else None)
            for (c0, n, smp) in tgs:
                if HK2 and c0 == 0:
                    continue
                final(ps_, c0, n, smp)

        P.emit(st)
    return nc


def prep(x_prompt, x_sample, state_gla, cache_swa_k, cache_swa_v, cache_mem_k, cache_mem_v,
         mem_prompt, ffn1_norm, ffn1_w_gu, ffn1_w_down, mix_norm, ffn2_norm, ffn2_w_gu,
         ffn2_w_down, mem_norm, mem_w_kv, a_w_in, a_w_gate, a_b_gate, a_out_norm, a_w_out,
         kv_norm, w_kv, b_w_in, b_sinks, b_w_out, final_norm):
    f = lambda a: np.ascontiguousarray(np.asarray(a, dtype=np.float32))
    x_prompt, x_sample, state_gla = f(x_prompt), f(x_sample), f(state_gla)
    cache_swa_k, cache_swa_v = f(cache_swa_k), f(cache_swa_v)
    cache_mem_k, cache_mem_v, mem_prompt = f(cache_mem_k), f(cache_mem_v), f(mem_prompt)
    n = 8
    gains = np.stack([f(ffn1_norm)[0], f(ffn1_norm)[1], f(mix_norm)[0], f(mix_norm)[1],
                      f(ffn2_norm)[0], f(ffn2_norm)[1], f(mem_norm)[0], f(mem_norm)[1],
                      f(kv_norm), f(final_norm)], 0)
    gains_p = f(gains.reshape(10, 8, 128).transpose(2, 0, 1))
    go = f(a_out_norm).reshape(192)
    gout = np.zeros((128, 2), np.float32)
    gout[:, 0] = go[0:128]
    gout[0:64, 1] = go[128:192]
    gout[64:128, 1] = go[128:192]
    sinks = f(np.broadcast_to(f(b_sinks).reshape(1, 12), (128, 12)))
    wkv = f(w_kv)
    wk = wkv[:, 0:192]
    wkp = wk[:, rope_perm_cols(192)]
    kd = np.concatenate([np.concatenate([wk[:, 64 * h:64 * h + 64]] * 2, 1) for h in range(3)], 1)
    kdp = np.concatenate([np.concatenate([wkp[:, 64 * h:64 * h + 64]] * 2, 1) for h in range(3)], 1)
    w_kvx = f(np.concatenate([kd, kdp, wkv[:, 192:384]], 1))
    bwi = f(b_w_in)[0]
    wq = bwi[:, 0:768]
    b_w_inx = f(np.concatenate([wq, wq[:, rope_perm_cols(768)], bwi[:, 768:1024]], 1))
    awi = f(a_w_in)[0]
    rcols = [1536 + 192 * h + j for h in range(4) for j in range(128)] + \
            [1536 + 192 * h + 128 + j for h in range(4) for j in range(64)]
    awi = np.concatenate([awi[:, 0:1536], awi[:, rcols], awi[:, 2304:2576]], 1)
    shared = dict(
        w_gu1=f(ffn1_w_gu), w_d1=f(ffn1_w_down), w_gu2=f(ffn2_w_gu), w_d2=f(ffn2_w_down),
        w_memkv=f(mem_w_kv), a_w_in=f(awi), a_w_gate=f(a_w_gate)[0], a_b_gate=f(a_b_gate).reshape(1, 384),
        a_w_out=f(a_w_out)[0], w_kvx=w_kvx, b_w_inx=b_w_inx, b_w_out=f(b_w_out)[0],
        gains=gains_p, gout=gout, sinks=sinks, consts=make_consts(), rope=make_rope(),
    )
    in_maps = []
    for c in range(n):
        bs = slice(16 * c, 16 * c + 16)
        m = dict(shared)
        m["xT_p"] = f(x_prompt[c].T)
        m["xT_s"] = f(x_sample[bs].reshape(64, D).T)
        m["memT"] = f(mem_prompt[c].T)
        m["state"] = f(state_gla[0, bs])
        m["kcT"] = f(cache_swa_k[bs].transpose(0, 2, 3, 1))
        m["vc"] = f(cache_swa_v[bs].reshape(16, 128, 192))
        m["cmkT"] = f(cache_mem_k[:, bs].transpose(0, 1, 3, 4, 2))
        m["cmv"] = f(cache_mem_v[:, bs].reshape(2, 16, 256, 256))
        in_maps.append(m)
    return in_maps


def assemble(R):
    n = 8
    y_prompt = np.stack([R[c]["yT"][:, 0:2048].T for c in range(n)], 0)
    y_sample = np.concatenate([R[c]["yT"][:, 2048:2112].T.reshape(16, 4, D) for c in range(n)], 0)
    gla_prompt = np.stack([R[c]["gla_p"] for c in range(n)], 0)[None]
    gla_sample = np.concatenate([R[c]["gla_s"] for c in range(n)], 0)[None]
    swa_k_prompt = np.stack([R[c]["swa_kT_p"].transpose(2, 0, 1) for c in range(n)], 0)
    swa_v_prompt = np.stack([R[c]["swa_v_p"].reshape(128, 3, 64) for c in range(n)], 0)
    swa_k_sample = np.concatenate([R[c]["swa_kT_s"].transpose(0, 3, 1, 2) for c in range(n)], 0)
    swa_v_sample = np.concatenate([R[c]["swa_v_s"].reshape(16, 128, 3, 64) for c in range(n)], 0)
    mem_k_prompt = np.stack([R[c]["mem_k_p"].reshape(2, 256, 4, 64) for c in range(n)], 1)
    mem_v_prompt = np.stack([R[c]["mem_v_p"].reshape(2, 256, 4, 64) for c in range(n)], 1)
    outs = (y_prompt, y_sample, gla_prompt, gla_sample, swa_k_prompt, swa_v_prompt,
            swa_k_sample, swa_v_sample, mem_k_prompt, mem_v_prompt)
    return tuple(np.ascontiguousarray(o, dtype=np.float32) for o in outs)


def kernel(**inputs):
    in_maps = prep(**inputs)
    nc = build_nc()
    res = run_bass_kernel_spmd(nc, in_maps, core_ids=list(range(8)))
    return assemble(res.results)
```

```python
import math
from bisect import bisect_left
from contextlib import ExitStack

import numpy as np
import concourse.bass as bass
import concourse.mybir as mybir
from concourse.bass_utils import run_bass_kernel_spmd

F32 = mybir.dt.float32
BF16 = mybir.dt.bfloat16
ALU = mybir.AluOpType
AF = mybir.ActivationFunctionType

D = 1024
FF = 2816
NCH = 22
EPS = 1e-6
NPP = 1024
NSM = 64
DBG = {}


class Op:
    __slots__ = ("idx", "eng", "fn", "deps", "fdeps", "dma", "signal", "sigval", "sem", "semval")


class Ent:
    __slots__ = ("w", "r")

    def __init__(self):
        self.w = None
        self.r = {}


class Space:
    def __init__(self):
        self.b = [0, 1 << 40]
        self.e = [Ent()]

    def split(self, x):
        i = bisect_left(self.b, x)
        if self.b[i] == x:
            return i
        old = self.e[i - 1]
        new = Ent()
        new.w = old.w
        new.r = dict(old.r)
        self.b.insert(i, x)
        self.e.insert(i, new)
        return i

    def segs(self, a, b):
        i = self.split(a)
        j = self.split(b)
        return self.e[i:j]


def ap_ranges(ap):
    dims = ap.ap
    esz = mybir.dt.size(ap.dtype)
    pstride = dims[0][0]
    off = int(ap.offset)
    if pstride:
        off = off % pstride
    free = [(s, n) for (s, n) in dims[1:] if n > 1 and s != 0]
    free.sort(key=lambda x: -x[0])
    run = 1
    while free and free[-1][0] == run:
        run *= free[-1][1]
        free.pop()
    starts = [off]
    for s, n in free:
        starts = [st + i * s for st in starts for i in range(n)]
        if len(starts) > 64:
            break
    if len(starts) > 64:
        hi = off + sum(s * (n - 1) for (s, n) in dims[1:] if n > 1 and s > 0) + 1
        return [(off * esz, hi * esz)]
    return [(st * esz, (st + run) * esz) for st in starts]


class Prog:
    ENGS = ("pe", "act", "dve", "pool", "sp")
    NS = 8

    def __init__(self, nc):
        self.nc = nc
        self.ops = []
        self.spaces = {}
        self.onchip = set()
        self.pe_recent = []

    def add(self, eng, fn, reads=(), writes=(), dma=False, pe_rows=None):
        o = Op()
        o.idx = len(self.ops)
        o.eng = eng
        o.fn = fn
        o.dma = dma
        o.signal = False
        o.sigval = 0
        o.sem = None
        o.semval = 0
        o.fdeps = []
        deps = {}
        rkey = ("dma", o.idx) if dma else eng
        acc = []
        for ap in reads:
            if ap is None or ap.name not in self.onchip:
                continue
            acc.append((ap, ap.name == "psum"))
        for ap in writes:
            if ap is None or ap.name not in self.onchip:
                continue
            acc.append((ap, True))
        banks = set()
        for ap, wr in acc:
            sp = self.spaces.setdefault(ap.name, Space())
            rngs = ap_ranges(ap)
            if ap.name == "psum":
                lo = min(r[0] for r in rngs) // 2048
                hi = (max(r[1] for r in rngs) + 2047) // 2048
                rngs = [(lo * 2048, hi * 2048)]
                if wr and eng == "pe":
                    banks.update(range(lo, hi))
            for (a_, b_) in rngs:
                for e in sp.segs(a_, b_):
                    if e.w is not None:
                        deps[e.w.idx] = e.w
                    if wr:
                        for r in e.r.values():
                            deps[r.idx] = r
                        e.w = o
                        e.r = {}
                    else:
                        e.r[rkey] = o
        deps.pop(o.idx, None)
        red = {}
        for d in deps.values():
            if d.dma:
                red[("dma", d.idx)] = d
            else:
                cur = red.get(d.eng)
                if cur is None or cur.idx < d.idx:
                    red[d.eng] = d
        o.deps = list(red.values())
        if eng == "pe" and pe_rows is not None:
            for (p, prow, pbanks) in reversed(self.pe_recent):
                if not (prow[1] <= pe_rows[0] or pe_rows[1] <= prow[0]):
                    break
                if pbanks & banks:
                    o.fdeps.append(p)
                    break
            self.pe_recent.append((o, pe_rows, banks))
            if len(self.pe_recent) > 32:
                self.pe_recent.pop(0)
        self.ops.append(o)
        return o

    def emit(self, stack):
        nc = self.nc
        ndma = {e: 0 for e in self.ENGS}
        for o in self.ops:
            for d in o.deps:
                if not d.dma:
                    if d.eng == "pe" and o.eng == "pe" and not o.dma:
                        continue
                    d.signal = True
            for d in o.fdeps:
                d.signal = True
        cnt = {e: 0 for e in self.ENGS}
        for o in self.ops:
            if o.dma:
                n = ndma[o.eng]
                ndma[o.eng] = n + 1
                o.sem = (o.eng, n % self.NS)
                o.semval = 16 * (n // self.NS + 1)
            elif o.signal:
                cnt[o.eng] += 1
                o.sigval = cnt[o.eng]
        if DBG.get("KDEBUG"):
            print("SEMCOUNTS", cnt, {e: 16 * (v // self.NS + 1) for e, v in ndma.items()}, flush=True)
        sems = {}
        for e in self.ENGS:
            sems[e] = stack.enter_context(nc.semaphore("s_" + e))
        dsems = {}
        for e in self.ENGS:
            for i in range(min(self.NS, ndma[e])):
                dsems[(e, i)] = stack.enter_context(nc.semaphore("d_%s_%d" % (e, i)))
        block = stack.enter_context(nc.Block())
        by_eng = {e: [o for o in self.ops if o.eng == e] for e in self.ENGS}

        def run(engname, eng):
            waited = {}
            for o in by_eng[engname]:
                for d in o.deps:
                    if d.dma:
                        k = d.sem
                        v = d.semval
                        s = dsems[k]
                    else:
                        if d.eng == "pe" and engname == "pe" and not o.dma:
                            continue
                        k = d.eng
                        v = d.sigval
                        s = sems[k]
                    if waited.get(k, 0) < v:
                        eng.wait_ge(s, v)
                        waited[k] = v
                for d in o.fdeps:
                    if waited.get(d.eng, 0) < d.sigval:
                        eng.wait_ge(sems[d.eng], d.sigval)
                        waited[d.eng] = d.sigval
                if o.dma:
                    if o.semval > 16 and waited.get(o.sem, 0) < o.semval - 16:
                        eng.wait_ge(dsems[o.sem], o.semval - 16)
                        waited[o.sem] = o.semval - 16
                    o.fn(eng).then_inc(dsems[o.sem], 16)
                else:
                    ins = o.fn(eng)
                    if o.signal:
                        ins.then_inc(sems[engname], 1)
            last = {}
            for o in by_eng[engname]:
                if o.dma:
                    last[o.sem] = max(last.get(o.sem, 0), o.semval)
            for k, v in last.items():
                if waited.get(k, 0) < v:
                    eng.wait_ge(dsems[k], v)

        @block.tensor
        def _(e):
            run("pe", e)

        @block.scalar
        def _(e):
            run("act", e)

        @block.vector
        def _(e):
            run("dve", e)

        @block.gpsimd
        def _(e):
            run("pool", e)

        @block.sync
        def _(e):
            run("sp", e)


C_TRIU = 0
C_TRIG = 128
C_SU = 256
C_SG = 320
C_OH = 384
C_MC = 400
C_MN = 404
C_PM = 468
C_W = 596


def make_consts():
    c = np.zeros((128, C_W), np.float32)
    p = np.arange(128)[:, None]
    i = np.arange(128)[None, :]
    c[:, C_TRIU:C_TRIU + 128] = (p <= i)
    c[:, C_TRIG:C_TRIG + 128] = (p > i)
    p6 = np.arange(64)[:, None]
    i6 = np.arange(64)[None, :]
    same = (p6 // 4) == (i6 // 4)
    c[:64, C_SU:C_SU + 64] = same & (p6 <= i6)
    c[:64, C_SG:C_SG + 64] = same & (p6 > i6)
    c[:64, C_OH:C_OH + 16] = (p6 // 4) == np.arange(16)[None, :]
    c[:, C_MC:C_MC + 4] = (p >= (np.arange(4)[None, :] + 1))
    mn = np.zeros((64, 16, 4), np.float32)
    for pp in range(64):
        for t in range(4):
            if pp % 4 <= t:
                mn[pp, pp // 4, t] = 1.0
    c[:64, C_MN:C_MN + 64] = mn.reshape(64, 64)
    part = rope_perm_cols(128)
    for d in range(128):
        c[part[d], C_PM + d] = 1.0
    return c


def make_rope():
    rot = 16
    half = 8
    inv_freq = np.exp(-math.log(500000.0) * np.arange(0, rot, 2, dtype=np.float32) / rot).astype(np.float32)
    out = np.zeros((2, 2, 128, NPP + NSM), np.float32)
    for ps in range(2):
        pos = np.concatenate([np.arange(ps * NPP, (ps + 1) * NPP, dtype=np.float32),
                              np.tile(8192.0 + np.arange(4, dtype=np.float32), 16)]).astype(np.float32)
        ang = (pos[:, None] * inv_freq[None, :]).astype(np.float32)
        cs = np.cos(ang).astype(np.float32).T
        sn = np.sin(ang).astype(np.float32).T
        cosr = np.ones((64, pos.shape[0]), np.float32)
        sinr = np.zeros((64, pos.shape[0]), np.float32)
        cosr[0:half] = cs
        cosr[half:rot] = cs
        sinr[0:half] = -sn
        sinr[half:rot] = sn
        out[ps, 0] = np.concatenate([cosr, cosr], 0)
        out[ps, 1] = np.concatenate([sinr, sinr], 0)
    return out


def rope_perm_cols(ncols):
    idx = np.arange(ncols)
    d = idx % 64
    partner = np.where(d < 8, idx + 8, np.where(d < 16, idx - 8, idx))
    return partner


def build_nc():
    nc = bass.Bass("TRN2", target_bir_lowering=False)
    P = Prog(nc)

    def din(name, shape):
        return nc.dram_tensor(name, list(shape), F32, kind="ExternalInput").ap()

    def dout(name, shape):
        return nc.dram_tensor(name, list(shape), F32, kind="ExternalOutput").ap()

    xT_p = din("xT_p", [D, 2048])
    xT_s = din("xT_s", [D, NSM])
    memT = din("memT", [D, 256])
    state = din("state", [16, 4, 96, 192])
    kcT = din("kcT", [16, 3, 64, 128])
    vc = din("vc", [16, 128, 192])
    cmkT = din("cmkT", [2, 16, 4, 64, 256])
    cmv = din("cmv", [2, 16, 256, 256])
    w_gu = [din("w_gu1", [2, D, 2 * FF]), din("w_gu2", [2, D, 2 * FF])]
    w_dn = [din("w_d1", [2, FF, D]), din("w_d2", [2, FF, D])]
    w_memkv = din("w_memkv", [2, D, 512])
    a_w_in = din("a_w_in", [D, 2576])
    a_w_gate = din("a_w_gate", [16, 384])
    a_b_gate = din("a_b_gate", [1, 384])
    a_w_out = din("a_w_out", [D, D])
    w_kvx = din("w_kvx", [D, 960])
    b_w_inx = din("b_w_inx", [D, 1792])
    b_w_out = din("b_w_out", [D, D])
    gains_d = din("gains", [128, 10, 8])
    gout_d = din("gout", [128, 2])
    sinks_d = din("sinks", [128, 12])
    consts_d = din("consts", [128, C_W])
    rope_d = din("rope", [2, 2, 128, NPP + NSM])

    yT_o = dout("yT", [D, 2048 + NSM])
    gla_p_o = dout("gla_p", [4, 96, 192])
    gla_s_o = dout("gla_s", [16, 4, 96, 192])
    swa_kT_p_o = dout("swa_kT_p", [3, 64, 128])
    swa_v_p_o = dout("swa_v_p", [128, 192])
    swa_kT_s_o = dout("swa_kT_s", [16, 3, 64, 128])
    swa_v_s_o = dout("swa_v_s", [16, 128, 192])
    mem_k_o = dout("mem_k_p", [2, 256, 256])
    mem_v_o = dout("mem_v_p", [2, 256, 256])

    with ExitStack() as st, nc.allow_low_precision("bf16 matmul operands"), \
            nc.allow_non_contiguous_dma("strided weight/cache tiles"):

        def sbt(name, shape, dt):
            t = st.enter_context(nc.sbuf_tensor("sb_" + name, list(shape), dt))
            P.onchip.add("sb_" + name)
            return t

        NT = NPP + NSM
        xT = sbt("xT", [128, 8, NT], F32)
        gsc = sbt("gsc", [128, 10, 8], F32)
        gout = sbt("gout", [128, 2], F32)
        esink = sbt("esink", [128, 12], F32)
        cst = sbt("cst", [128, C_W], F32)
        ones_bf = sbt("ones_bf", [128, 128], BF16)
        ones_f = sbt("ones_f", [1, 128], F32)
        epsD = sbt("epsD", [128, 2], F32)
        wgate = sbt("wgate", [16, 384], F32)
        bgate = sbt("bgate", [1, 384], F32)
        cosT = sbt("cosT", [128, NT], F32)
        sinT = sbt("sinT", [128, NT], F32)
        mkT = sbt("mkT", [128, 2, 2, 256], BF16)
        mv = sbt("mv", [128, 2, 2, 256], BF16)
        S = sbt("S", [96, 4, 192], F32)
        S_bf = sbt("S_bf", [96, 4, 192], BF16)
        kTd = sbt("kTd", [128, 3, NT], BF16)
        vtokB = sbt("vtokB", [128, 9, 192], BF16)
        kT_prev = sbt("kT_prev", [128, 3, 128], BF16)
        v_prev = sbt("v_prev", [128, 192], BF16)
        perm_bf = sbt("perm_bf", [128, 128], BF16)
        kf_p = sbt("kf_p", [64, 3, 128], F32)
        vf_p = sbt("vf_p", [128, 192], F32)
        ARENA_BYTES = (nc.sbuf_bytes_remaining - 2048) // 64 * 64
        arena = sbt("arena", [128, ARENA_BYTES // 4], F32)
        psum = st.enter_context(nc.psum_tensor("psum", [128, 4096], F32))
        P.onchip.add("psum")

        state_ = {"aoff": 0, "bank": 0}

        tmpc = {}

        def arena_reset():
            state_["aoff"] = 0
            state_["atop"] = ARENA_BYTES
            tmpc.clear()

        def amark():
            return state_["aoff"]

        def apop(m):
            state_["aoff"] = m
            for k in [k for k, v in tmpc.items() if v[2] >= m]:
                del tmpc[k]

        def TM(key, shape, dt, nbuf=2):
            ent = tmpc.get(key)
            if ent is None:
                off = state_["aoff"]
                ent = tmpc[key] = [[A(shape, dt) for _ in range(nbuf)], 0, off]
            ent[1] += 1
            return ent[0][(ent[1] - 1) % nbuf]

        def A(shape, dt, top=False, at=None):
            esz = mybir.dt.size(dt)
            nfree = 1
            for s in shape[1:]:
                nfree *= s
            nbytes = (nfree * esz + 63) // 64 * 64
            if at is not None:
                off = at
            elif top:
                state_["atop"] -= nbytes
                off = state_["atop"]
            else:
                off = state_["aoff"]
                state_["aoff"] = off + nbytes
            assert state_["aoff"] <= state_["atop"], ("arena overflow", state_["aoff"], state_["atop"])
            state_["hw"] = max(state_.get("hw", 0), state_["aoff"])
            ap = arena[:, off // 4:(off + nbytes) // 4]
            if dt != F32:
                ap = ap.bitcast(dt)
            ap = ap[:, 0:nfree]
            if len(shape) == 3:
                ap = ap.rearrange("p (a b) -> p a b", a=shape[1])
            elif len(shape) == 4:
                ap = ap.rearrange("p (a b c) -> p a b c", a=shape[1], b=shape[2])
            if shape[0] < 128:
                ap = ap[0:shape[0]]
            return ap

        def PS(nb=1):
            b = state_["bank"]
            if nb == 2 and b % 2 == 1:
                b += 1
            if b + nb > 8:
                b = 0
            state_["bank"] = (b + nb) % 8
            return psum[:, b * 512:(b + nb) * 512]

        def mm(out, lhsT, rhs, start=True, stop=True):
            r0 = lhsT.base_partition()
            P.add("pe", lambda e: e.matmul(out, lhsT=lhsT, rhs=rhs, start=start, stop=stop),
                  reads=[lhsT, rhs], writes=[out], pe_rows=(r0, r0 + lhsT.shape[0]))

        def act(out, in_, func, bias=None, scale=1.0):
            if bias is None:
                P.add("act", lambda e: e.activation(out=out, in_=in_, func=func, scale=scale),
                      reads=[in_], writes=[out])
            elif isinstance(bias, float):
                P.add("act", lambda e: e.activation(out=out, in_=in_, func=func, bias=bias, scale=scale),
                      reads=[in_], writes=[out])
            else:
                P.add("act", lambda e: e.activation(out=out, in_=in_, func=func, bias=bias, scale=scale),
                      reads=[in_, bias], writes=[out])

        def tt(eng, out, in0, in1, op):
            P.add(eng, lambda e: e.tensor_tensor(out=out, in0=in0, in1=in1, op=op), reads=[in0, in1], writes=[out])

        def ts(eng, out, in0, s1, op0, s2=None, op1=None):
            rd = [in0] + [s for s in (s1, s2) if s is not None and not isinstance(s, (int, float))]
            if op1 is None:
                P.add(eng, lambda e: e.tensor_scalar(out=out, in0=in0, scalar1=s1, scalar2=None, op0=op0),
                      reads=rd, writes=[out])
            else:
                P.add(eng, lambda e: e.tensor_scalar(out=out, in0=in0, scalar1=s1, scalar2=s2, op0=op0, op1=op1),
                      reads=rd, writes=[out])

        def stt(eng, out, in0, scalar, in1, op0, op1):
            rd = [in0, in1] + ([] if isinstance(scalar, (int, float)) else [scalar])
            P.add(eng, lambda e: e.scalar_tensor_tensor(out=out, in0=in0, scalar=scalar, in1=in1, op0=op0, op1=op1),
                  reads=rd, writes=[out])

        def cp(eng, out, in_):
            if eng == "act":
                P.add("act", lambda e: e.copy(out=out, in_=in_), reads=[in_], writes=[out])
            else:
                P.add(eng, lambda e: e.tensor_copy(out=out, in_=in_), reads=[in_], writes=[out])

        def recip(out, in_):
            P.add("dve", lambda e: e.reciprocal(out=out, in_=in_), reads=[in_], writes=[out])

        def act_recip(out, in_):
            act(out, in_, AF.Ln)
            act(out, out, AF.Exp, scale=-1.0)

        def act_rsqrt(out, in_, eps_ap):
            act(out, in_, AF.Ln, bias=eps_ap)
            act(out, out, AF.Exp, scale=-0.5)

        def memset(eng, out, val):
            P.add(eng, lambda e: e.memset(out, val), writes=[out])

        def dma(eng, out, in_):
            P.add(eng, lambda e: e.dma_start(out=out, in_=in_), reads=[in_], writes=[out], dma=True)

        def kp(ap):
            return ap.rearrange("(k p) n -> p k n", p=128)

        MUL = ALU.mult
        ADD = ALU.add

        dma("sp", gsc[:], gains_d)
        dma("sp", gout[:], gout_d)
        dma("sp", esink[:], sinks_d)
        dma("sp", cst[:], consts_d)
        dma("sp", wgate[:], a_w_gate)
        dma("sp", bgate[:], a_b_gate)
        memset("dve", ones_bf[:], 1.0)
        memset("dve", ones_f[:], 1.0)
        memset("dve", epsD[:, 0:1], D * EPS)
        memset("dve", epsD[:, 1:2], 192 * EPS)
        memset("dve", S[:], 0.0)
        memset("dve", S_bf[:], 0.0)
        ts("dve", gsc[:], gsc[:], float(math.sqrt(D)), MUL)
        ts("dve", gout[:], gout[:], float(math.sqrt(192.0)), MUL)
        act(esink[:], esink[:], AF.Exp)
        cp("dve", perm_bf[:], cst[:, C_PM:C_PM + 128])

        triU = cst[:, C_TRIU:C_TRIU + 128]
        triG = cst[:, C_TRIG:C_TRIG + 128]
        sU = cst[0:64, C_SU:C_SU + 64]
        sG = cst[0:64, C_SG:C_SG + 64]
        onehot = cst[0:64, C_OH:C_OH + 16]
        maskC = cst[:, C_MC:C_MC + 4]
        maskN = cst[0:64, C_MN:C_MN + 64]

        def norm(src, n, gi, dst, sq):
            act(sq[:, :, 0:n], src, AF.Square)
            pp = PS()
            for k in range(8):
                mm(pp[:, 0:n], ones_bf[:], sq[:, k, 0:n], start=(k == 0), stop=(k == 7))
            sd = TM("nsd", [128, 512], F32)
            act_rsqrt(sd[:, 0:n], pp[:, 0:n], epsD[:, 0:1])
            for k in range(8):
                stt("dve", dst[:, k, 0:n], src[:, k, :], gsc[:, gi, k:k + 1], sd[:, 0:n], MUL, MUL)

        def load_pass_inputs(ps_, part=None):
            for pi, (a0, a1) in enumerate(((0, 512), (512, NPP))):
                if part is None or part == pi:
                    dma("sp", xT[:, :, a0:a1], kp(xT_p[:, ps_ * NPP + a0:ps_ * NPP + a1]))
            if part == 0:
                return
            if ps_ == 1:
                dma("sp", xT[:, :, NPP:NT], kp(xT_s))
            dma("sp", cosT[:], rope_d[ps_, 0])
            dma("sp", sinT[:], rope_d[ps_, 1])

        load_pass_inputs(0)

        arena_reset()
        memx = A([128, 8, 256], F32)
        sqm = A([128, 8, 256], BF16)
        hm = A([128, 8, 256], BF16)
        Wm = A([128, 8, 512], BF16)
        dma("sp", memx, kp(memT))
        for l in range(2):
            dma("pool", Wm, kp(w_memkv[l]))
            norm(memx, 256, 6 + l, hm, sqm)
            for jt in range(2):
                pp = PS()
                for k in range(8):
                    mm(pp[:, 0:512], hm[:, k, jt * 128:(jt + 1) * 128], Wm[:, k, :], start=(k == 0), stop=(k == 7))
                stg = TM("stg", [128, 512], F32)
                cp("act", stg, pp[:, 0:512])
                dma("sp", mem_k_o[l, jt * 128:(jt + 1) * 128, :], stg[:, 0:256])
                dma("sp", mem_v_o[l, jt * 128:(jt + 1) * 128, :], stg[:, 256:512])
                cp("dve", mv[:, l, jt, :], stg[:, 256:512])
            for c in range(2):
                pp = PS()
                for k in range(8):
                    mm(pp[:, 0:256], Wm[:, k, c * 128:(c + 1) * 128], hm[:, k, :], start=(k == 0), stop=(k == 7))
                cp("act", mkT[:, l, c, :], pp[:, 0:256])

        prepped = {}

        def ffn(l, which, tgs, ntp, hook=None):
            if DBG.get("KSKIPFFN"):
                return
            arena_reset()
            gi = (0 if which == 0 else 4) + l
            wgu = w_gu[which][l]
            wd = w_dn[which][l]
            hT = A([128, 8, ntp], BF16)
            sg = [A([128, 512], F32) for _ in range(2)]
            sq = A([128, 8, 512], BF16)
            for (c0, n) in tgs:
                norm(xT[:, :, c0:c0 + n], n, gi, hT[:, :, c0:c0 + n], sq)
            state_["aoff_lz"] = state_["aoff"]
            wbuf = [A([128, 8, 512], BF16, top=True) for _ in range(3)]
            Wds = [A([128, 12, D], BF16, top=True) for _ in range(2)]
            actT = A([128, 12, ntp], BF16)
            cnt = 0
            ui = 0
            halves = ((0, 10), (10, 12))

            def load_wd(hi):
                f0_, nf_ = halves[hi]
                dma("pool", Wds[hi][:, 0:nf_, :], wd[f0_ * 128:(f0_ + nf_) * 128, :].rearrange("(c p) n -> p c n", p=128))

            def load_unit(fc0_, buf_):
                dma("pool", buf_[:, :, 0:256], kp(wgu[:, fc0_ * 128:fc0_ * 128 + 256]))
                dma("pool", buf_[:, :, 256:512], kp(wgu[:, FF + fc0_ * 128:FF + fc0_ * 128 + 256]))

            units = [2 * u for u in range(11)]
            load_unit(units[0], wbuf[0])
            load_unit(units[1], wbuf[1])
            load_wd(0)
            nload = 2
            for hi, (f0, nf) in enumerate(halves):
                Wd = Wds[hi]
                for u in range(nf // 2):
                    fc0 = f0 + 2 * u
                    buf = wbuf[ui % 3]
                    ui += 1
                    if nload < len(units):
                        load_unit(units[nload], wbuf[nload % 3])
                        nload += 1
                    if hi == 0 and u == 1:
                        load_wd(1)
                    for j in range(2):
                        fi = 2 * u + j
                        for (c0, n) in tgs:
                            pg = PS()
                            pu = PS()
                            for k in range(8):
                                mm(pg[:, 0:n], buf[:, k, j * 128:(j + 1) * 128], hT[:, k, c0:c0 + n],
                                   start=(k == 0), stop=(k == 7))
                            for k in range(8):
                                mm(pu[:, 0:n], buf[:, k, 256 + j * 128:256 + (j + 1) * 128], hT[:, k, c0:c0 + n],
                                   start=(k == 0), stop=(k == 7))
                            s = sg[cnt % 2]
                            cnt += 1
                            act(s[:, 0:n], pg[:, 0:n], AF.Silu)
                            tt("dve", actT[:, fi, c0:c0 + n], s[:, 0:n], pu[:, 0:n], MUL)
                for ti, (c0, n) in enumerate(tgs):
                    for dc in range(8):
                        po = PS()
                        for fi in range(nf):
                            mm(po[:, 0:n], Wd[:, fi, dc * 128:(dc + 1) * 128], actT[:, fi, c0:c0 + n],
                               start=(fi == 0), stop=(fi == nf - 1))
                        stt("dve", xT[:, dc, c0:c0 + n], po[:, 0:n], 0.5, xT[:, dc, c0:c0 + n], MUL, ADD)
                        if hook is not None and hi == 1 and ti == 1 and dc == 0:
                            lz = state_["aoff_lz"]
                            hook()
                            assert state_["aoff"] <= lz, ("landing zone overflow", state_["aoff"], lz)

        def phase_entry(key, mode, c0, n, gi, wspecs):
            if mode == "body":
                d = prepped.pop(key)
                state_["aoff"] = d["aoff"]
                state_["atop"] = ARENA_BYTES
                tmpc.clear()
                return d["hT"], d["W"]
            arena_reset()
            W = []
            if mode == "all":
                for (shape, src) in wspecs:
                    w = A(shape, BF16)
                    dma("pool", w, src)
                    W.append(w)
            hT = A([128, 8, n], BF16)
            mk_ = amark()
            sq = A([128, 8, n], BF16)
            norm(xT[:, :, c0:c0 + n], n, gi, hT, sq)
            apop(mk_)
            if mode == "prep":
                for (shape, src) in wspecs:
                    w = A(shape, BF16)
                    dma("pool", w, src)
                    W.append(w)
                prepped[key] = dict(hT=hT, W=W, aoff=state_["aoff"])
                return None, None
            return hT, W

        def mem_attn_prompt(l, mqT, n, mix, slot0):
            def BK(b_):
                return psum[:, b_ * 512:(b_ + 1) * 512]
            for pr_ in range(2):
                pts = {}
                for jt in range(2):
                    for mi in range(2):
                        m = 2 * pr_ + mi
                        c = m // 2
                        p0 = (m % 2) * 64
                        sp_ = BK(2 * mi + jt)
                        mm(sp_[:, 0:n], mkT[p0:p0 + 64, l, c, jt * 128:(jt + 1) * 128], mqT[p0:p0 + 64, c, 0:n])
                for mi in range(2):
                    for jt in range(2):
                        pt = TM("mpt", [128, 512], BF16, 4)
                        act(pt[:, 0:n], BK(2 * mi + jt)[:, 0:n], AF.Exp, scale=0.125)
                        pts[(mi, jt)] = pt
                for mi in range(2):
                    m = 2 * pr_ + mi
                    p0 = (m % 2) * 64
                    po = BK(4 + 2 * mi)
                    pr = BK(5 + 2 * mi)
                    for jt in range(2):
                        mm(po[p0:p0 + 64, 0:n], mv[:, l, jt, m * 64:(m + 1) * 64], pts[(mi, jt)][:, 0:n], start=(jt == 0), stop=(jt == 1))
                    for jt in range(2):
                        mm(pr[p0:p0 + 64, 0:n], ones_bf[:, 0:64], pts[(mi, jt)][:, 0:n], start=(jt == 0), stop=(jt == 1))
                for mi in range(2):
                    m = 2 * pr_ + mi
                    p0 = (m % 2) * 64
                    rinv = TM("mrinv", [128, 512], F32)
                    act_recip(rinv[p0:p0 + 64, 0:n], BK(5 + 2 * mi)[p0:p0 + 64, 0:n])
                    tt("dve", mix[p0:p0 + 64, slot0 + m // 2, 0:n], BK(4 + 2 * mi)[p0:p0 + 64, 0:n], rinv[p0:p0 + 64, 0:n], MUL)

        def load_mem_cache(l, hb, ck, cv):
            dma("pool", ck, cmkT[l, 8 * hb:8 * hb + 8].rearrange("b (c two) d j -> (two d) b c j", two=2))
            dma("pool", cv, cmv[l, 8 * hb:8 * hb + 8].rearrange("b (jt p) f -> p b jt f", p=128))

        def mem_attn_sample(l, mqT, mix, slot0, pre=None):
            for hb in range(2):
                if pre is not None:
                    ck, cv = pre[hb]
                else:
                    ck = TM("ck", [128, 8, 2, 256], BF16, 1)
                    cv = TM("cv", [128, 8, 2, 256], BF16, 1)
                    load_mem_cache(l, hb, ck, cv)
                sp_ = PS()
                for (m, b) in [(m_, b_) for m_ in (0, 2, 1, 3) for b_ in range(8)]:
                    bg = 8 * hb + b
                    c = m // 2
                    p0 = (m % 2) * 64
                    for jt in range(2):
                        col = jt * 128 + (b * 4 + m) * 4
                        mm(sp_[:, col:col + 4], ck[p0:p0 + 64, b, c, jt * 128:(jt + 1) * 128],
                           mqT[p0:p0 + 64, c, 4 * bg:4 * bg + 4])
                pt = TM("spt", [128, 256], BF16)
                act(pt, sp_[:, 0:256], AF.Exp, scale=0.125)
                po = PS()
                pr = PS()
                for b in range(8):
                    for m in range(4):
                        col = (b * 4 + m) * 4
                        p0 = (m % 2) * 64
                        for jt in range(2):
                            mm(po[p0:p0 + 64, col:col + 4], cv[:, b, jt, m * 64:(m + 1) * 64],
                               pt[:, jt * 128 + col:jt * 128 + col + 4], start=(jt == 0), stop=(jt == 1))
                for jt in range(2):
                    mm(pr[:, 0:128], ones_bf[:, :], pt[:, jt * 128:(jt + 1) * 128], start=(jt == 0), stop=(jt == 1))
                rinv = TM("srinv", [128, 128], F32)
                act_recip(rinv, pr[:, 0:128])
                for m in range(4):
                    p0 = (m % 2) * 64
                    ov = po[p0:p0 + 64, 0:128].rearrange("p (b m t) -> p b m t", b=8, m=4)[:, :, m, :]
                    rv = rinv[p0:p0 + 64, :].rearrange("p (b m t) -> p b m t", b=8, m=4)[:, :, m, :]
                    outv = mix[p0:p0 + 64, slot0 + m // 2, 32 * hb:32 * hb + 32].rearrange("p (b t) -> p b t", b=8)
                    tt("dve", outv, ov, rv, MUL)

        def mixer_a(c0, n, sample, mode="all", cache_copy=False):
            T = 64 if sample else 128
            ntile = 1 if sample else n // 128
            tmask = sU if sample else triU
            gmask = sG if sample else triG
            wspecs = [([128, 8, 272], kp(a_w_in[:, 2304:2576])), ([128, 8, 768], kp(a_w_in[:, 768:1536]))]
            if mode == "all":
                wspecs.append(([128, 8, 768], kp(a_w_in[:, 0:768])))
            hT, W = phase_entry("ma", mode, c0, n, 2, wspecs)
            if mode == "prep":
                return
            Wgm = W[0]
            if len(W) < 3:
                w1 = A([128, 8, 768], BF16)
                dma("pool", w1, kp(a_w_in[:, 0:768]))
                W.append(w1)
            Wa = [W[2], W[1]]
            Shs = {}
            if sample:
                ShT = [A([96, 16, 192], F32, top=True) for _ in range(2)]
                ShbT = [A([96, 16, 192], BF16, top=True) for _ in range(2)]

                def load_state(h_):
                    dma("sp", ShT[h_ % 2], state[:, h_].rearrange("b d v -> d b v"))
                    dma("pool", ShbT[h_ % 2], state[:, h_].rearrange("b d v -> d b v"))
                    Shs[h_] = (ShT[h_ % 2], ShbT[h_ % 2])

                load_state(0)
                load_state(1)
            if cache_copy:
                stgT = A([64, 4, 3, 124], F32, top=True)
                stvT = A([124, 4, 192], F32, top=True)
                for hb_ in range(4):
                    bs_ = slice(4 * hb_, 4 * hb_ + 4)
                    dma("sp", stgT, kcT[bs_, :, :, 4:128].rearrange("b h d j -> d b h j"))
                    dma("sp", stvT, vc[bs_, 4:128, :].rearrange("b j f -> j b f"))
                    dma("sp", swa_kT_s_o[bs_, :, :, 0:124].rearrange("b h d j -> d b h j"), stgT)
                    dma("sp", swa_v_s_o[bs_, 0:124, :].rearrange("b j f -> j b f"), stvT)
            qs = A([96, 4, n], BF16)
            ks = A([96, 4, n], BF16)
            khat = A([128, ntile, 384], BF16)
            vtok = A([128, ntile, 768], BF16)
            sr = A([128, 6, n], BF16)
            mix = A([128, 8, n], BF16)
            mqT = A([128, 2, n], BF16)
            ebl = A([96, 4, 64], F32)
            mk_ = amark()
            glr = A([16, n], F32)
            lp = A([128, ntile, 384], F32)
            ekr = A([128, ntile, 384], F32)
            pp = PS()
            for k in range(8):
                mm(pp[0:16, 0:n], Wgm[:, k, 0:16], hT[:, k, :], start=(k == 0), stop=(k == 7))
            cp("act", glr, pp[0:16, 0:n])
            for c in range(2):
                pp = PS()
                for k in range(8):
                    mm(pp[:, 0:n], Wgm[:, k, 16 + c * 128:16 + (c + 1) * 128], hT[:, k, :], start=(k == 0), stop=(k == 7))
                cp("act", mqT[:, c, :], pp[:, 0:n])
            for t in range(ntile):
                tc0 = t * 128
                pz = PS()
                mm(pz[0:T, 0:384], glr[:, tc0:tc0 + T], wgate[:], start=True, stop=False)
                mm(pz[0:T, 0:384], ones_f[0:1, 0:T], bgate[:], start=False, stop=True)
                e1 = TM("e1", [128, 384], F32)
                act(e1[0:T], pz[0:T, 0:384], AF.Exp, scale=-1.0)
                act(lp[0:T, t, :], e1[0:T], AF.Ln, bias=1.0)
                pr = PS()
                mm(pr[0:T, 0:384], gmask[0:T, 0:T], lp[0:T, t, :])
                act(ekr[0:T, t, :], pr[0:T, 0:384], AF.Exp, scale=-1.0 / 16.0)
            for t in range(ntile):
                tc0 = t * 128
                pa = PS()
                pb = PS()
                for k in range(8):
                    mm(pa[0:T, 0:512], hT[:, k, tc0:tc0 + T], Wa[1][:, k, 0:512], start=(k == 0), stop=(k == 7))
                for k in range(8):
                    mm(pb[0:T, 0:256], hT[:, k, tc0:tc0 + T], Wa[1][:, k, 512:768], start=(k == 0), stop=(k == 7))
                cp("act", vtok[0:T, t, 0:512], pa[0:T, 0:512])
                cp("act", vtok[0:T, t, 512:768], pb[0:T, 0:256])
            dma("pool", Wa[1], kp(a_w_in[:, 1536:2304]))
            for h in range(4):
                ebh = TM("ebh", [96, n], F32)
                enbh = TM("enbh", [96, n], F32)
                for t in range(ntile):
                    tc0 = t * 128
                    pb = PS()
                    mm(pb[0:96, 0:T], lp[0:T, t, 96 * h:96 * h + 96], tmask[0:T, 0:T])
                    act(ebh[:, tc0:tc0 + T], pb[0:96, 0:T], AF.Exp, scale=-1.0 / 16.0)
                    act(enbh[:, tc0:tc0 + T], pb[0:96, 0:T], AF.Exp, scale=1.0 / 16.0)
                if sample:
                    cp("dve", ebl[:, h, :], ebh[:, 0:64])
                else:
                    cp("dve", ebl[:, h, 0:ntile], ebh.rearrange("p (t i) -> p t i", i=128)[:, :, 127])
                pq = PS()
                for k in range(8):
                    mm(pq[0:96, 0:n], Wa[0][:, k, 96 * h:96 * h + 96], hT[:, k, :], start=(k == 0), stop=(k == 7))
                stt("dve", qs[:, h, :], pq[0:96, 0:n], float(96.0 ** -0.5), ebh, MUL, MUL)
                pk = PS()
                for k in range(8):
                    mm(pk[0:96, 0:n], Wa[0][:, k, 384 + 96 * h:384 + 96 * h + 96], hT[:, k, :], start=(k == 0), stop=(k == 7))
                tt("dve", ks[:, h, :], pk[0:96, 0:n], enbh, MUL)
            for t in range(ntile):
                tc0 = t * 128
                pk = PS()
                for k in range(8):
                    mm(pk[0:T, 0:384], hT[:, k, tc0:tc0 + T], Wa[0][:, k, 384:768], start=(k == 0), stop=(k == 7))
                tt("dve", khat[0:T, t, :], pk[0:T, 0:384], ekr[0:T, t, :], MUL)
            for ch in range(6):
                pa = PS()
                for k in range(8):
                    mm(pa[:, 0:n], Wa[1][:, k, 128 * ch:128 * ch + 128], hT[:, k, :], start=(k == 0), stop=(k == 7))
                act(sr[:, ch, :], pa[:, 0:n], AF.Silu)
            apop(mk_)
            Wo = A([128, 8, D], BF16, at=0)
            for h in range(4):
                p0 = (h % 2) * 64
                dma("pool", Wo[:, h, :], a_w_out[192 * h:192 * h + 128, :])
                dma("pool", Wo[p0:p0 + 64, 4 + h // 2, :], a_w_out[192 * h + 128:192 * h + 192, :])
            dma("pool", Wo[:, 6:8, :], a_w_out[768:1024, :].rearrange("(s p) n -> p s n", p=128))

            def BK(b_):
                return psum[:, b_ * 512:(b_ + 1) * 512]

            def gate_store(h, tc0, oA, oB, rsA, rsB):
                p0 = (h % 2) * 64
                ta = TM("fta%d" % h, [128, 128], F32)
                tb = TM("ftb%d" % h, [128, 128], F32)
                stt("dve", ta[:, 0:T], oA, gout[:, 0:1], rsA, MUL, MUL)
                tt("pool", mix[:, h, tc0:tc0 + T], ta[:, 0:T], sr[:, h, tc0:tc0 + T], MUL)
                stt("dve", tb[p0:p0 + 64, 0:T], oB, gout[p0:p0 + 64, 1:2], rsB, MUL, MUL)
                tt("pool", mix[p0:p0 + 64, 4 + h // 2, tc0:tc0 + T], tb[p0:p0 + 64, 0:T],
                   sr[p0:p0 + 64, 4 + h // 2, tc0:tc0 + T], MUL)

            if not sample:
                ams = {}
                pob = {}
                sqs = {}

                def S1(t):
                    tc0 = t * 128
                    for h in range(4):
                        mm(BK(0)[:, h * 128:(h + 1) * 128], ks[:, h, tc0:tc0 + 128], qs[:, h, tc0:tc0 + 128])

                def S2(t):
                    for h in range(4):
                        am = TM("am%d" % h, [128, 128], BF16)
                        tt("dve", am, BK(0)[:, h * 128:(h + 1) * 128], triU, MUL)
                        ams[(t, h)] = am

                def S3(t):
                    tc0 = t * 128
                    for h in range(4):
                        p0 = (h % 2) * 64
                        bk = BK(2 + state_["gl"] % 6)
                        state_["gl"] += 1
                        pob[(t, h)] = bk
                        am = ams[(t, h)]
                        mm(bk[:, 0:128], vtok[:, t, 192 * h:192 * h + 128], am, start=True, stop=False)
                        mm(bk[:, 0:128], S_bf[:, h, 0:128], qs[:, h, tc0:tc0 + 128], start=False, stop=True)
                        mm(bk[p0:p0 + 64, 128:256], vtok[:, t, 192 * h + 128:192 * h + 192], am, start=True, stop=False)
                        mm(bk[p0:p0 + 64, 128:256], S_bf[:, h, 128:192], qs[:, h, tc0:tc0 + 128], start=False, stop=True)
                        mm(bk[0:96, 256:448], khat[:, t, 96 * h:96 * h + 96], vtok[:, t, 192 * h:192 * h + 192])

                def S4(t):
                    for h in range(4):
                        p0 = (h % 2) * 64
                        bk = pob[(t, h)]
                        stt("dve", S[:, h, :], S[:, h, :], ebl[:, h, t:t + 1], bk[0:96, 256:448], MUL, ADD)
                        cp("act", S_bf[:, h, :], S[:, h, :])
                        sqa = TM("sqa%d" % h, [128, 128], BF16)
                        sqb = TM("sqb%d" % h, [128, 128], BF16)
                        act(sqa, bk[:, 0:128], AF.Square)
                        act(sqb[p0:p0 + 64, :], bk[p0:p0 + 64, 128:256], AF.Square)
                        sqs[(t, h)] = (sqa, sqb)

                def S5(t):
                    for h in range(4):
                        p0 = (h % 2) * 64
                        sqa, sqb = sqs[(t, h)]
                        mm(BK(1)[:, h * 128:(h + 1) * 128], ones_bf[:, :], sqa, start=True, stop=False)
                        mm(BK(1)[:, h * 128:(h + 1) * 128], ones_bf[p0:p0 + 64, :], sqb[p0:p0 + 64, :], start=False, stop=True)

                def S6(t):
                    tc0 = t * 128
                    rs = TM("frs4", [128, 512], F32)
                    act_rsqrt(rs, BK(1)[:, 0:512], epsD[:, 1:2])
                    for h in range(4):
                        p0 = (h % 2) * 64
                        bk = pob[(t, h)]
                        gate_store(h, tc0, bk[:, 0:128], bk[p0:p0 + 64, 128:256],
                                   rs[:, h * 128:(h + 1) * 128], rs[p0:p0 + 64, h * 128:(h + 1) * 128])

                state_["gl"] = 0
                S1(0)
                S2(0)
                for t in range(ntile):
                    S3(t)
                    if t + 1 < ntile:
                        S1(t + 1)
                        S2(t + 1)
                    S4(t)
                    S5(t)
                    S6(t)
            else:
                def out_stages(h):
                    p0 = (h % 2) * 64
                    Sh, Shb = Shs[h]
                    st_ = {}

                    def s1():
                        st_["pA"] = PS()
                        mm(st_["pA"][0:64, 0:64], ks[:, h, 0:64], qs[:, h, 0:64])

                    def s2():
                        st_["am"] = TM("ams", [64, 64], BF16)
                        tt("dve", st_["am"], st_["pA"][0:64, 0:64], sU, MUL)

                    def s3():
                        st_["poA"] = PS()
                        mm(st_["poA"][:, 0:64], vtok[0:64, 0, 192 * h:192 * h + 128], st_["am"])
                        st_["poB"] = PS()
                        mm(st_["poB"][p0:p0 + 64, 0:64], vtok[0:64, 0, 192 * h + 128:192 * h + 192], st_["am"])

                    def s4():
                        st_["p2A"] = PS()
                        st_["p2B"] = PS()
                        for b in range(16):
                            mm(st_["p2A"][:, 4 * b:4 * b + 4], Shb[:, b, 0:128], qs[:, h, 4 * b:4 * b + 4])
                        for b in range(16):
                            mm(st_["p2B"][p0:p0 + 64, 4 * b:4 * b + 4], Shb[:, b, 128:192], qs[:, h, 4 * b:4 * b + 4])

                    def s5():
                        oA1 = TM("oA1", [128, 64], F32)
                        oB1 = TM("oB1", [128, 64], F32)
                        cp("act", oA1, st_["p2A"][:, 0:64])
                        cp("act", oB1[p0:p0 + 64, :], st_["p2B"][p0:p0 + 64, 0:64])
                        tt("dve", oA1, st_["poA"][:, 0:64], oA1, ADD)
                        tt("dve", oB1[p0:p0 + 64, :], st_["poB"][p0:p0 + 64, 0:64], oB1[p0:p0 + 64, :], ADD)
                        st_["oA1"] = oA1
                        st_["oB1"] = oB1

                    def s6():
                        sqa = TM("sqa", [128, 64], BF16)
                        sqb = TM("sqb", [128, 64], BF16)
                        act(sqa, st_["oA1"], AF.Square)
                        act(sqb[p0:p0 + 64, :], st_["oB1"][p0:p0 + 64, :], AF.Square)
                        st_["pq"] = PS()
                        mm(st_["pq"][:, 0:64], ones_bf[:, :], sqa, start=True, stop=False)
                        mm(st_["pq"][:, 0:64], ones_bf[p0:p0 + 64, :], sqb[p0:p0 + 64, :], start=False, stop=True)

                    def s7():
                        rs = TM("frs", [128, 64], F32)
                        act_rsqrt(rs, st_["pq"][:, 0:64], epsD[:, 1:2])
                        st_["rs"] = rs

                    def s8():
                        gate_store(h, 0, st_["oA1"], st_["oB1"][p0:p0 + 64, :], st_["rs"], st_["rs"][p0:p0 + 64, :])

                    return [s1, s2, s3, s4, s5, s6, s7, s8]

                def upd_steps(h):
                    Sh, Shb = Shs[h]
                    st_ = {}

                    def u0():
                        Vb = TM("Vb", [64, 16, 192], BF16, 1)
                        vin = vtok[0:64, 0, 192 * h:192 * h + 192].unsqueeze(1).to_broadcast([64, 16, 192])
                        ohb = onehot.unsqueeze(2).to_broadcast([64, 16, 192])
                        tt("dve", Vb, vin, ohb, MUL)
                        st_["Vb"] = Vb
                        st_["Sn"] = TM("Sn", [96, 16, 192], F32, 1)

                    def ub(bp):
                        def f():
                            eblv = ebl[:, h, :].rearrange("p (b t) -> p b t", t=4)[:, :, 3]
                            pd = PS()
                            mm(pd[0:96, 0:384], khat[0:64, 0, 96 * h:96 * h + 96],
                               st_["Vb"][:, 2 * bp:2 * bp + 2, :].rearrange("p b v -> p (b v)"))
                            tmp = TM("stmp", [96, 2, 192], F32)
                            tt("dve", tmp, Sh[:, 2 * bp:2 * bp + 2, :],
                               eblv[:, 2 * bp:2 * bp + 2].unsqueeze(2).to_broadcast([96, 2, 192]), MUL)
                            tt("dve", st_["Sn"][:, 2 * bp:2 * bp + 2, :], tmp,
                               pd[0:96, 0:384].rearrange("p (b v) -> p b v", b=2), ADD)
                        return f

                    def ue():
                        dma("sp", gla_s_o[:, h].rearrange("b d v -> d b v"), st_["Sn"])

                    return [u0] + [ub(bp) for bp in range(8)] + [ue]

                for f in out_stages(0):
                    f()
                for h in range(4):
                    if 1 <= h + 1 < 4 and h + 1 >= 2:
                        load_state(h + 1)
                    ups = upd_steps(h)
                    outs = out_stages(h + 1) if h + 1 < 4 else []
                    k_ = 0
                    for i_, u in enumerate(ups):
                        u()
                        if i_ >= 1 and k_ < len(outs):
                            outs[k_]()
                            k_ += 1
                    while k_ < len(outs):
                        outs[k_]()
                        k_ += 1
            if sample:
                mem_attn_sample(0, mqT, mix, 6)
            else:
                mem_attn_prompt(0, mqT, n, mix, 6)
            for dc in range(8):
                po = PS()
                for s_ in range(8):
                    mm(po[:, 0:n], Wo[:, s_, dc * 128:(dc + 1) * 128], mix[:, s_, :], start=(s_ == 0), stop=(s_ == 7))
                tt("dve", xT[:, dc, c0:c0 + n], po[:, 0:n], xT[:, dc, c0:c0 + n], ADD)

        def kv_phase(ps_, c0, n, sample, mode="all"):
            hT, W = phase_entry("kv", mode, c0, n, 8, [([128, 8, 960], kp(w_kvx))])
            if mode == "prep":
                return
            Wk = W[0]
            need_f32 = sample or (ps_ == 1 and c0 == 512)
            kf = kf_p[:] if need_f32 else None
            for kvh in range(3):
                p1 = PS()
                p2 = PS()
                for k in range(8):
                    mm(p1[:, 0:n], Wk[:, k, 128 * kvh:128 * kvh + 128], hT[:, k, :], start=(k == 0), stop=(k == 7))
                for k in range(8):
                    mm(p2[:, 0:n], Wk[:, k, 384 + 128 * kvh:384 + 128 * kvh + 128], hT[:, k, :], start=(k == 0), stop=(k == 7))
                t1 = TM("t1", [128, 512], F32)
                t2 = TM("t2", [128, 512], F32)
                tt("dve", t1[:, 0:n], p1[:, 0:n], cosT[:, c0:c0 + n], MUL)
                tt("dve", t2[:, 0:n], p2[:, 0:n], sinT[:, c0:c0 + n], MUL)
                tt("pool", kTd[:, kvh, c0:c0 + n], t1[:, 0:n], t2[:, 0:n], ADD)
                if need_f32:
                    if sample:
                        tt("pool", kf[:, kvh, 0:64], t1[0:64, 0:64], t2[0:64, 0:64], ADD)
                    else:
                        tt("pool", kf[:, kvh, :], t1[0:64, n - 128:n], t2[0:64, n - 128:n], ADD)
            T = 64 if sample else 128
            ntile = 1 if sample else n // 128
            for t in range(ntile):
                tc0 = t * 128
                ti = 8 if sample else (c0 // 128 + t)
                pv = PS()
                for k in range(8):
                    mm(pv[0:T, 0:192], hT[:, k, tc0:tc0 + T], Wk[:, k, 768:960], start=(k == 0), stop=(k == 7))
                cp("act", vtokB[0:T, ti, :], pv[0:T, 0:192])
                last = (not sample) and ps_ == 1 and c0 == 512 and t == ntile - 1
                if last or sample:
                    vf = vf_p[:]
                    cp("dve", vf[0:T], pv[0:T, 0:192])
                    if last:
                        dma("sp", swa_v_p_o, vf)
                    else:
                        for b in range(16):
                            dma("sp", swa_v_s_o[b, 124:128, :], vf[4 * b:4 * b + 4, :])
            if (not sample) and ps_ == 1 and c0 == 512:
                dma("sp", swa_kT_p_o.rearrange("h d j -> d h j"), kf)
            if (not sample) and ps_ == 0 and c0 == 512:
                cp("pool", kT_prev[:], kTd[:, :, NPP - 128:NPP])
                cp("pool", v_prev[:], vtokB[:, 7, :])
            if sample:
                for kvh in range(3):
                    dma("sp", swa_kT_s_o[:, kvh, :, 124:128].rearrange("b d t -> d b t"),
                        kf[:, kvh, 0:64].rearrange("p (b t) -> p b t", t=4))

        def mixer_b(ps_, c0, n, sample, mode="all"):
            wspecs = [([128, 8, 768], kp(b_w_inx[:, 0:768]))]
            if mode == "all":
                wspecs.append(([128, 8, 256], kp(b_w_inx[:, 1536:1792])))
            hT, W = phase_entry("mb", mode, c0, n, 3, wspecs)
            if mode == "prep":
                return
            Wq = W[0]
            if len(W) < 2:
                w2 = A([128, 8, 256], BF16)
                dma("pool", w2, kp(b_w_inx[:, 1536:1792]))
                W.append(w2)
            Wq2 = W[1]
            if sample:
                kc = A([128, 16, 3, 128], BF16, top=True)
                vcb = A([128, 16, 192], BF16, top=True)
                dma("pool", kc[0:64], kcT.rearrange("b h d j -> d b h j"))
                dma("pool", kc[64:128], kcT.rearrange("b h d j -> d b h j"))
                dma("pool", vcb, vc.rearrange("b j f -> j b f"))
                mem_pre = []
                for hb_ in range(2):
                    ck_ = A([128, 8, 2, 256], BF16, top=True)
                    cv_ = A([128, 8, 2, 256], BF16, top=True)
                    load_mem_cache(1, hb_, ck_, cv_)
                    mem_pre.append((ck_, cv_))
            Wo = A([128, 8, D], BF16)
            dma("pool", Wo, kp(b_w_out))
            qT = A([128, 6, n], BF16)
            pend = None

            def q_finish(c_, p1_, qb_):
                p2_ = PS()
                mm(p2_[:, 0:n], perm_bf[:], qb_[:, 0:n])
                t1 = TM("t1", [128, 512], F32)
                t2 = TM("t2", [128, 512], F32)
                tt("dve", t1[:, 0:n], p1_[:, 0:n], cosT[:, c0:c0 + n], MUL)
                tt("dve", t2[:, 0:n], p2_[:, 0:n], sinT[:, c0:c0 + n], MUL)
                tt("pool", qT[:, c_, :], t1[:, 0:n], t2[:, 0:n], ADD)

            for c in range(6):
                p1 = PS()
                for k in range(8):
                    mm(p1[:, 0:n], Wq[:, k, 128 * c:128 * c + 128], hT[:, k, :], start=(k == 0), stop=(k == 7))
                qb = TM("qb", [128, 512], BF16)
                cp("act", qb[:, 0:n], p1[:, 0:n])
                if pend is not None:
                    q_finish(*pend)
                pend = (c, p1, qb)
            q_finish(*pend)
            mqT = A([128, 2, n], BF16)
            for c in range(2):
                pp = PS()
                for k in range(8):
                    mm(pp[:, 0:n], Wq2[:, k, c * 128:(c + 1) * 128], hT[:, k, :], start=(k == 0), stop=(k == 7))
                cp("act", mqT[:, c, :], pp[:, 0:n])
            mix = A([128, 8, n], BF16)
            KSUB = int(DBG.get("KSUB", 99))
            if KSUB <= 1:
                return
            if not sample:
                def BK(b_):
                    return psum[:, b_ * 512:(b_ + 1) * 512]
                units = [(t, kvh) for t in range(n // 128) for kvh in range(3)]
                pTs = {}

                def info(ui):
                    t, kvh = units[ui]
                    lt = c0 // 128 + t
                    gt = ps_ * 8 + lt
                    return t, kvh, lt, gt, ([1] if gt == 0 else [0, 1])

                def U1(ui):
                    t, kvh, lt, gt, ws = info(ui)
                    tc0 = t * 128
                    worder = [1, 0] if len(ws) == 2 else [1]
                    for wi, which in enumerate(worder):
                        for ge in range(2):
                            for par in range(2):
                                hd = 4 * kvh + 2 * ge + par
                                c = hd // 2
                                p0 = par * 64
                                if which == 1:
                                    kk = kTd[p0:p0 + 64, kvh, c0 + tc0:c0 + tc0 + 128]
                                elif lt == 0:
                                    kk = kT_prev[p0:p0 + 64, kvh, :]
                                else:
                                    kk = kTd[p0:p0 + 64, kvh, c0 + tc0 - 128:c0 + tc0]
                                col = (wi * 2 + ge) * 128
                                mm(BK(2 * (ui % 2) + par)[:, col:col + 128], kk, qT[p0:p0 + 64, c, tc0:tc0 + 128])

                def U2(ui):
                    t, kvh, lt, gt, ws = info(ui)
                    nw = len(ws)
                    pT = TM("pT", [128, 2, 512], BF16)
                    pTs[ui] = pT
                    for par in range(2):
                        ex = TM("ex", [128, 512], F32, 4)
                        act(ex[:, 0:256 * nw], BK(2 * (ui % 2) + par)[:, 0:256 * nw], AF.Exp, scale=0.125)
                        for wi in range(nw):
                            msk = cst[:, C_TRIU + 128 * wi:C_TRIU + 128 * wi + 128].unsqueeze(1).to_broadcast([128, 2, 128])
                            outv = pT[:, wi, :].rearrange("p (ge pa i) -> p ge pa i", ge=2, pa=2)[:, :, par, :]
                            inv = ex[:, 256 * wi:256 * wi + 256].rearrange("p (ge i) -> p ge i", ge=2)
                            tt("pool", outv, inv, msk, MUL)

                def U3(ui):
                    t, kvh, lt, gt, ws = info(ui)
                    pT = pTs[ui]
                    po = BK(4 + 2 * (ui % 2))
                    pr = BK(5 + 2 * (ui % 2))
                    for par in range(2):
                        for which in ws:
                            if which == 1:
                                vv = vtokB[:, lt, 64 * kvh:64 * kvh + 64]
                            elif lt == 0:
                                vv = v_prev[:, 64 * kvh:64 * kvh + 64]
                            else:
                                vv = vtokB[:, lt - 1, 64 * kvh:64 * kvh + 64]
                            rhs = pT[:, 1 - which, :].rearrange("p (ge pa i) -> p ge pa i", ge=2, pa=2)[:, :, par, :]
                            mm(po[par * 64:(par + 1) * 64, 0:256], vv, rhs, start=(which == ws[0]), stop=(which == 1))
                    for which in ws:
                        mm(pr[:, 0:512], ones_bf[:, :], pT[:, 1 - which, :], start=(which == ws[0]), stop=(which == 1))

                def U4(ui):
                    t, kvh, lt, gt, ws = info(ui)
                    tc0 = t * 128
                    po = BK(4 + 2 * (ui % 2))
                    pr = BK(5 + 2 * (ui % 2))
                    den = TM("den", [128, 2, 128], F32)
                    for par in range(2):
                        p0 = par * 64
                        prv = pr[p0:p0 + 64, 0:512].rearrange("p (ge pa i) -> p ge pa i", ge=2, pa=2)[:, :, par, :]
                        esv = esink[p0:p0 + 64, 4 * kvh:4 * kvh + 4].rearrange("p (ge pa) -> p ge pa", pa=2)[:, :, par]
                        tt("dve", den[p0:p0 + 64], prv, esv.unsqueeze(2).to_broadcast([64, 2, 128]), ADD)
                    act_recip(den, den)
                    for par in range(2):
                        p0 = par * 64
                        tt("dve", mix[p0:p0 + 64, 2 * kvh:2 * kvh + 2, tc0:tc0 + 128],
                           po[p0:p0 + 64, 0:256].rearrange("p (ge i) -> p ge i", ge=2), den[p0:p0 + 64], MUL)

                U1(0)
                U2(0)
                for ui in range(len(units)):
                    if ui + 1 < len(units):
                        U1(ui + 1)
                    U3(ui)
                    if ui + 1 < len(units):
                        U2(ui + 1)
                    U4(ui)
                if KSUB <= 3:
                    return
                mem_attn_prompt(1, mqT, n, mix, 6)
                if KSUB <= 4:
                    return
            else:
                sC = PS(2)
                sN = PS(2)
                for hd in (0, 2, 4, 6, 8, 10, 1, 3, 5, 7, 9, 11):
                    kvh = hd // 4
                    c = hd // 2
                    p0 = (hd % 2) * 64
                    for b in range(16):
                        col = hd * 64 + b * 4
                        mm(sC[:, col:col + 4], kc[p0:p0 + 64, b, kvh, :], qT[p0:p0 + 64, c, 4 * b:4 * b + 4])
                    mm(sN[0:64, hd * 64:(hd + 1) * 64], kTd[p0:p0 + 64, kvh, NPP:NPP + 64], qT[p0:p0 + 64, c, 0:64])
                exC = A([128, 768], F32)
                exN = A([64, 768], F32)
                act(exC, sC[:, 0:768], AF.Exp, scale=0.125)
                act(exN, sN[0:64, 0:768], AF.Exp, scale=0.125)
                eC = A([128, 768], BF16)
                eN = A([64, 768], BF16)
                tt("dve", eC.rearrange("p (a t) -> p a t", t=4), exC.rearrange("p (a t) -> p a t", t=4),
                   maskC.unsqueeze(1).to_broadcast([128, 192, 4]), MUL)
                tt("dve", eN.rearrange("p (h f) -> p h f", h=12), exN.rearrange("p (h f) -> p h f", h=12),
                   maskN.unsqueeze(1).to_broadcast([64, 12, 64]), MUL)
                po = PS(2)
                pr = PS(2)
                po2 = sC
                for hd in range(12):
                    kvh = hd // 4
                    p0 = (hd % 2) * 64
                    for b in range(16):
                        col = hd * 64 + b * 4
                        mm(po[p0:p0 + 64, col:col + 4], vcb[:, b, 64 * kvh:64 * kvh + 64], eC[:, col:col + 4])
                    mm(po2[p0:p0 + 64, hd * 64:(hd + 1) * 64], vtokB[0:64, 8, 64 * kvh:64 * kvh + 64],
                       eN[:, hd * 64:(hd + 1) * 64])
                o2s = A([128, 768], F32)
                cp("act", o2s, po2[:, 0:768])
                for (a0, a1) in ((0, 512), (512, 768)):
                    mm(pr[:, a0:a1], ones_bf[:, :], eC[:, a0:a1], start=True, stop=False)
                    mm(pr[:, a0:a1], ones_bf[0:64, :], eN[:, a0:a1], start=False, stop=True)
                den = TM("dens", [128, 12, 64], F32, 1)
                tt("dve", den, pr[:, 0:768].rearrange("p (h f) -> p h f", h=12),
                   esink[:, 0:12].unsqueeze(2).to_broadcast([128, 12, 64]), ADD)
                act_recip(den, den)
                for par in range(2):
                    p0 = par * 64
                    pov = po[p0:p0 + 64, 0:768].rearrange("p (c pa f) -> p c pa f", c=6, pa=2)[:, :, par, :]
                    o2v = o2s[p0:p0 + 64, :].rearrange("p (c pa f) -> p c pa f", c=6, pa=2)[:, :, par, :]
                    dnv = den[p0:p0 + 64].rearrange("p (c pa) f -> p c pa f", pa=2)[:, :, par, :]
                    osum = TM("osum", [128, 6, 64], F32, 1)
                    tt("dve", osum[p0:p0 + 64], pov, o2v, ADD)
                    tt("dve", mix[p0:p0 + 64, 0:6, :], osum[p0:p0 + 64], dnv, MUL)
                mem_attn_sample(1, mqT, mix, 6, pre=mem_pre)
            for dc in range(8):
                po = PS()
                for s_ in range(8):
                    mm(po[:, 0:n], Wo[:, s_, dc * 128:(dc + 1) * 128], mix[:, s_, :], start=(s_ == 0), stop=(s_ == 7))
                tt("dve", xT[:, dc, c0:c0 + n], po[:, 0:n], xT[:, dc, c0:c0 + n], ADD)

        def final(ps_, c0, n, sample):
            arena_reset()
            sq = A([128, 8, n], BF16)
            yst = A([128, 8, n], F32)
            norm(xT[:, :, c0:c0 + n], n, 9, yst, sq)
            oc = 2048 if sample else ps_ * NPP + c0
            dma("sp", kp(yT_o[:, oc:oc + n]), yst)

        KST = int(DBG.get("KSTAGE", 99))
        KPASS = int(DBG.get("KPASS", 2))
        for ps_ in range(min(2, KPASS)):
            tgs = [(0, 512, False), (512, 512, False)] + ([(NPP, NSM, True)] if ps_ == 1 else [])
            ntp = NPP + (NSM if ps_ == 1 else 0)
            tg2 = [(c0, n) for (c0, n, _) in tgs]
            HK = not DBG.get("KNOHOOK")
            HK2 = HK and KST >= 7 and KPASS >= 2
            if ps_ == 1:
                load_pass_inputs(1, part=(1 if HK2 else None))
            if KST >= 1:
                ffn(0, 0, tg2, ntp, hook=(lambda: mixer_a(0, 512, False, mode="prep")) if (HK and KST >= 2) else None)
            if KST >= 2:
                for (c0, n, smp) in tgs:
                    if smp and KST == 2 and DBG.get("KNOSAMPLE"):
                        continue
                    mixer_a(c0, n, smp, mode=("body" if (HK and c0 == 0) else "all"),
                            cache_copy=(ps_ == 1 and c0 == 0))
                if ps_ == 1:
                    dma("sp", gla_p_o.rearrange("h d v -> d h v"), S[:])
            if KST >= 3:
                ffn(0, 1, tg2, ntp, hook=(lambda: kv_phase(ps_, 0, 512, False, mode="prep")) if (HK and KST >= 4) else None)
            if KST >= 4:
                for (c0, n, smp) in tgs:
                    kv_phase(ps_, c0, n, smp, mode=("body" if (HK and c0 == 0) else "all"))
            if KST >= 5:
                ffn(1, 0, tg2, ntp, hook=(lambda: mixer_b(ps_, 0, 512, False, mode="prep")) if (HK and KST >= 6) else None)
            if KST >= 6:
                for (c0, n, smp) in tgs:
                    mixer_b(ps_, c0, n, smp, mode=("body" if (HK and c0 == 0) else "all"))
            def fin_hook(ps_=ps_):
                final(ps_, 0, 512, False)
                if ps_ == 0:
                    load_pass_inputs(1, part=0)
            if KST >= 7:
                ffn(1, 1, tg2, ntp, hook=fin_hook if HK2 else None)
            for (c0, n, smp) in tgs:
                if HK2 and c0 == 0:
                    continue
                final(ps_, c0, n, smp)

        P.emit(st)
    return nc


def prep(x_prompt, x_sample, state_gla, cache_swa_k, cache_swa_v, cache_mem_k, cache_mem_v,
         mem_prompt, ffn1_norm, ffn1_w_gu, ffn1_w_down, mix_norm, ffn2_norm, ffn2_w_gu,
         ffn2_w_down, mem_norm, mem_w_kv, a_w_in, a_w_gate, a_b_gate, a_out_norm, a_w_out,
         kv_norm, w_kv, b_w_in, b_sinks, b_w_out, final_norm):
    f = lambda a: np.ascontiguousarray(np.asarray(a, dtype=np.float32))
    x_prompt, x_sample, state_gla = f(x_prompt), f(x_sample), f(state_gla)
    cache_swa_k, cache_swa_v = f(cache_swa_k), f(cache_swa_v)
    cache_mem_k, cache_mem_v, mem_prompt = f(cache_mem_k), f(cache_mem_v), f(mem_prompt)
    n = 8
    gains = np.stack([f(ffn1_norm)[0], f(ffn1_norm)[1], f(mix_norm)[0], f(mix_norm)[1],
                      f(ffn2_norm)[0], f(ffn2_norm)[1], f(mem_norm)[0], f(mem_norm)[1],
                      f(kv_norm), f(final_norm)], 0)
    gains_p = f(gains.reshape(10, 8, 128).transpose(2, 0, 1))
    go = f(a_out_norm).reshape(192)
    gout = np.zeros((128, 2), np.float32)
    gout[:, 0] = go[0:128]
    gout[0:64, 1] = go[128:192]
    gout[64:128, 1] = go[128:192]
    sinks = f(np.broadcast_to(f(b_sinks).reshape(1, 12), (128, 12)))
    wkv = f(w_kv)
    wk = wkv[:, 0:192]
    wkp = wk[:, rope_perm_cols(192)]
    kd = np.concatenate([np.concatenate([wk[:, 64 * h:64 * h + 64]] * 2, 1) for h in range(3)], 1)
    kdp = np.concatenate([np.concatenate([wkp[:, 64 * h:64 * h + 64]] * 2, 1) for h in range(3)], 1)
    w_kvx = f(np.concatenate([kd, kdp, wkv[:, 192:384]], 1))
    bwi = f(b_w_in)[0]
    wq = bwi[:, 0:768]
    b_w_inx = f(np.concatenate([wq, wq[:, rope_perm_cols(768)], bwi[:, 768:1024]], 1))
    awi = f(a_w_in)[0]
    rcols = [1536 + 192 * h + j for h in range(4) for j in range(128)] + \
            [1536 + 192 * h + 128 + j for h in range(4) for j in range(64)]
    awi = np.concatenate([awi[:, 0:1536], awi[:, rcols], awi[:, 2304:2576]], 1)
    shared = dict(
        w_gu1=f(ffn1_w_gu), w_d1=f(ffn1_w_down), w_gu2=f(ffn2_w_gu), w_d2=f(ffn2_w_down),
        w_memkv=f(mem_w_kv), a_w_in=f(awi), a_w_gate=f(a_w_gate)[0], a_b_gate=f(a_b_gate).reshape(1, 384),
        a_w_out=f(a_w_out)[0], w_kvx=w_kvx, b_w_inx=b_w_inx, b_w_out=f(b_w_out)[0],
        gains=gains_p, gout=gout, sinks=sinks, consts=make_consts(), rope=make_rope(),
    )
    in_maps = []
    for c in range(n):
        bs = slice(16 * c, 16 * c + 16)
        m = dict(shared)
        m["xT_p"] = f(x_prompt[c].T)
        m["xT_s"] = f(x_sample[bs].reshape(64, D).T)
        m["memT"] = f(mem_prompt[c].T)
        m["state"] = f(state_gla[0, bs])
        m["kcT"] = f(cache_swa_k[bs].transpose(0, 2, 3, 1))
        m["vc"] = f(cache_swa_v[bs].reshape(16, 128, 192))
        m["cmkT"] = f(cache_mem_k[:, bs].transpose(0, 1, 3, 4, 2))
        m["cmv"] = f(cache_mem_v[:, bs].reshape(2, 16, 256, 256))
        in_maps.append(m)
    return in_maps


def assemble(R):
    n = 8
    y_prompt = np.stack([R[c]["yT"][:, 0:2048].T for c in range(n)], 0)
    y_sample = np.concatenate([R[c]["yT"][:, 2048:2112].T.reshape(16, 4, D) for c in range(n)], 0)
    gla_prompt = np.stack([R[c]["gla_p"] for c in range(n)], 0)[None]
    gla_sample = np.concatenate([R[c]["gla_s"] for c in range(n)], 0)[None]
    swa_k_prompt = np.stack([R[c]["swa_kT_p"].transpose(2, 0, 1) for c in range(n)], 0)
    swa_v_prompt = np.stack([R[c]["swa_v_p"].reshape(128, 3, 64) for c in range(n)], 0)
    swa_k_sample = np.concatenate([R[c]["swa_kT_s"].transpose(0, 3, 1, 2) for c in range(n)], 0)
    swa_v_sample = np.concatenate([R[c]["swa_v_s"].reshape(16, 128, 3, 64) for c in range(n)], 0)
    mem_k_prompt = np.stack([R[c]["mem_k_p"].reshape(2, 256, 4, 64) for c in range(n)], 1)
    mem_v_prompt = np.stack([R[c]["mem_v_p"].reshape(2, 256, 4, 64) for c in range(n)], 1)
    outs = (y_prompt, y_sample, gla_prompt, gla_sample, swa_k_prompt, swa_v_prompt,
            swa_k_sample, swa_v_sample, mem_k_prompt, mem_v_prompt)
    return tuple(np.ascontiguousarray(o, dtype=np.float32) for o in outs)


def kernel(**inputs):
    in_maps = prep(**inputs)
    nc = build_nc()
    res = run_bass_kernel_spmd(nc, in_maps, core_ids=list(range(8)))
    return assemble(res.results)
```
